# Optimizing a Trainium2 kernel written in Bass

```python
import jax, jax.numpy as jnp
from jax import lax
import numpy as np

D_MODEL = 2048
BATCH = 4
SEQ = 2048
DEPTH = 1

CHUNK = 64
HG_HEADS = 8
HG_DK = 128
HG_DV = 128
HG_WIDTH = HG_HEADS * HG_DV
AT_HEADS = 16
AT_DH = 64
AT_WIDTH = AT_HEADS * AT_DH
LEFT_CHUNKS = 8
BAND = (LEFT_CHUNKS + 1) * CHUNK
REL_CLIP = 256
N_REL = 2 * REL_CLIP + 1
D_FF = 4 * D_MODEL
N_BRANCH = 2
EPS = 1e-6
D_IN = 4 * HG_WIDTH + 3 * AT_WIDTH + N_BRANCH * D_MODEL
SPLIT_POINTS = (HG_WIDTH, 2 * HG_WIDTH, 3 * HG_WIDTH, 4 * HG_WIDTH,
                4 * HG_WIDTH + AT_WIDTH, 4 * HG_WIDTH + 2 * AT_WIDTH, 4 * HG_WIDTH + 3 * AT_WIDTH)

kernel_name = "hybrid_hgrn2_chunkattn_gated_block"


def rms_norm(x, w):
    xf = x.astype(jnp.float32)
    y = xf * lax.rsqrt(jnp.mean(xf * xf, axis=-1, keepdims=True) + EPS)
    return (y * w.astype(jnp.float32)).astype(x.dtype)


def hgrn2_scan(q, k, v, log_g):
    B, H, T, DK = q.shape
    DV = v.shape[-1]
    nc = T // CHUNK

    def to_chunks(a):
        return a.reshape(B, H, nc, CHUNK, a.shape[-1]).transpose(2, 0, 1, 3, 4)

    causal = jnp.tril(jnp.ones((CHUNK, CHUNK), dtype=bool))

    def step(S, inp):
        qc, kc, vc, gc = inp
        b = jnp.cumsum(gc, axis=2)
        o_inter = jnp.einsum('bhtk,bhkv->bhtv', qc * jnp.exp(b), S)
        diff = b[:, :, :, None, :] - b[:, :, None, :, :]
        decay = jnp.exp(jnp.where(causal[:, :, None], diff, -jnp.inf))
        scores = jnp.einsum('bhtk,bhtsk,bhsk->bhts', qc, decay, kc)
        o_intra = jnp.einsum('bhts,bhsv->bhtv', scores, vc)
        b_last = b[:, :, -1:, :]
        S_new = jnp.exp(b_last[:, :, 0, :])[..., None] * S + jnp.einsum(
            'bhsk,bhsv->bhkv', kc * jnp.exp(b_last - b), vc)
        return S_new, o_inter + o_intra

    S0 = jnp.zeros((B, H, DK, DV), jnp.float32)
    _, o = lax.scan(step, S0, (to_chunks(q), to_chunks(k), to_chunks(v), to_chunks(log_g)))
    return o.transpose(1, 2, 0, 3, 4).reshape(B, H, T, DV)


def chunk_band_attention(q, k, v, rel_bias):
    B, H, T, Dh = q.shape
    nc = T // CHUNK
    pad = LEFT_CHUNKS * CHUNK
    kp = jnp.pad(k, ((0, 0), (0, 0), (pad, 0), (0, 0)))
    vp = jnp.pad(v, ((0, 0), (0, 0), (pad, 0), (0, 0)))
    idx = (jnp.arange(nc) * CHUNK)[:, None] + jnp.arange(BAND)[None, :]
    kb = kp[:, :, idx, :]
    vb = vp[:, :, idx, :]
    qc = q.reshape(B, H, nc, CHUNK, Dh)
    valid = idx >= pad
    t = jnp.arange(CHUNK)
    j = jnp.arange(BAND)
    rel = t[:, None] + pad - j[None, :]
    rel_idx = jnp.clip(rel, -REL_CLIP, REL_CLIP) + REL_CLIP
    bias = rel_bias[:, rel_idx].astype(jnp.float32)
    s = jnp.einsum('bhnqd,bhnkd->bhnqk', qc, kb).astype(jnp.float32) * (Dh ** -0.5) + bias[:, None]
    s = jnp.where(valid[:, None, :], s, -jnp.inf)
    p = jax.nn.softmax(s, axis=-1).astype(v.dtype)
    o = jnp.einsum('bhnqk,bhnkd->bhnqd', p, vb)
    return o.reshape(B, H, T, Dh)


def mixer_block(u, w_in, lb, hg_norm_w, rel_bias, w_branch_a, w_branch_b, w_out):
    B, T, _ = u.shape
    z = u @ w_in
    hq, hf, hi, hg, aq, ak, av, gates = jnp.split(z, SPLIT_POINTS, axis=-1)

    def to_heads(a, h):
        return a.reshape(B, T, h, -1).transpose(0, 2, 1, 3)

    f = jax.nn.sigmoid(hf.astype(jnp.float32))
    g = lb + (1.0 - lb) * f
    log_g = jnp.log(g)
    kk = 1.0 - g
    q = jax.nn.silu(hq.astype(jnp.float32)) * (HG_DK ** -0.5)
    o = hgrn2_scan(to_heads(q, HG_HEADS), to_heads(kk, HG_HEADS),
                   to_heads(hi.astype(jnp.float32), HG_HEADS), to_heads(log_g, HG_HEADS))
    o = o.transpose(0, 2, 1, 3)
    o = rms_norm(o, hg_norm_w) * jax.nn.silu(hg.reshape(B, T, HG_HEADS, HG_DV).astype(jnp.float32))
    y_a = o.reshape(B, T, HG_WIDTH).astype(u.dtype)

    ya = chunk_band_attention(to_heads(aq, AT_HEADS), to_heads(ak, AT_HEADS),
                              to_heads(av, AT_HEADS), rel_bias)
    y_b = ya.transpose(0, 2, 1, 3).reshape(B, T, AT_WIDTH)

    gate_a, gate_b = jnp.split(jax.nn.sigmoid(gates), 2, axis=-1)
    merged = gate_a * (y_a @ w_branch_a) + gate_b * (y_b @ w_branch_b)
    return merged @ w_out


def setup_inputs(seed: int = 0) -> dict:
    key = jax.random.key(seed)
    ks = jax.random.split(key, 13)
    f32 = jnp.float32
    x = jax.random.normal(ks[0], (BATCH, SEQ, D_MODEL), f32)
    w_in = jax.random.normal(ks[1], (DEPTH, D_MODEL, D_IN), f32) * D_MODEL ** -0.5
    lb_logits = jax.random.normal(ks[2], (DEPTH + 1, HG_WIDTH), f32)
    hg_norm_w = 1.0 + 0.02 * jax.random.normal(ks[3], (DEPTH, HG_DV), f32)
    rel_bias = 0.1 * jax.random.normal(ks[4], (DEPTH, AT_HEADS, N_REL), f32)
    w_branch_a = jax.random.normal(ks[5], (DEPTH, HG_WIDTH, D_MODEL), f32) * HG_WIDTH ** -0.5
    w_branch_b = jax.random.normal(ks[6], (DEPTH, AT_WIDTH, D_MODEL), f32) * AT_WIDTH ** -0.5
    w_out = jax.random.normal(ks[7], (DEPTH, D_MODEL, D_MODEL), f32) * D_MODEL ** -0.5
    norm_mix_w = 1.0 + 0.02 * jax.random.normal(ks[8], (DEPTH, D_MODEL), f32)
    norm_mlp_w = 1.0 + 0.02 * jax.random.normal(ks[9], (DEPTH, D_MODEL), f32)
    w_up = jax.random.normal(ks[10], (DEPTH, D_MODEL, D_FF), f32) * D_MODEL ** -0.5
    w_down = jax.random.normal(ks[11], (DEPTH, D_FF, D_MODEL), f32) * D_FF ** -0.5
    norm_final_w = 1.0 + 0.02 * jax.random.normal(ks[12], (D_MODEL,), f32)
    return {"x": x, "w_in": w_in, "lb_logits": lb_logits, "hg_norm_w": hg_norm_w,
            "rel_bias": rel_bias, "w_branch_a": w_branch_a, "w_branch_b": w_branch_b,
            "w_out": w_out, "norm_mix_w": norm_mix_w, "norm_mlp_w": norm_mlp_w,
            "w_up": w_up, "w_down": w_down, "norm_final_w": norm_final_w}


def reference(x, w_in, lb_logits, hg_norm_w, rel_bias, w_branch_a, w_branch_b, w_out,
              norm_mix_w, norm_mlp_w, w_up, w_down, norm_final_w):
    lb_all = jnp.cumsum(jax.nn.softmax(lb_logits.astype(jnp.float32), axis=0), axis=0)
    h = x
    for l in range(DEPTH):
        h = h + mixer_block(rms_norm(h, norm_mix_w[l]), w_in[l], lb_all[l], hg_norm_w[l],
                            rel_bias[l], w_branch_a[l], w_branch_b[l], w_out[l])
        u = rms_norm(h, norm_mlp_w[l])
        h = h + jnp.square(jax.nn.relu(u @ w_up[l])) @ w_down[l]
    return rms_norm(h, norm_final_w)
```

```python
from contextlib import ExitStack

import numpy as np
import concourse.bass as bass
import concourse.mybir as mybir
from concourse.bass_utils import run_bass_kernel_spmd

F32 = mybir.dt.float32
BF16 = mybir.dt.bfloat16
U8 = mybir.dt.uint8
AF = mybir.ActivationFunctionType
ALU = mybir.AluOpType

NEG = -30000.0
EPS = 1e-6


class Reg:
    __slots__ = ("name", "w", "r")

    def __init__(self, name):
        self.name = name
        self.w = {}
        self.r = {}


class Sched:
    LIMIT = 30000
    NDQ = 6

    def __init__(self, nc, es):
        self.nc = nc
        self.es = es
        self.engs = ("pe", "act", "dve", "pool", "sp")
        self.streams = {e: [] for e in self.engs}
        self.seen = {e: {} for e in self.engs}
        self.cur = {}
        self.nsem = 0
        for e in ("pe", "act", "dve", "pool"):
            self._newsem(e)
        self.dq = {q: [[self._alloc(), 0] for _ in range(self.NDQ)] for q in ("sp", "pool")}
        self.dq_rr = {"sp": 0, "pool": 0}

    def _alloc(self):
        sem = self.es.enter_context(self.nc.semaphore("s%d" % self.nsem))
        self.nsem += 1
        return (self.nsem, sem)

    def _newsem(self, e):
        self.cur[e] = [self._alloc(), 0]

    def reg(self, name="r"):
        return Reg(name)

    def _waits(self, eng, reads, writes, excl, waw):
        need = {}

        def add(tok, same_ok):
            key, sem, val, teng = tok
            if teng == eng and not same_ok:
                return
            if self.seen[eng].get(key, 0) >= val:
                return
            if key in need and need[key][1] >= val:
                return
            need[key] = (sem, val)

        for r in reads:
            for t in r.w.values():
                add(t, True)
        for w in writes:
            if waw:
                for t in w.w.values():
                    add(t, True)
            for t in w.r.values():
                add(t, True)
        for x in excl:
            for t in x.w.values():
                add(t, True)
            for t in x.r.values():
                add(t, True)
        for key, (sem, val) in need.items():
            self.seen[eng][key] = val
            self.streams[eng].append(lambda e, sem=sem, val=val: e.wait_ge(sem, val))

    def _mark(self, tok, reads, writes, excl, waw):
        key = tok[0]
        for r in reads:
            r.r[key] = tok
        for w in writes:
            if waw or w.r:
                w.w = {key: tok}
            else:
                w.w[key] = tok
            w.r = {}
        for x in excl:
            x.w = {key: tok}
            x.r = {}

    def _tick(self, eng):
        cur = self.cur[eng]
        if cur[1] >= self.LIMIT:
            self._newsem(eng)
            cur = self.cur[eng]
        (key, sem) = cur[0]
        cur[1] += 1
        return key, sem, cur[1]

    def op(self, eng, fn, reads=(), writes=(), excl=(), waw=True):
        self.group(eng, [fn], reads, writes, excl, waw)

    def group(self, eng, fns, reads=(), writes=(), excl=(), waw=True):
        self._waits(eng, reads, writes, excl, waw)
        key, sem, val = self._tick(eng)
        for fn in fns[:-1]:
            self.streams[eng].append(lambda e, fn=fn: fn(e))
        fn = fns[-1]
        self.streams[eng].append(lambda e, fn=fn, sem=sem: fn(e).then_inc(sem, 1))
        self._mark((key, sem, val, eng), reads, writes, excl, waw)

    def dma(self, q, fn, reads=(), writes=(), waw=True):
        i = self.dq_rr[q]
        self.dq_rr[q] = (i + 1) % self.NDQ
        slot = self.dq[q][i]
        (key, sem) = slot[0]
        if slot[1] > 0 and self.seen[q].get(key, 0) < 16 * slot[1]:
            v = 16 * slot[1]
            self.seen[q][key] = v
            self.streams[q].append(lambda e, sem=sem, v=v: e.wait_ge(sem, v))
        self._waits(q, reads, writes, (), waw)
        slot[1] += 1
        val = 16 * slot[1]
        self.streams[q].append(lambda e, fn=fn, sem=sem: fn(e).then_inc(sem, 16))
        self._mark((key, sem, val, "dma_" + q), reads, writes, (), waw)

    def barrier(self, engs=("pe", "act", "dve")):
        for e in engs:
            for o in engs:
                if o == e:
                    continue
                (key, sem), cnt = self.cur[o]
                if cnt > 0 and self.seen[e].get(key, 0) < cnt:
                    self.seen[e][key] = cnt
                    self.streams[e].append(lambda en, sem=sem, cnt=cnt: en.wait_ge(sem, cnt))

    def finish(self):
        for q in ("sp", "pool"):
            for (key, sem), cnt in self.dq[q]:
                if cnt > 0:
                    v = 16 * cnt
                    self.streams["sp"].append(lambda e, sem=sem, v=v: e.wait_ge(sem, v))

    def replay(self):
        with self.nc.Block() as block:
            @block.sync
            def _(e):
                for f in self.streams["sp"]:
                    f(e)

            @block.tensor
            def _(e):
                for f in self.streams["pe"]:
                    f(e)

            @block.scalar
            def _(e):
                for f in self.streams["act"]:
                    f(e)

            @block.vector
            def _(e):
                for f in self.streams["dve"]:
                    f(e)

            @block.gpsimd
            def _(e):
                for f in self.streams["pool"]:
                    f(e)


class Arena:
    def __init__(self, nc, es, nbytes, name="arena"):
        self.t = es.enter_context(nc.sbuf_tensor(name, [128, nbytes], U8))
        self.nbytes = nbytes
        self.off = 0
        self.marks = []
        self.peak = 0
        self.log = []

    def alloc(self, shape, dtype):
        esz = 2 if dtype == BF16 else 4
        n = int(np.prod(shape))
        nb = n * esz
        off = self.reserve(nb)
        return self.at(off, shape, dtype)

    def reserve(self, nb):
        off = (self.off + 63) // 64 * 64
        assert off + nb <= self.nbytes, ("arena overflow", off, nb, self.nbytes)
        self.off = off + nb
        self.peak = max(self.peak, self.off)
        return off

    def at(self, off, shape, dtype):
        self.log.append((off, list(shape), "bf16" if dtype == BF16 else "f32"))
        esz = 2 if dtype == BF16 else 4
        nb = int(np.prod(shape)) * esz
        ap = self.t[:, off:off + nb].bitcast(dtype)
        if len(shape) == 2:
            ap = ap.rearrange("p (a b) -> p a b", b=shape[1])
        elif len(shape) == 3:
            ap = ap.rearrange("p (a b c) -> p a b c", b=shape[1], c=shape[2])
        return ap

    def push(self):
        self.marks.append(self.off)

    def pop(self):
        self.off = self.marks.pop()


class WStream:
    def __init__(self, S, units):
        self.S = S
        self.units = units
        self.free = []
        self.nload = 0
        self.nuse = 0
        self.loaded = {}
        self.extra_reads = []

    def add_slot(self, ap, reg):
        self.free.append((ap, reg))
        self.pump()

    def pump(self):
        while self.free and self.nload < len(self.units):
            ap, reg = self.free.pop(0)
            for mk in self.units[self.nload]:
                o, i = mk(ap)
                self.S.dma("pool", lambda e, o=o, i=i: e.dma_start(out=o, in_=i), reads=self.extra_reads, writes=[reg], waw=False)
            self.loaded[self.nload] = (ap, reg)
            self.nload += 1

    def get(self):
        assert self.nuse in self.loaded, "weight unit not loaded (no free slot)"
        r = self.loaded.pop(self.nuse)
        self.nuse += 1
        return r

    def release(self, slot):
        self.free.append(slot)
        self.pump()


def MM(out, lhsT, rhs, start=True, stop=True):
    return lambda e: e.matmul(out, lhsT=lhsT, rhs=rhs, start=start, stop=stop)


def TR(out, in_, ident):
    return lambda e: e.transpose(out=out, in_=in_, identity=ident)


def ACTV(out, in_, func, bias=None, scale=None, accum=None):
    kw = {}
    if bias is not None:
        kw["bias"] = bias
    if scale is not None:
        kw["scale"] = scale
    if accum is not None:
        kw["accum_out"] = accum
    return lambda e: e.activation(out=out, in_=in_, func=func, **kw)


def TS(out, in0, s1, op0, s2=None, op1=None):
    if op1 is None:
        return lambda e: e.tensor_scalar(out=out, in0=in0, scalar1=s1, scalar2=None, op0=op0)
    return lambda e: e.tensor_scalar(out=out, in0=in0, scalar1=s1, scalar2=s2, op0=op0, op1=op1)


def TT(out, in0, in1, op):
    return lambda e: e.tensor_tensor(out=out, in0=in0, in1=in1, op=op)


def STT(out, in0, scalar, in1, op0, op1):
    return lambda e: e.scalar_tensor_tensor(out=out, in0=in0, scalar=scalar, in1=in1, op0=op0, op1=op1)


def SCAN(out, d0, d1):
    return lambda e: e.tensor_tensor_scan(out=out, data0=d0, data1=d1, initial=0.0, op0=ALU.mult, op1=ALU.add)


def RECIP(out, in_):
    return lambda e: e.reciprocal(out=out, in_=in_)


def MEMSET(ap, v):
    return lambda e: e.memset(ap, v)


def COPY(out, in_):
    return lambda e: e.tensor_copy(out=out, in_=in_)


def DMA(out, in_):
    return lambda e: e.dma_start(out=out, in_=in_)


def ASEL(out, in_, pattern, op, fill, base, cm):
    return lambda e: e.affine_select(out=out, in_=in_, pattern=pattern, compare_op=op, fill=fill, base=base, channel_multiplier=cm)


ARENA_BYTES = 204 * 1024
DEBUG = 0
RUN_P1 = RUN_P2 = RUN_P3 = RUN_P4 = RUN_P6 = True
P1_NIT = 32
P1_STAGE = 9

def build_nc():
    nc = bass.Bass("TRN2", target_bir_lowering=False)

    def din(name, shape):
        return nc.dram_tensor(name, shape, F32, kind="ExternalInput").ap()

    xw = din("xw", [2048, 2048])
    w_in = din("w_in", [2048, 11264])
    w_a = din("w_a", [1024, 2048])
    w_b = din("w_b", [1024, 2048])
    w_out = din("w_out", [2048, 2048])
    w_up = din("w_up", [2048, 8192])
    w_down = din("w_down", [8192, 2048])
    lbl_d = din("lbl", [128, 16])
    hgw_d = din("hgw", [128, 1])
    wcols_d = din("wcols", [128, 32])
    wfin_d = din("wfin", [128, 2048])
    wmlp_d = din("wmlp", [128, 2048])
    rbt_d = din("rbt", [128, 16, 384])
    crep_d = din("crep", [128, 16])
    hb_d = din("hbias", [128, 1])
    out_d = nc.dram_tensor("out", [1024, 2048], F32, kind="ExternalOutput").ap()

    def wcols_unit(w, nk, c0, ncols, k0, d0):
        def mk(slot):
            return (slot[:, k0:k0 + nk, d0:d0 + ncols],
                    w[0:nk * 128, c0:c0 + ncols].rearrange("(k p) c -> p k c", p=128))
        return mk

    def wdown_unit(fb):
        def mk(slot):
            return (slot.rearrange("p (k g) c -> p k g c", g=4),
                    w_down[fb * 512:(fb + 1) * 512, :].rearrange("(k p) (g c) -> p k g c", p=128, c=512))
        return mk

    units = []
    for h in range(8):
        units.append([wcols_unit(w_in, 16, s * 1024 + h * 128, 128, 0, s * 128) for s in range(4)])
    for j in range(8):
        units.append([wcols_unit(w_in, 16, 4096 + s * 1024 + j * 128, 128, 0, s * 128) for s in range(3)])
    for mg in range(4):
        units.append([wcols_unit(w_in, 16, 7168 + mg * 512, 512, 0, 0)])
        units.append([wcols_unit(w_in, 16, 9216 + mg * 512, 512, 0, 0)])
        units.append([wcols_unit(w_a, 8, mg * 512, 512, 0, 0), wcols_unit(w_b, 8, mg * 512, 512, 8, 0)])
    for cg in range(4):
        units.append([wcols_unit(w_out, 16, cg * 512, 512, 0, 0)])
    for fb in range(16):
        units.append([wcols_unit(w_up, 16, fb * 512, 512, 0, 0)])
        units.append([wdown_unit(fb)])

    with ExitStack() as es:
        S = Sched(nc, es)
        A = Arena(nc, es, ARENA_BYTES)
        banks = [es.enter_context(nc.psum_tensor("bank%d" % i, [128, 512], F32)) for i in range(8)]
        B = [b[:, :] for b in banks]
        BR = [S.reg("bank%d" % i) for i in range(8)]
        PH = S.reg("phase")

        ident_f = A.alloc([128], F32)
        ident_b = A.alloc([128], BF16)
        ones_f = A.alloc([128], F32)
        maskA = A.alloc([512], F32)
        ones_b = A.alloc([128], BF16)
        rmask = A.alloc([512], F32)
        eps_c = A.alloc([1], F32)
        dummy = A.alloc([1], F32)
        lbl = A.alloc([16], F32)
        lbw = A.alloc([16], F32)
        lb = A.alloc([8], F32)
        oml = A.alloc([8], F32)
        hgw = A.alloc([1], F32)
        hgw_h = A.alloc([1], F32)
        a_col = A.alloc([8], F32)
        b_col = A.alloc([8], F32)
        lna_col = A.alloc([8], F32)
        wcols = A.alloc([32], F32)
        crep = A.alloc([16], F32)
        cbh = A.alloc([16], F32)
        hb = A.alloc([1], F32)
        CR = S.reg("consts")
        C2 = S.reg("consts2")
        CONS = [CR, C2]

        def phase_barrier(extra_reads=()):
            S.barrier()
            S.op("dve", MEMSET(dummy, 0.0), reads=list(extra_reads), writes=[PH])

        for dst, src in ((lbl, lbl_d), (hgw, hgw_d), (wcols, wcols_d), (crep, crep_d), (hb, hb_d)):
            S.dma("sp", DMA(dst, src), writes=[CR], waw=False)
        S.op("pool", MEMSET(ident_f, 0.0), writes=[C2])
        S.op("pool", ASEL(ident_f, ident_f, [[-1, 128]], ALU.not_equal, 1.0, 0, 1), reads=[C2], writes=[C2])
        S.op("pool", COPY(ident_b, ident_f), reads=[C2], writes=[C2])
        S.op("pool", MEMSET(ones_f, 1.0 / 128.0), writes=[C2])
        S.op("pool", MEMSET(eps_c, EPS), writes=[C2])
        S.op("pool", MEMSET(rmask, 1.0), writes=[C2])
        S.op("pool", MEMSET(rmask.rearrange("p (c t) -> p c t", t=64)[:, :, 0:1], 0.0), reads=[C2], writes=[C2])
        S.op("pool", MEMSET(maskA, 1.0), writes=[C2])
        mlo = maskA[0:64, :].rearrange("p (j t) -> p j t", t=128)
        mhi = maskA[64:128, :].rearrange("p (j t) -> p j t", t=128)
        S.op("pool", ASEL(mlo, mlo, [[0, 4], [1, 128]], ALU.is_ge, 0.0, 0, -1), reads=[C2], writes=[C2])
        S.op("pool", ASEL(mlo, mlo, [[0, 4], [-1, 128]], ALU.is_ge, 0.0, 63, 0), reads=[C2], writes=[C2])
        S.op("pool", ASEL(mhi, mhi, [[0, 4], [1, 128]], ALU.is_ge, 0.0, -64, -1), reads=[C2], writes=[C2])
        S.op("pool", MEMSET(ones_b, 1.0 / 128.0), writes=[C2])
        S.op("dve", TT(lbw[:, 0:8], lbl[:, 8:16], lbl[:, 0:8], ALU.subtract), reads=[CR], writes=[C2])
        S.op("act", ACTV(lbw[:, 8:16], lbw[:, 0:8], AF.Exp, scale=-1.0), reads=[C2], writes=[C2])
        S.op("act", ACTV(lbw[:, 0:8], lbw[:, 0:8], AF.Exp), reads=[C2], writes=[C2])
        S.op("dve", TS(lbw, lbw, 1.0, ALU.add), reads=[C2], writes=[C2])
        S.op("dve", RECIP(lb, lbw[:, 0:8]), reads=[C2], writes=[C2])
        S.op("dve", RECIP(oml, lbw[:, 8:16]), reads=[C2], writes=[C2])
        S.op("dve", TS(cbh, crep, hb[:, 0:1], ALU.add), reads=[CR], writes=[C2])
        S.op("dve", TS(a_col, oml, 0.5, ALU.mult), reads=[C2], writes=[C2])
        S.op("dve", TT(b_col, a_col, lb, ALU.add), reads=[C2], writes=[C2])
        S.op("act", ACTV(lna_col, a_col, AF.Ln), reads=[C2], writes=[C2])
        S.op("dve", TS(hgw_h, hgw, 0.5, ALU.mult), reads=[CR], writes=[C2])

        slots = [(A.alloc([16, 512], BF16), S.reg("slot%d" % i)) for i in range(3)]
        H_off = A.reserve(32768)
        uT_h0 = A.at(H_off, [16, 512], BF16)
        uT_h1 = A.at(H_off + 16384, [16, 512], BF16)
        uT_own = A.alloc([16, 1024], BF16)
        R_uh0, R_uh1, R_uo = S.reg("uh0"), S.reg("uh1"), S.reg("uo")
        ws = WStream(S, units)
        ws.add_slot(*slots[0])

        def ublk(blk):
            if blk == 0:
                return uT_h0, R_uh0, 0
            if blk == 1:
                return uT_h1, R_uh1, 0
            return uT_own, R_uo, (blk - 2) * 512

        def norm_transpose(n_tiles, load_tile, dst_of_group, wc0, xts, xregs, ssq, sd, rstd, junk, NR):
            bi = 0
            for g in range(n_tiles // 4):
                for j in range(4):
                    i = g * 4 + j
                    xt, xr = xts[j], xregs[j]
                    src, sreg = load_tile(i, xt, xr)
                    S.op("act", ACTV(junk, src, AF.Square, accum=ssq[:, i:i + 1]), reads=[sreg], writes=[NR])
                    S.op("act", ACTV(sd[:, i:i + 1], ssq[:, i:i + 1], AF.Sqrt, bias=eps_c[:, 0:1], scale=1.0 / 2048.0),
                         reads=[NR, C2], writes=[NR])
                    S.op("dve", RECIP(rstd[:, i:i + 1], sd[:, i:i + 1]), reads=[NR], writes=[NR])
                    S.op("dve", TS(xt, src, rstd[:, i:i + 1], ALU.mult), reads=[NR, sreg], writes=[xr])
                dstT, dreg, tok0 = dst_of_group(g)
                for f in range(16):
                    bk = bi % 8
                    bi += 1
                    fns = [TR(B[bk][:, j * 128:(j + 1) * 128], xts[j][:, f * 128:(f + 1) * 128], ident_f) for j in range(4)]
                    S.group("pe", fns, reads=list(xregs) + CONS, excl=[BR[bk]])
                    sc = wcols[:, wc0 + f:wc0 + f + 1]
                    if f % 2 == 0:
                        S.op("act", ACTV(dstT[:, f, tok0:tok0 + 512], B[bk], AF.Copy, scale=sc),
                             reads=CONS, excl=[BR[bk]], writes=[dreg], waw=False)
                    else:
                        S.op("dve", TS(dstT[:, f, tok0:tok0 + 512], B[bk], sc, ALU.mult),
                             reads=CONS, excl=[BR[bk]], writes=[dreg], waw=False)

        A.push()
        xts = [A.alloc([2048], F32) for _ in range(4)]
        xregs = [S.reg("xt%d" % j) for j in range(4)]
        junk = A.alloc([2048], BF16)
        ssq = A.alloc([16], F32)
        sd = A.alloc([16], F32)
        rstd = A.alloc([16], F32)
        NR = S.reg("nr")

        def load0(i, xt, xr):
            S.dma("sp", DMA(xt, xw[i * 128:(i + 1) * 128, :]), writes=[xr])
            return xt, xr

        norm_transpose(16, load0, ublk, 0, xts, xregs, ssq, sd, rstd, junk, NR)
        ws.extra_reads = list(xregs)
        ws.add_slot(*slots[1])
        ws.add_slot(*slots[2])
        ws.extra_reads = []
        A.pop()
        phase_barrier()

        A.push()
        yaT = A.alloc([8, 1024], BF16)
        ybT = A.alloc([8, 1024], BF16)
        R_ya, R_yb = S.reg("ya"), S.reg("yb")
        A.push()
        f_sq = A.alloc([512], F32)
        f_ga = A.alloc([512], F32)
        f_sn = A.alloc([512], F32)
        f_lg = A.alloc([512], F32)
        f_bb = A.alloc([512], F32)
        f_eb = A.alloc([512], F32)
        f_sg = A.alloc([512], F32)
        f_oT = A.alloc([512], F32)
        f_os = A.alloc([512], F32)
        b_vT = A.alloc([512], BF16)
        b_kT = A.alloc([512], BF16)
        b_qT = A.alloc([512], BF16)
        b_kh = A.alloc([512], BF16)
        b_tok = A.alloc([8, 128], BF16)
        b_As = A.alloc([512], BF16)
        b_os = A.alloc([512], BF16)
        Sring = [A.alloc([9, 128], F32) for _ in range(2)]
        Sb = [A.alloc([9, 128], BF16) for _ in range(2)]
        R = {n: S.reg(n) for n in ["sq", "ga", "sn", "lg", "bb", "eb", "sg", "oT", "os", "vT", "kT", "qT", "kh", "tok", "As", "S",
                                   "Sb0", "Sb1"]}
        RSb = [R["Sb0"], R["Sb1"]]
        BTb = B[3].bitcast(BF16)
        QS = 128.0 ** -0.5
        its = [(h, blk) for h in range(8) for blk in range(4)][:P1_NIT]
        hslot = {}
        f_sg2 = [f_sg, A.alloc([512], F32), A.alloc([512], F32)]
        R["sg0"], R["sg1"], R["sg2"] = S.reg("sg0"), S.reg("sg1"), S.reg("sg2")
        b_qT2 = [b_qT, A.alloc([512], BF16)]
        R["qT0"], R["qT1"] = S.reg("qT0"), S.reg("qT1")

        f_eb2 = [f_eb, A.alloc([512], F32)]
        R["eb0"], R["eb1"] = S.reg("eb0"), S.reg("eb1")

        def slot_of(n):
            h, blk = its[n]
            if h not in hslot:
                hslot[h] = ws.get()
            return hslot[h]

        def pe_proj(n, seg, bk):
            h, blk = its[n]
            sl, sreg = slot_of(n)
            uT, ureg, t0 = ublk(blk)
            fns = [MM(B[bk], sl[:, k, seg * 128:(seg + 1) * 128], uT[:, k, t0:t0 + 512], k == 0, k == 15) for k in range(16)]
            S.group("pe", fns, reads=[sreg, ureg], excl=[BR[bk]])

        def own_(n):
            return 0 <= n < len(its) and its[n][1] >= 2

        def s_hq(n):
            pe_proj(n, 0, 0)
            S.op("act", ACTV(f_sq, B[0], AF.Tanh, scale=0.5), excl=[BR[0]], writes=[R["sq"]])
            S.op("dve", STT(f_sq, f_sq, 1.0, B[0], ALU.add, ALU.mult), reads=[R["sq"]], excl=[BR[0]], writes=[R["sq"]])

        def s_hf(n):
            h = its[n][0]
            pe_proj(n, 1, 1)
            S.op("act", ACTV(f_ga, B[1], AF.Tanh, scale=0.5), excl=[BR[1]], writes=[R["ga"]])
            S.op("act", ACTV(f_sn, B[1], AF.Tanh, scale=-0.5), excl=[BR[1]], writes=[R["sn"]])
            S.op("dve", TS(f_ga, f_ga, a_col[:, h:h + 1], ALU.mult, b_col[:, h:h + 1], ALU.add), reads=[R["ga"]] + CONS, writes=[R["ga"]])

        def s_ln(n):
            S.op("act", ACTV(f_lg, f_ga, AF.Ln), reads=[R["ga"]], writes=[R["lg"]])

        def s_scan(n):
            S.op("dve", SCAN(f_bb, rmask, f_lg), reads=[R["lg"]] + CONS, writes=[R["bb"]])

        def s_exps(n):
            h = its[n][0]
            eb, reb = f_eb2[n % 2], R["eb%d" % (n % 2)]
            S.op("act", ACTV(eb, f_bb, AF.Exp), reads=[R["bb"]], writes=[reb])
            S.op("act", ACTV(f_lg, f_bb, AF.Exp, scale=-1.0, bias=lna_col[:, h:h + 1]), reads=[R["bb"]] + CONS, writes=[R["lg"]])

        def s_kq(n):
            h = its[n][0]
            eb, reb = f_eb2[n % 2], R["eb%d" % (n % 2)]
            S.op("dve", STT(b_kT, f_sn, 1.0, f_lg, ALU.add, ALU.mult), reads=[R["sn"], R["lg"]] + CONS, writes=[R["kT"]])
            if own_(n):
                S.op("dve", STT(b_qT2[n % 2], f_sq, 0.5 * QS, eb, ALU.mult, ALU.mult), reads=[R["sq"], reb], writes=[R["qT%d" % (n % 2)]])
            eb3 = eb.rearrange("p (c t) -> p c t", t=64)
            S.op("dve", TT(b_kh.rearrange("p (c t) -> p c t", t=64), b_kT.rearrange("p (c t) -> p c t", t=64),
                           eb3[:, :, 63:64].broadcast_to([128, 8, 64]), ALU.mult), reads=[R["kT"], reb], writes=[R["kh"]])

        def s_hg(n):
            pe_proj(n, 3, 4)
            sg, rsg = f_sg2[n % 3], R["sg%d" % (n % 3)]
            S.op("act", ACTV(sg, B[4], AF.Tanh, scale=0.5), excl=[BR[4]], writes=[rsg])

        def s_hg_b(n):
            sg, rsg = f_sg2[n % 3], R["sg%d" % (n % 3)]
            S.op("dve", STT(sg, sg, 1.0, B[4], ALU.add, ALU.mult), reads=[rsg], excl=[BR[4]], writes=[rsg])

        def s_hi(n):
            pe_proj(n, 2, 2)
            S.op("act", ACTV(b_vT, B[2], AF.Copy), excl=[BR[2]], writes=[R["vT"]])

        def s_tr(n):
            fns = [TR(BTb[:, j * 128:(j + 1) * 128], b_kh[:, j * 128:(j + 1) * 128], ident_b) for j in range(4)]
            fns += [TR(BTb[:, (4 + j) * 128:(5 + j) * 128], b_vT[:, j * 128:(j + 1) * 128], ident_b) for j in range(4)]
            S.group("pe", fns, reads=[R["kh"], R["vT"]] + CONS, excl=[BR[3]])
            S.op("act", ACTV(b_tok, BTb.rearrange("p (a b) -> p a b", b=128), AF.Copy), excl=[BR[3]], writes=[R["tok"]])

        def s_ms_a(n):
            S.op("pe", MM(B[7], ones_b, b_os), reads=[R["os"]] + CONS, excl=[BR[7]])
            S.op("act", ACTV(f_os, B[7], AF.Ln, bias=eps_c[:, 0:1]), reads=CONS, excl=[BR[7]], writes=[R["os"]])
            S.op("act", ACTV(f_os, f_os, AF.Exp, scale=-0.5), reads=[R["os"]], writes=[R["os"]])

        def s_ms_b(n):
            h, blk = its[n]
            S.op("dve", TT(f_oT, f_oT, f_os, ALU.mult), reads=[R["oT"], R["os"]], writes=[R["oT"]])
            t0 = (blk - 2) * 512
            S.op("dve", STT(yaT[:, h, t0:t0 + 512], f_oT, hgw_h[:, 0:1], f_sg2[n % 3], ALU.mult, ALU.mult),
                 reads=[R["oT"], R["sg%d" % (n % 3)]] + CONS, writes=[R_ya], waw=False)

        def s_u(n):
            fns = []
            for c in range(8):
                j, hh = c // 2, c % 2
                rows = slice(hh * 64, hh * 64 + 64)
                fns.append(MM(B[5 + hh][:, j * 128:(j + 1) * 128], b_tok[rows, j, :], b_tok[rows, 4 + j, :]))
            S.group("pe", fns, reads=[R["tok"]], excl=[BR[5], BR[6]])
            if own_(n):
                q = b_qT2[n % 2]
                fns = [MM(B[4][:, j * 128:(j + 1) * 128], b_kT[:, j * 128:(j + 1) * 128], q[:, j * 128:(j + 1) * 128]) for j in range(4)]
                S.group("pe", fns, reads=[R["kT"], R["qT%d" % (n % 2)]], excl=[BR[4]])
                S.op("dve", TT(b_As, B[4], maskA, ALU.mult), reads=CONS, excl=[BR[4]], writes=[R["As"]])

        def s_chain(n, c0, c1):
            par = its[n][1] % 2
            eb, reb = f_eb2[n % 2], R["eb%d" % (n % 2)]
            for c in range(c0, c1):
                ub = B[5 + c % 2][:, (c // 2) * 128:(c // 2 + 1) * 128]
                sin = Sring[1 - par][:, 8, :] if c == 0 else Sring[par][:, c, :]
                S.op("dve", STT(Sring[par][:, c + 1, :], sin, eb[:, c * 64 + 63:c * 64 + 64], ub, ALU.mult, ALU.add),
                     reads=[R["S"], reb], writes=[R["S"]], excl=[BR[5 + c % 2]])

        def s_cast(n):
            h, blk = its[n]
            par = blk % 2
            if blk >= 2:
                S.op("act", ACTV(Sb[par][:, 1:9, :], Sring[par][:, 1:9, :], AF.Copy), reads=[R["S"]], writes=[RSb[par]], waw=False)
            elif blk == 1:
                S.op("act", ACTV(Sb[par][:, 8, :], Sring[par][:, 8, :], AF.Copy), reads=[R["S"]], writes=[RSb[par]], waw=False)

        def s_o(n):
            par = its[n][1] % 2
            q = b_qT2[n % 2]
            fns = []
            for j in range(4):
                oc = B[7][:, j * 128:(j + 1) * 128]
                fns.append(MM(oc, b_tok[:, 4 + j, :], b_As[:, j * 128:(j + 1) * 128], True, False))
                for c in (2 * j, 2 * j + 1):
                    sprev = Sb[1 - par][:, 8, :] if c == 0 else Sb[par][:, c, :]
                    fns.append(MM(B[7][:, c * 64:(c + 1) * 64], sprev, q[:, c * 64:(c + 1) * 64], False, c == 2 * j + 1))
            S.group("pe", fns, reads=[RSb[0], RSb[1], R["qT%d" % (n % 2)], R["tok"], R["As"]], excl=[BR[7]])
            S.op("act", ACTV(f_oT, B[7], AF.Copy), excl=[BR[7]], writes=[R["oT"]])
            S.op("act", ACTV(b_os, B[7], AF.Square), excl=[BR[7]], writes=[R["os"]])

        if RUN_P1:
            NI = len(its)

            def load_first_half(nx):
                if own_(nx):
                    s_hq(nx)
                s_hf(nx)

            load_first_half(0)
            s_ln(0)
            s_scan(0)
            if own_(0):
                s_hg(0)
            s_hi(0)
            s_exps(0)
            s_kq(0)
            if own_(0):
                s_hg_b(0)
            for n in range(NI):
                h, blk = its[n]
                nx = n + 1 if n + 1 < NI else None
                if blk == 0:
                    S.op("dve", MEMSET(Sring[1][:, 8, :], 0.0), writes=[R["S"]])
                if nx is not None and own_(nx):
                    s_hq(nx)
                s_tr(n)
                if nx is not None:
                    s_hf(nx)
                if own_(n - 1):
                    s_ms_a(n - 1)
                if nx is not None:
                    s_ln(nx)
                s_u(n)
                s_chain(n, 0, 8)
                if own_(n - 1):
                    s_ms_b(n - 1)
                if nx is not None:
                    s_scan(nx)
                    if own_(nx):
                        s_hg(nx)
                s_cast(n)
                if nx is not None:
                    if own_(nx):
                        s_hg_b(nx)
                    s_hi(nx)
                    s_exps(nx)
                    s_kq(nx)
                if own_(n):
                    s_o(n)
                if blk == 0 and h >= 1:
                    ws.release(hslot[h - 1])
            if own_(NI - 1):
                s_ms_a(NI - 1)
                s_ms_b(NI - 1)
            ws.release(hslot[its[NI - 1][0]])
        A.pop()
        phase_barrier()
        if DEBUG == 1:
            dbg = nc.dram_tensor("dbg", [128, 8, 1024], BF16, kind="ExternalOutput").ap()
            S.dma("sp", DMA(dbg, yaT), reads=[R_ya, PH])

        A.push()
        rbt = A.alloc([16, 384], F32)
        aqT = A.alloc([1024], BF16)
        akT = A.alloc([1536], BF16)
        Vext = A.alloc([12, 2, 65], BF16)
        tmpS = [A.alloc([384], F32) for _ in range(3)]
        PT = [A.alloc([640], BF16) for _ in range(3)]
        ybt = A.alloc([8, 128], BF16)
        rcp = A.alloc([16], F32)
        avT = A.alloc([1536], BF16)
        R2 = {n: S.reg(n) for n in ["rbt", "aq", "ak", "V", "tmp0", "tmp1", "tmp2", "PT0", "PT1", "PT2", "ybt", "rcp", "avT"]}
        S.dma("sp", DMA(rbt, rbt_d), reads=[PH], writes=[R2["rbt"]])
        S.op("dve", MEMSET(Vext, 1.0), reads=[PH], writes=[R2["V"]])
        for i in range(3):
            S.op("dve", MEMSET(PT[i], 0.0), reads=[PH], writes=[R2["PT%d" % i]])
        BT6 = B[6].bitcast(BF16)
        SC = 64.0 ** -0.5
        unit_i = 0
        for j in range(8 if RUN_P2 else 0):
            sl, sreg = ws.get()
            pb = 0
            for name, dst, nblk, b0, seg in (("aq", aqT, 2, 2, 0), ("ak", akT, 3, 1, 1)):
                for bi_ in range(nblk):
                    uT, ureg, t0 = ublk(b0 + bi_)
                    bk = 6 + (pb % 2)
                    pb += 1
                    fns = [MM(B[bk], sl[:, k, seg * 128:(seg + 1) * 128], uT[:, k, t0:t0 + 512], k == 0, k == 15) for k in range(16)]
                    S.group("pe", fns, reads=[sreg, ureg], excl=[BR[bk]])
                    if pb % 2:
                        S.op("act", ACTV(dst[:, bi_ * 512:(bi_ + 1) * 512], B[bk], AF.Copy), excl=[BR[bk]], writes=[R2[name]], waw=False)
                    else:
                        S.op("dve", COPY(dst[:, bi_ * 512:(bi_ + 1) * 512], B[bk]), excl=[BR[bk]], writes=[R2[name]], waw=False)
            for bi_ in range(3):
                uT, ureg, t0 = ublk(1 + bi_)
                bk = 6 + (pb % 2)
                pb += 1
                fns = [MM(B[bk], sl[:, k, 256:384], uT[:, k, t0:t0 + 512], k == 0, k == 15) for k in range(16)]
                S.group("pe", fns, reads=[sreg, ureg], excl=[BR[bk]])
                if pb % 2:
                    S.op("act", ACTV(avT[:, bi_ * 512:(bi_ + 1) * 512], B[bk], AF.Copy), excl=[BR[bk]], writes=[R2["avT"]], waw=False)
                else:
                    S.op("dve", COPY(avT[:, bi_ * 512:(bi_ + 1) * 512], B[bk]), excl=[BR[bk]], writes=[R2["avT"]], waw=False)
            for (w0, nw) in ((0, 8), (8, 4)):
                bk = 6 + (pb % 2)
                pb += 1
                bt = B[bk].bitcast(BF16)
                fns = [TR(bt[:, i * 128:(i + 1) * 128], avT[:, (w0 + i) * 128:(w0 + i + 1) * 128], ident_b) for i in range(nw)]
                S.group("pe", fns, reads=[R2["avT"]] + CONS, excl=[BR[bk]])
                S.op("act", ACTV(Vext[:, w0:w0 + nw, :, 0:64], bt[:, 0:nw * 128].rearrange("p (a b c) -> p a b c", b=2, c=64), AF.Copy),
                     excl=[BR[bk]], writes=[R2["V"]], waw=False)
            ws.release((sl, sreg))
            def stage_a(qt, hh, u):
                head = 2 * j + hh
                rows = slice(hh * 64, hh * 64 + 64)
                bsA, bsB = 2 * u, 2 * u + 1
                fns = []
                for o in range(5):
                    kt = qt + 4 - o
                    ob_ = B[bsA][:, o * 128:(o + 1) * 128] if o < 3 else B[bsB][:, (o - 3) * 128:(o - 2) * 128]
                    fns.append(MM(ob_, akT[rows, kt * 128:(kt + 1) * 128], aqT[rows, qt * 128:(qt + 1) * 128]))
                S.group("pe", fns, reads=[R2["aq"], R2["ak"]], excl=[BR[bsA], BR[bsB]])
                tr, pr = R2["tmp%d" % u], R2["PT%d" % u]
                S.op("dve", STT(tmpS[u], B[bsA][:, 0:384], SC, rbt[:, head, :], ALU.mult, ALU.add),
                     reads=[R2["rbt"]], excl=[BR[bsA]], writes=[tr])
                o = 0
                while o < 3:
                    hist = (qt + 4 - o) < 4
                    o2 = o
                    while o2 + 1 < 3 and ((qt + 4 - (o2 + 1)) < 4) == hist:
                        o2 += 1
                    c0, c1 = o * 128, (o2 + 1) * 128
                    if hist:
                        S.op("act", ACTV(PT[u][:, c0:c1], tmpS[u][:, c0:c1], AF.Exp, bias=hb[:, 0:1]),
                             reads=[tr] + CONS, writes=[pr], waw=False)
                    else:
                        S.op("act", ACTV(PT[u][:, c0:c1], tmpS[u][:, c0:c1], AF.Exp), reads=[tr], writes=[pr], waw=False)
                    o = o2 + 1
                h3, h4 = (qt + 1) < 4, qt < 4
                if h3 == h4:
                    bias = (cbh if h3 else crep)[:, head:head + 1]
                    S.op("act", ACTV(PT[u][:, 384:640], B[bsB][:, 0:256], AF.Exp, bias=bias, scale=SC),
                         reads=CONS, excl=[BR[bsB]], writes=[pr], waw=False)
                else:
                    for o in (3, 4):
                        hist = (qt + 4 - o) < 4
                        bias = (cbh if hist else crep)[:, head:head + 1]
                        S.op("act", ACTV(PT[u][:, o * 128:(o + 1) * 128], B[bsB][:, (o - 3) * 128:(o - 2) * 128], AF.Exp, bias=bias, scale=SC),
                             reads=CONS, excl=[BR[bsB]], writes=[pr], waw=False)
                S.op("pool", MEMSET(PT[u][0:64, 4 * 128 + 64:5 * 128], 0.0), writes=[pr])

            def stage_b(qt, hh, u, ob):
                pr = R2["PT%d" % u]
                fns = []
                for o in range(5):
                    kt = qt + 4 - o
                    fns.append(MM(B[ob][:, 0:65], PT[u][:, o * 128:(o + 1) * 128], Vext[:, kt, hh, :], o == 0, o == 4))
                S.group("pe", fns, reads=[pr, R2["V"]], excl=[BR[ob]])
                rc = rcp[:, (ob - 6):(ob - 5)]
                S.op("dve", RECIP(rc, B[ob][:, 64:65]), excl=[BR[ob]], writes=[R2["rcp"]])
                S.op("dve", TS(ybt[:, qt, hh * 64:(hh + 1) * 64], B[ob][:, 0:64], rc, ALU.mult),
                     reads=[R2["rcp"]], excl=[BR[ob]], writes=[R2["ybt"]], waw=False)

            ulist = [(qt, hh) for qt in range(8) for hh in range(2)]
            NU = len(ulist)
            for n_ in range(min(2, NU)):
                stage_a(ulist[n_][0], ulist[n_][1], n_ % 3)
            for n_, (qt, hh) in enumerate(ulist):
                if n_ + 2 < NU:
                    stage_a(ulist[n_ + 2][0], ulist[n_ + 2][1], (n_ + 2) % 3)
                stage_b(qt, hh, n_ % 3, 6 + n_ % 2)
            fns = [TR(BT6[:, qt * 128:(qt + 1) * 128], ybt[:, qt, :], ident_b) for qt in range(8)]
            S.group("pe", fns, reads=[R2["ybt"]] + CONS, excl=[BR[6]])
            S.op("act", ACTV(ybT[:, j, :], BT6, AF.Copy), excl=[BR[6]], writes=[R_yb], waw=False)
        A.pop()
        phase_barrier()
        if DEBUG == 2:
            dbg = nc.dram_tensor("dbg", [128, 8, 1024], BF16, kind="ExternalOutput").ap()
            S.dma("sp", DMA(dbg, ybT), reads=[R_yb, PH])

        mergedT = A.at(H_off, [16, 1024], BF16)
        R_mg = S.reg("merged")
        A.push()
        g_a = [A.alloc([512], F32) for _ in range(2)]
        g_b = [A.alloc([512], F32) for _ in range(2)]
        R3 = {n: S.reg(n) for n in ["ga0", "ga1", "gb0", "gb1"]}
        it3 = 0
        for mg in range(4 if RUN_P3 else 0):
            sx = ws.get()
            sy = ws.get()
            sz = ws.get()
            for mi in range(4):
                m = mg * 4 + mi
                cs = slice(mi * 128, (mi + 1) * 128)
                for t2 in range(2):
                    u = it3 % 2
                    it3 += 1
                    bb = 4 * u
                    ts_ = slice(t2 * 512, (t2 + 1) * 512)
                    S.group("pe", [MM(B[bb + 0], sx[0][:, k, cs], uT_own[:, k, ts_], k == 0, k == 15) for k in range(16)],
                            reads=[sx[1], R_uo], excl=[BR[bb + 0]])
                    S.group("pe", [MM(B[bb + 1], sy[0][:, k, cs], uT_own[:, k, ts_], k == 0, k == 15) for k in range(16)],
                            reads=[sy[1], R_uo], excl=[BR[bb + 1]])
                    S.group("pe", [MM(B[bb + 2], sz[0][:, k, cs], yaT[:, k, ts_], k == 0, k == 7) for k in range(8)],
                            reads=[sz[1], R_ya], excl=[BR[bb + 2]])
                    S.group("pe", [MM(B[bb + 3], sz[0][:, 8 + k, cs], ybT[:, k, ts_], k == 0, k == 7) for k in range(8)],
                            reads=[sz[1], R_yb], excl=[BR[bb + 3]])
                    ra, rb_ = R3["ga%d" % u], R3["gb%d" % u]
                    S.op("act", ACTV(g_a[u], B[bb + 0], AF.Sigmoid), excl=[BR[bb + 0]], writes=[ra])
                    S.op("act", ACTV(g_b[u], B[bb + 1], AF.Sigmoid), excl=[BR[bb + 1]], writes=[rb_])
                    S.op("dve", TT(g_a[u], g_a[u], B[bb + 2], ALU.mult), reads=[ra], excl=[BR[bb + 2]], writes=[ra])
                    S.op("dve", TT(g_b[u], g_b[u], B[bb + 3], ALU.mult), reads=[rb_], excl=[BR[bb + 3]], writes=[rb_])
                    S.op("dve", TT(mergedT[:, m, ts_], g_a[u], g_b[u], ALU.add), reads=[ra, rb_, PH], writes=[R_mg], waw=False)
            ws.release(sx)
            ws.release(sy)
            ws.release(sz)
        A.pop()
        A.pop()
        phase_barrier()
        if DEBUG == 3:
            dbg = nc.dram_tensor("dbg", [128, 16, 1024], BF16, kind="ExternalOutput").ap()
            S.dma("sp", DMA(dbg, mergedT), reads=[R_mg, PH])

        hres = A.alloc([8, 2048], F32)
        R_h = [S.reg("h%d" % t) for t in range(8)]
        u2T = uT_own
        R_u2 = S.reg("u2T")
        A.push()
        hnb = [A.alloc([2048], BF16) for _ in range(2)]
        R_hn = [S.reg("hnb%d" % i) for i in range(2)]
        wrep_mlp = A.alloc([2048], F32)
        ssq5 = A.alloc([8], F32)
        sd5 = A.alloc([8], F32)
        rstd5 = A.alloc([8], F32)
        NR5 = S.reg("nr5")
        R_wm = S.reg("wrep_mlp")
        S.dma("sp", DMA(wrep_mlp, wmlp_d), reads=[PH], writes=[R_wm])
        S.op("dve", MEMSET(ssq5, 0.0), reads=[PH], writes=[NR5] + R_hn)
        for tt in range(8):
            S.dma("sp", DMA(hres[:, tt, :], xw[1024 + tt * 128:1024 + (tt + 1) * 128, :]), reads=[PH], writes=[R_h[tt]])
        BT6b, BT7b = B[6].bitcast(BF16), B[7].bitcast(BF16)

        def emit_norm_stats(tt):
            u = tt % 2
            hn, rhn = hnb[u], R_hn[u]
            S.op("act", ACTV(hn, hres[:, tt, :], AF.Square, accum=ssq5[:, tt:tt + 1]), reads=[R_h[tt]], writes=[NR5, rhn])
            S.op("act", ACTV(sd5[:, tt:tt + 1], ssq5[:, tt:tt + 1], AF.Sqrt, bias=eps_c[:, 0:1], scale=1.0 / 2048.0),
                 reads=[NR5] + CONS, writes=[NR5])
            S.op("dve", RECIP(rstd5[:, tt:tt + 1], sd5[:, tt:tt + 1]), reads=[NR5], writes=[NR5])
            S.op("dve", STT(hn, hres[:, tt, :], rstd5[:, tt:tt + 1], wrep_mlp, ALU.mult, ALU.mult),
                 reads=[R_h[tt], NR5, R_wm], writes=[rhn])

        def emit_norm_trans(tt):
            u = tt % 2
            hn, rhn = hnb[u], R_hn[u]
            for half, bt, bk in ((0, BT6b, 6), (1, BT7b, 7)):
                fns = [TR(bt[:, i * 128:(i + 1) * 128], hn[:, (half * 8 + i) * 128:(half * 8 + i + 1) * 128], ident_b) for i in range(8)]
                S.group("pe", fns, reads=[rhn] + CONS, excl=[BR[bk]])
                dst = u2T[:, half * 8:(half + 1) * 8, tt * 128:(tt + 1) * 128]
                srcv = bt.rearrange("p (a b) -> p a b", b=128)
                if half == 0:
                    S.op("act", ACTV(dst, srcv, AF.Copy), excl=[BR[bk]], writes=[R_u2], waw=False)
                else:
                    S.op("dve", COPY(dst, srcv), excl=[BR[bk]], writes=[R_u2], waw=False)

        it4 = 0
        for cg in range(4 if RUN_P4 else 0):
            sl, sreg = ws.get()
            for tt in range(8):
                bk = it4 % 6
                it4 += 1
                S.group("pe", [MM(B[bk], mergedT[:, k, tt * 128:(tt + 1) * 128], sl[:, k, :], k == 0, k == 15) for k in range(16)],
                        reads=[sreg, R_mg], excl=[BR[bk]])
                hv = hres[:, tt, cg * 512:(cg + 1) * 512]
                S.op("dve", TT(hv, hv, B[bk], ALU.add), reads=[R_h[tt]], excl=[BR[bk]], writes=[R_h[tt]])
                if cg == 3 and tt >= 1:
                    emit_norm_stats(tt - 1)
                if cg == 3 and tt >= 2:
                    emit_norm_trans(tt - 2)
            ws.release((sl, sreg))
        if RUN_P4:
            emit_norm_stats(7)
            emit_norm_trans(6)
            emit_norm_trans(7)
        else:
            for tt in range(8):
                emit_norm_stats(tt)
                emit_norm_trans(tt)
        A.pop()
        phase_barrier([s[1] for s in slots])

        A.push()
        slot4 = (A.at(H_off, [16, 512], BF16), S.reg("slot3"))
        hid = [A.at(H_off + 16384 + i * 8192, [4, 1024], BF16) for i in range(2)]
        rr = [A.alloc([512], F32) for _ in range(2)]
        R6 = {n: S.reg(n) for n in ["hid0", "hid1", "rr0", "rr1"]}
        S.op("dve", MEMSET(dummy, 0.0), reads=[PH, slot4[1]], writes=[R6["hid0"], R6["hid1"]])
        ws.add_slot(*slot4)
        upc = [0]

        def emit_up_unit(fb, un, sa):
            fi, half = un // 2, un % 2
            par = fb % 2
            bk = upc[0] % 4
            u = upc[0] % 2
            upc[0] += 1
            S.group("pe", [MM(B[bk], sa[0][:, k, fi * 128:(fi + 1) * 128], u2T[:, k, half * 512:(half + 1) * 512], k == 0, k == 15)
                           for k in range(16)], reads=[sa[1], R_u2], excl=[BR[bk]])
            S.op("act", ACTV(rr[u], B[bk], AF.Relu), excl=[BR[bk]], writes=[R6["rr%d" % u]])
            S.op("dve", TT(hid[par][:, fi, half * 512:(half + 1) * 512], rr[u], rr[u], ALU.mult),
                 reads=[R6["rr%d" % u]], writes=[R6["hid%d" % par]], waw=False)

        def emit_down_tt(fb, tt, sb):
            par = fb % 2
            fns = []
            for k in range(4):
                for cg in range(4):
                    fns.append(MM(B[4 + cg], hid[par][:, k, tt * 128:(tt + 1) * 128], sb[0][:, k * 4 + cg, :], k == 0, k == 3))
            S.group("pe", fns, reads=[sb[1], R6["hid%d" % par]], excl=[BR[4], BR[5], BR[6], BR[7]])
            for cg in range(4):
                hv = hres[:, tt, cg * 512:(cg + 1) * 512]
                S.op("dve", TT(hv, hv, B[4 + cg], ALU.add), reads=[R_h[tt]], excl=[BR[4 + cg]], writes=[R_h[tt]])

        wrep = A.alloc([2048], F32)
        junk = A.alloc([2048], BF16)
        ssq = A.alloc([8], F32)
        sd = A.alloc([8], F32)
        rstd = A.alloc([8], F32)
        R7 = {n: S.reg(n) for n in ["wrep", "nr"]}
        S.dma("sp", DMA(wrep, wfin_d), reads=[PH], writes=[R7["wrep"]])
        S.op("dve", MEMSET(ssq, 0.0), reads=[PH], writes=[R7["nr"]])

        def emit_final(tt):
            S.op("act", ACTV(junk, hres[:, tt, :], AF.Square, accum=ssq[:, tt:tt + 1]), reads=[R_h[tt], PH], writes=[R7["nr"]])
            S.op("act", ACTV(sd[:, tt:tt + 1], ssq[:, tt:tt + 1], AF.Sqrt, bias=eps_c[:, 0:1], scale=1.0 / 2048.0),
                 reads=[R7["nr"]] + CONS, writes=[R7["nr"]])
            S.op("dve", RECIP(rstd[:, tt:tt + 1], sd[:, tt:tt + 1]), reads=[R7["nr"]], writes=[R7["nr"]])
            S.op("dve", STT(hres[:, tt, :], hres[:, tt, :], rstd[:, tt:tt + 1], wrep, ALU.mult, ALU.mult),
                 reads=[R_h[tt], R7["nr"], R7["wrep"]], writes=[R_h[tt]])
            S.dma("sp", DMA(out_d[tt * 128:(tt + 1) * 128, :], hres[:, tt, :]), reads=[R_h[tt]])

        if RUN_P6:
            sa = ws.get()
            for un in range(8):
                emit_up_unit(0, un, sa)
            ws.release(sa)
            for fb in range(16):
                sb = ws.get()
                sa = ws.get() if fb + 1 < 16 else None
                for tt in range(8):
                    emit_down_tt(fb, tt, sb)
                    if sa is not None:
                        emit_up_unit(fb + 1, tt, sa)
                    else:
                        emit_final(tt)
                ws.release(sb)
                if sa is not None:
                    ws.release(sa)
        else:
            for tt in range(8):
                emit_final(tt)
        A.pop()
        S.finish()
        S.replay()
        build_nc.peak = A.peak
        build_nc.log = A.log
    return nc


def _host_prep(inputs):
    x = np.asarray(inputs["x"], dtype=np.float32)
    rb = np.asarray(inputs["rel_bias"], dtype=np.float32)[0]
    lbl = np.ascontiguousarray(np.asarray(inputs["lb_logits"], np.float32).reshape(2, 8, 128).transpose(2, 0, 1).reshape(128, 16))
    hgw = np.ascontiguousarray(np.asarray(inputs["hg_norm_w"], np.float32)[0].reshape(128, 1))
    wc = np.concatenate([np.asarray(inputs["norm_mix_w"], np.float32)[0].reshape(16, 128).T,
                         np.asarray(inputs["norm_mlp_w"], np.float32)[0].reshape(16, 128).T], axis=1)
    wc = np.ascontiguousarray(wc)
    wfin = np.ascontiguousarray(np.broadcast_to(np.asarray(inputs["norm_final_w"], np.float32).reshape(1, 2048), (128, 2048)))
    wmlp = np.ascontiguousarray(np.broadcast_to(np.asarray(inputs["norm_mlp_w"], np.float32)[0].reshape(1, 2048), (128, 2048)))
    k = np.arange(128)[:, None, None]
    o = np.arange(3)[None, :, None]
    t = np.arange(128)[None, None, :]
    idx = np.clip(128 * o + t - k, -256, 256) + 256
    rbt = rb[:, idx]
    invalid = np.broadcast_to((o == 0) & (k >= 64) & (t < 64), idx.shape)
    rbt = np.where(invalid[None], np.float32(NEG), rbt)
    rbt = np.ascontiguousarray(rbt.transpose(1, 0, 2, 3).reshape(128, 16, 384)).astype(np.float32)
    crep = np.ascontiguousarray(np.broadcast_to(rb[:, 512][None, :], (128, 16))).astype(np.float32)
    shared = {
        "w_in": np.ascontiguousarray(np.asarray(inputs["w_in"], np.float32)[0]),
        "w_a": np.ascontiguousarray(np.asarray(inputs["w_branch_a"], np.float32)[0]),
        "w_b": np.ascontiguousarray(np.asarray(inputs["w_branch_b"], np.float32)[0]),
        "w_out": np.ascontiguousarray(np.asarray(inputs["w_out"], np.float32)[0]),
        "w_up": np.ascontiguousarray(np.asarray(inputs["w_up"], np.float32)[0]),
        "w_down": np.ascontiguousarray(np.asarray(inputs["w_down"], np.float32)[0]),
        "lbl": lbl, "hgw": hgw, "wcols": wc, "wfin": wfin, "wmlp": wmlp, "rbt": rbt, "crep": crep,
    }
    in_maps = []
    for c in range(8):
        b, half = c // 2, c % 2
        xwin = np.zeros((2048, 2048), np.float32)
        if half == 1:
            xwin[:] = x[b]
        else:
            xwin[1024:] = x[b, 0:1024]
        d = dict(shared)
        d["xw"] = xwin
        d["hbias"] = np.full((128, 1), 0.0 if half == 1 else NEG, np.float32)
        in_maps.append(d)
    return in_maps


_NC_CACHE = {}


def kernel(**inputs):
    in_maps = _host_prep(inputs)
    if "nc" not in _NC_CACHE:
        _NC_CACHE["nc"] = build_nc()
    nc = _NC_CACHE["nc"]
    res = run_bass_kernel_spmd(nc, in_maps, core_ids=list(range(8)))
    out = np.zeros((4, 2048, 2048), np.float32)
    for c in range(8):
        b, half = c // 2, c % 2
        out[b, half * 1024:(half + 1) * 1024] = res.results[c]["out"]
    return out
```

```python
from contextlib import ExitStack

import numpy as np
import concourse.bass as bass
import concourse.mybir as mybir
from concourse.bass_utils import run_bass_kernel_spmd

F32 = mybir.dt.float32
BF16 = mybir.dt.bfloat16
U8 = mybir.dt.uint8
AF = mybir.ActivationFunctionType
ALU = mybir.AluOpType

NEG = -30000.0
EPS = 1e-6


class Reg:
    __slots__ = ("name", "w", "r")

    def __init__(self, name):
        self.name = name
        self.w = {}
        self.r = {}


class Sched:
    LIMIT = 30000
    NDQ = 6

    def __init__(self, nc, es):
        self.nc = nc
        self.es = es
        self.engs = ("pe", "act", "dve", "pool", "sp")
        self.streams = {e: [] for e in self.engs}
        self.seen = {e: {} for e in self.engs}
        self.cur = {}
        self.nsem = 0
        for e in ("pe", "act", "dve", "pool"):
            self._newsem(e)
        self.dq = {q: [[self._alloc(), 0] for _ in range(self.NDQ)] for q in ("sp", "pool")}
        self.dq_rr = {"sp": 0, "pool": 0}

    def _alloc(self):
        sem = self.es.enter_context(self.nc.semaphore("s%d" % self.nsem))
        self.nsem += 1
        return (self.nsem, sem)

    def _newsem(self, e):
        self.cur[e] = [self._alloc(), 0]

    def reg(self, name="r"):
        return Reg(name)

    def _waits(self, eng, reads, writes, excl, waw):
        need = {}

        def add(tok, same_ok):
            key, sem, val, teng = tok
            if teng == eng and not same_ok:
                return
            if self.seen[eng].get(key, 0) >= val:
                return
            if key in need and need[key][1] >= val:
                return
            need[key] = (sem, val)

        for r in reads:
            for t in r.w.values():
                add(t, True)
        for w in writes:
            if waw:
                for t in w.w.values():
                    add(t, True)
            for t in w.r.values():
                add(t, True)
        for x in excl:
            for t in x.w.values():
                add(t, True)
            for t in x.r.values():
                add(t, True)
        for key, (sem, val) in need.items():
            self.seen[eng][key] = val
            self.streams[eng].append(lambda e, sem=sem, val=val: e.wait_ge(sem, val))

    def _mark(self, tok, reads, writes, excl, waw):
        key = tok[0]
        for r in reads:
            r.r[key] = tok
        for w in writes:
            if waw or w.r:
                w.w = {key: tok}
            else:
                w.w[key] = tok
            w.r = {}
        for x in excl:
            x.w = {key: tok}
            x.r = {}

    def _tick(self, eng):
        cur = self.cur[eng]
        if cur[1] >= self.LIMIT:
            self._newsem(eng)
            cur = self.cur[eng]
        (key, sem) = cur[0]
        cur[1] += 1
        return key, sem, cur[1]

    def op(self, eng, fn, reads=(), writes=(), excl=(), waw=True):
        self.group(eng, [fn], reads, writes, excl, waw)

    def group(self, eng, fns, reads=(), writes=(), excl=(), waw=True):
        self._waits(eng, reads, writes, excl, waw)
        key, sem, val = self._tick(eng)
        for fn in fns[:-1]:
            self.streams[eng].append(lambda e, fn=fn: fn(e))
        fn = fns[-1]
        self.streams[eng].append(lambda e, fn=fn, sem=sem: fn(e).then_inc(sem, 1))
        self._mark((key, sem, val, eng), reads, writes, excl, waw)

    def dma(self, q, fn, reads=(), writes=(), waw=True):
        i = self.dq_rr[q]
        self.dq_rr[q] = (i + 1) % self.NDQ
        slot = self.dq[q][i]
        (key, sem) = slot[0]
        if slot[1] > 0 and self.seen[q].get(key, 0) < 16 * slot[1]:
            v = 16 * slot[1]
            self.seen[q][key] = v
            self.streams[q].append(lambda e, sem=sem, v=v: e.wait_ge(sem, v))
        self._waits(q, reads, writes, (), waw)
        slot[1] += 1
        val = 16 * slot[1]
        self.streams[q].append(lambda e, fn=fn, sem=sem: fn(e).then_inc(sem, 16))
        self._mark((key, sem, val, "dma_" + q), reads, writes, (), waw)

    def barrier(self, engs=("pe", "act", "dve")):
        for e in engs:
            for o in engs:
                if o == e:
                    continue
                (key, sem), cnt = self.cur[o]
                if cnt > 0 and self.seen[e].get(key, 0) < cnt:
                    self.seen[e][key] = cnt
                    self.streams[e].append(lambda en, sem=sem, cnt=cnt: en.wait_ge(sem, cnt))

    def finish(self):
        for q in ("sp", "pool"):
            for (key, sem), cnt in self.dq[q]:
                if cnt > 0:
                    v = 16 * cnt
                    self.streams["sp"].append(lambda e, sem=sem, v=v: e.wait_ge(sem, v))

    def replay(self):
        with self.nc.Block() as block:
            @block.sync
            def _(e):
                for f in self.streams["sp"]:
                    f(e)

            @block.tensor
            def _(e):
                for f in self.streams["pe"]:
                    f(e)

            @block.scalar
            def _(e):
                for f in self.streams["act"]:
                    f(e)

            @block.vector
            def _(e):
                for f in self.streams["dve"]:
                    f(e)

            @block.gpsimd
            def _(e):
                for f in self.streams["pool"]:
                    f(e)


class Arena:
    def __init__(self, nc, es, nbytes, name="arena"):
        self.t = es.enter_context(nc.sbuf_tensor(name, [128, nbytes], U8))
        self.nbytes = nbytes
        self.off = 0
        self.marks = []
        self.peak = 0
        self.log = []

    def alloc(self, shape, dtype):
        esz = 2 if dtype == BF16 else 4
        n = int(np.prod(shape))
        nb = n * esz
        off = self.reserve(nb)
        return self.at(off, shape, dtype)

    def reserve(self, nb):
        off = (self.off + 63) // 64 * 64
        assert off + nb <= self.nbytes, ("arena overflow", off, nb, self.nbytes)
        self.off = off + nb
        self.peak = max(self.peak, self.off)
        return off

    def at(self, off, shape, dtype):
        self.log.append((off, list(shape), "bf16" if dtype == BF16 else "f32"))
        esz = 2 if dtype == BF16 else 4
        nb = int(np.prod(shape)) * esz
        ap = self.t[:, off:off + nb].bitcast(dtype)
        if len(shape) == 2:
            ap = ap.rearrange("p (a b) -> p a b", b=shape[1])
        elif len(shape) == 3:
            ap = ap.rearrange("p (a b c) -> p a b c", b=shape[1], c=shape[2])
        return ap

    def push(self):
        self.marks.append(self.off)

    def pop(self):
        self.off = self.marks.pop()


class WStream:
    def __init__(self, S, units):
        self.S = S
        self.units = units
        self.free = []
        self.nload = 0
        self.nuse = 0
        self.loaded = {}
        self.extra_reads = []

    def add_slot(self, ap, reg):
        self.free.append((ap, reg))
        self.pump()

    def pump(self):
        while self.free and self.nload < len(self.units):
            ap, reg = self.free.pop(0)
            for mk in self.units[self.nload]:
                o, i = mk(ap)
                self.S.dma("pool", lambda e, o=o, i=i: e.dma_start(out=o, in_=i), reads=self.extra_reads, writes=[reg], waw=False)
            self.loaded[self.nload] = (ap, reg)
            self.nload += 1

    def get(self):
        assert self.nuse in self.loaded, "weight unit not loaded (no free slot)"
        r = self.loaded.pop(self.nuse)
        self.nuse += 1
        return r

    def release(self, slot):
        self.free.append(slot)
        self.pump()


def MM(out, lhsT, rhs, start=True, stop=True):
    return lambda e: e.matmul(out, lhsT=lhsT, rhs=rhs, start=start, stop=stop)


def TR(out, in_, ident):
    return lambda e: e.transpose(out=out, in_=in_, identity=ident)


def ACTV(out, in_, func, bias=None, scale=None, accum=None):
    kw = {}
    if bias is not None:
        kw["bias"] = bias
    if scale is not None:
        kw["scale"] = scale
    if accum is not None:
        kw["accum_out"] = accum
    return lambda e: e.activation(out=out, in_=in_, func=func, **kw)


def TS(out, in0, s1, op0, s2=None, op1=None):
    if op1 is None:
        return lambda e: e.tensor_scalar(out=out, in0=in0, scalar1=s1, scalar2=None, op0=op0)
    return lambda e: e.tensor_scalar(out=out, in0=in0, scalar1=s1, scalar2=s2, op0=op0, op1=op1)


def TT(out, in0, in1, op):
    return lambda e: e.tensor_tensor(out=out, in0=in0, in1=in1, op=op)


def STT(out, in0, scalar, in1, op0, op1):
    return lambda e: e.scalar_tensor_tensor(out=out, in0=in0, scalar=scalar, in1=in1, op0=op0, op1=op1)


def SCAN(out, d0, d1):
    return lambda e: e.tensor_tensor_scan(out=out, data0=d0, data1=d1, initial=0.0, op0=ALU.mult, op1=ALU.add)


def RECIP(out, in_):
    return lambda e: e.reciprocal(out=out, in_=in_)


def MEMSET(ap, v):
    return lambda e: e.memset(ap, v)


def COPY(out, in_):
    return lambda e: e.tensor_copy(out=out, in_=in_)


def DMA(out, in_):
    return lambda e: e.dma_start(out=out, in_=in_)


def ASEL(out, in_, pattern, op, fill, base, cm):
    return lambda e: e.affine_select(out=out, in_=in_, pattern=pattern, compare_op=op, fill=fill, base=base, channel_multiplier=cm)


ARENA_BYTES = 204 * 1024
DEBUG = 0
RUN_P1 = RUN_P2 = RUN_P3 = RUN_P4 = RUN_P6 = True
P1_NIT = 32
P1_STAGE = 9

def build_nc():
    nc = bass.Bass("TRN2", target_bir_lowering=False)

    def din(name, shape):
        return nc.dram_tensor(name, shape, F32, kind="ExternalInput").ap()

    xw = din("xw", [2048, 2048])
    w_in = din("w_in", [2048, 11264])
    w_a = din("w_a", [1024, 2048])
    w_b = din("w_b", [1024, 2048])
    w_out = din("w_out", [2048, 2048])
    w_up = din("w_up", [2048, 8192])
    w_down = din("w_down", [8192, 2048])
    lbl_d = din("lbl", [128, 16])
    hgw_d = din("hgw", [128, 1])
    wcols_d = din("wcols", [128, 32])
    wfin_d = din("wfin", [128, 2048])
    wmlp_d = din("wmlp", [128, 2048])
    rbt_d = din("rbt", [128, 16, 384])
    crep_d = din("crep", [128, 16])
    hb_d = din("hbias", [128, 1])
    out_d = nc.dram_tensor("out", [1024, 2048], F32, kind="ExternalOutput").ap()

    def wcols_unit(w, nk, c0, ncols, k0, d0):
        def mk(slot):
            return (slot[:, k0:k0 + nk, d0:d0 + ncols],
                    w[0:nk * 128, c0:c0 + ncols].rearrange("(k p) c -> p k c", p=128))
        return mk

    def wdown_unit(fb):
        def mk(slot):
            return (slot.rearrange("p (k g) c -> p k g c", g=4),
                    w_down[fb * 512:(fb + 1) * 512, :].rearrange("(k p) (g c) -> p k g c", p=128, c=512))
        return mk

    units = []
    for h in range(8):
        units.append([wcols_unit(w_in, 16, s * 1024 + h * 128, 128, 0, s * 128) for s in range(4)])
    for j in range(8):
        units.append([wcols_unit(w_in, 16, 4096 + s * 1024 + j * 128, 128, 0, s * 128) for s in range(3)])
    for mg in range(4):
        units.append([wcols_unit(w_in, 16, 7168 + mg * 512, 512, 0, 0)])
        units.append([wcols_unit(w_in, 16, 9216 + mg * 512, 512, 0, 0)])
        units.append([wcols_unit(w_a, 8, mg * 512, 512, 0, 0), wcols_unit(w_b, 8, mg * 512, 512, 8, 0)])
    for cg in range(4):
        units.append([wcols_unit(w_out, 16, cg * 512, 512, 0, 0)])
    for fb in range(16):
        units.append([wcols_unit(w_up, 16, fb * 512, 512, 0, 0)])
        units.append([wdown_unit(fb)])

    with ExitStack() as es:
        S = Sched(nc, es)
        A = Arena(nc, es, ARENA_BYTES)
        banks = [es.enter_context(nc.psum_tensor("bank%d" % i, [128, 512], F32)) for i in range(8)]
        B = [b[:, :] for b in banks]
        BR = [S.reg("bank%d" % i) for i in range(8)]
        PH = S.reg("phase")

        ident_f = A.alloc([128], F32)
        ident_b = A.alloc([128], BF16)
        ones_f = A.alloc([128], F32)
        maskA = A.alloc([512], F32)
        ones_b = A.alloc([128], BF16)
        rmask = A.alloc([512], F32)
        eps_c = A.alloc([1], F32)
        dummy = A.alloc([1], F32)
        lbl = A.alloc([16], F32)
        lbw = A.alloc([16], F32)
        lb = A.alloc([8], F32)
        oml = A.alloc([8], F32)
        hgw = A.alloc([1], F32)
        hgw_h = A.alloc([1], F32)
        a_col = A.alloc([8], F32)
        b_col = A.alloc([8], F32)
        lna_col = A.alloc([8], F32)
        wcols = A.alloc([32], F32)
        crep = A.alloc([16], F32)
        cbh = A.alloc([16], F32)
        hb = A.alloc([1], F32)
        CR = S.reg("consts")
        C2 = S.reg("consts2")
        CONS = [CR, C2]

        def phase_barrier(extra_reads=()):
            S.barrier()
            S.op("dve", MEMSET(dummy, 0.0), reads=list(extra_reads), writes=[PH])

        for dst, src in ((lbl, lbl_d), (hgw, hgw_d), (wcols, wcols_d), (crep, crep_d), (hb, hb_d)):
            S.dma("sp", DMA(dst, src), writes=[CR], waw=False)
        S.op("pool", MEMSET(ident_f, 0.0), writes=[C2])
        S.op("pool", ASEL(ident_f, ident_f, [[-1, 128]], ALU.not_equal, 1.0, 0, 1), reads=[C2], writes=[C2])
        S.op("pool", COPY(ident_b, ident_f), reads=[C2], writes=[C2])
        S.op("pool", MEMSET(ones_f, 1.0 / 128.0), writes=[C2])
        S.op("pool", MEMSET(eps_c, EPS), writes=[C2])
        S.op("pool", MEMSET(rmask, 1.0), writes=[C2])
        S.op("pool", MEMSET(rmask.rearrange("p (c t) -> p c t", t=64)[:, :, 0:1], 0.0), reads=[C2], writes=[C2])
        S.op("pool", MEMSET(maskA, 1.0), writes=[C2])
        mlo = maskA[0:64, :].rearrange("p (j t) -> p j t", t=128)
        mhi = maskA[64:128, :].rearrange("p (j t) -> p j t", t=128)
        S.op("pool", ASEL(mlo, mlo, [[0, 4], [1, 128]], ALU.is_ge, 0.0, 0, -1), reads=[C2], writes=[C2])
        S.op("pool", ASEL(mlo, mlo, [[0, 4], [-1, 128]], ALU.is_ge, 0.0, 63, 0), reads=[C2], writes=[C2])
        S.op("pool", ASEL(mhi, mhi, [[0, 4], [1, 128]], ALU.is_ge, 0.0, -64, -1), reads=[C2], writes=[C2])
        S.op("pool", MEMSET(ones_b, 1.0 / 128.0), writes=[C2])
        S.op("dve", TT(lbw[:, 0:8], lbl[:, 8:16], lbl[:, 0:8], ALU.subtract), reads=[CR], writes=[C2])
        S.op("act", ACTV(lbw[:, 8:16], lbw[:, 0:8], AF.Exp, scale=-1.0), reads=[C2], writes=[C2])
        S.op("act", ACTV(lbw[:, 0:8], lbw[:, 0:8], AF.Exp), reads=[C2], writes=[C2])
        S.op("dve", TS(lbw, lbw, 1.0, ALU.add), reads=[C2], writes=[C2])
        S.op("dve", RECIP(lb, lbw[:, 0:8]), reads=[C2], writes=[C2])
        S.op("dve", RECIP(oml, lbw[:, 8:16]), reads=[C2], writes=[C2])
        S.op("dve", TS(cbh, crep, hb[:, 0:1], ALU.add), reads=[CR], writes=[C2])
        S.op("dve", TS(a_col, oml, 0.5, ALU.mult), reads=[C2], writes=[C2])
        S.op("dve", TT(b_col, a_col, lb, ALU.add), reads=[C2], writes=[C2])
        S.op("act", ACTV(lna_col, a_col, AF.Ln), reads=[C2], writes=[C2])
        S.op("dve", TS(hgw_h, hgw, 0.5, ALU.mult), reads=[CR], writes=[C2])

        slots = [(A.alloc([16, 512], BF16), S.reg("slot%d" % i)) for i in range(3)]
        H_off = A.reserve(32768)
        uT_h0 = A.at(H_off, [16, 512], BF16)
        uT_h1 = A.at(H_off + 16384, [16, 512], BF16)
        uT_own = A.alloc([16, 1024], BF16)
        R_uh0, R_uh1, R_uo = S.reg("uh0"), S.reg("uh1"), S.reg("uo")
        ws = WStream(S, units)
        ws.add_slot(*slots[0])

        def ublk(blk):
            if blk == 0:
                return uT_h0, R_uh0, 0
            if blk == 1:
                return uT_h1, R_uh1, 0
            return uT_own, R_uo, (blk - 2) * 512

        def norm_transpose(n_tiles, load_tile, dst_of_group, wc0, xts, xregs, ssq, sd, rstd, junk, NR):
            bi = 0
            for g in range(n_tiles // 4):
                for j in range(4):
                    i = g * 4 + j
                    xt, xr = xts[j], xregs[j]
                    src, sreg = load_tile(i, xt, xr)
                    S.op("act", ACTV(junk, src, AF.Square, accum=ssq[:, i:i + 1]), reads=[sreg], writes=[NR])
                    S.op("act", ACTV(sd[:, i:i + 1], ssq[:, i:i + 1], AF.Sqrt, bias=eps_c[:, 0:1], scale=1.0 / 2048.0),
                         reads=[NR, C2], writes=[NR])
                    S.op("dve", RECIP(rstd[:, i:i + 1], sd[:, i:i + 1]), reads=[NR], writes=[NR])
                    S.op("dve", TS(xt, src, rstd[:, i:i + 1], ALU.mult), reads=[NR, sreg], writes=[xr])
                dstT, dreg, tok0 = dst_of_group(g)
                for f in range(16):
                    bk = bi % 8
                    bi += 1
                    fns = [TR(B[bk][:, j * 128:(j + 1) * 128], xts[j][:, f * 128:(f + 1) * 128], ident_f) for j in range(4)]
                    S.group("pe", fns, reads=list(xregs) + CONS, excl=[BR[bk]])
                    sc = wcols[:, wc0 + f:wc0 + f + 1]
                    if f % 2 == 0:
                        S.op("act", ACTV(dstT[:, f, tok0:tok0 + 512], B[bk], AF.Copy, scale=sc),
                             reads=CONS, excl=[BR[bk]], writes=[dreg], waw=False)
                    else:
                        S.op("dve", TS(dstT[:, f, tok0:tok0 + 512], B[bk], sc, ALU.mult),
                             reads=CONS, excl=[BR[bk]], writes=[dreg], waw=False)

        A.push()
        xts = [A.alloc([2048], F32) for _ in range(4)]
        xregs = [S.reg("xt%d" % j) for j in range(4)]
        junk = A.alloc([2048], BF16)
        ssq = A.alloc([16], F32)
        sd = A.alloc([16], F32)
        rstd = A.alloc([16], F32)
        NR = S.reg("nr")

        def load0(i, xt, xr):
            S.dma("sp", DMA(xt, xw[i * 128:(i + 1) * 128, :]), writes=[xr])
            return xt, xr

        norm_transpose(16, load0, ublk, 0, xts, xregs, ssq, sd, rstd, junk, NR)
        ws.extra_reads = list(xregs)
        ws.add_slot(*slots[1])
        ws.add_slot(*slots[2])
        ws.extra_reads = []
        A.pop()
        phase_barrier()

        A.push()
        yaT = A.alloc([8, 1024], BF16)
        ybT = A.alloc([8, 1024], BF16)
        R_ya, R_yb = S.reg("ya"), S.reg("yb")
        A.push()
        f_sq = A.alloc([512], F32)
        f_ga = A.alloc([512], F32)
        f_sn = A.alloc([512], F32)
        f_lg = A.alloc([512], F32)
        f_bb = A.alloc([512], F32)
        f_eb = A.alloc([512], F32)
        f_sg = A.alloc([512], F32)
        f_oT = A.alloc([512], F32)
        f_os = A.alloc([512], F32)
        b_vT = A.alloc([512], BF16)
        b_kT = A.alloc([512], BF16)
        b_qT = A.alloc([512], BF16)
        b_kh = A.alloc([512], BF16)
        b_tok = A.alloc([8, 128], BF16)
        b_As = A.alloc([512], BF16)
        b_os = A.alloc([512], BF16)
        Sring = [A.alloc([9, 128], F32) for _ in range(2)]
        Sb = [A.alloc([9, 128], BF16) for _ in range(2)]
        R = {n: S.reg(n) for n in ["sq", "ga", "sn", "lg", "bb", "eb", "sg", "oT", "os", "vT", "kT", "qT", "kh", "tok", "As", "S",
                                   "Sb0", "Sb1"]}
        RSb = [R["Sb0"], R["Sb1"]]
        BTb = B[3].bitcast(BF16)
        QS = 128.0 ** -0.5
        its = [(h, blk) for h in range(8) for blk in range(4)][:P1_NIT]
        hslot = {}
        f_sg2 = [f_sg, A.alloc([512], F32), A.alloc([512], F32)]
        R["sg0"], R["sg1"], R["sg2"] = S.reg("sg0"), S.reg("sg1"), S.reg("sg2")
        b_qT2 = [b_qT, A.alloc([512], BF16)]
        R["qT0"], R["qT1"] = S.reg("qT0"), S.reg("qT1")

        f_eb2 = [f_eb, A.alloc([512], F32)]
        R["eb0"], R["eb1"] = S.reg("eb0"), S.reg("eb1")

        def slot_of(n):
            h, blk = its[n]
            if h not in hslot:
                hslot[h] = ws.get()
            return hslot[h]

        def pe_proj(n, seg, bk):
            h, blk = its[n]
            sl, sreg = slot_of(n)
            uT, ureg, t0 = ublk(blk)
            fns = [MM(B[bk], sl[:, k, seg * 128:(seg + 1) * 128], uT[:, k, t0:t0 + 512], k == 0, k == 15) for k in range(16)]
            S.group("pe", fns, reads=[sreg, ureg], excl=[BR[bk]])

        def own_(n):
            return 0 <= n < len(its) and its[n][1] >= 2

        def s_hq(n):
            pe_proj(n, 0, 0)
            S.op("act", ACTV(f_sq, B[0], AF.Tanh, scale=0.5), excl=[BR[0]], writes=[R["sq"]])
            S.op("dve", STT(f_sq, f_sq, 1.0, B[0], ALU.add, ALU.mult), reads=[R["sq"]], excl=[BR[0]], writes=[R["sq"]])

        def s_hf(n, mid=None):
            h = its[n][0]
            if mid is None:
                pe_proj(n, 1, 1)
            else:
                hh_, blk_ = its[n]
                sl, sreg = slot_of(n)
                uT, ureg, t0 = ublk(blk_)
                mk = lambda k: MM(B[1], sl[:, k, 128:256], uT[:, k, t0:t0 + 512], k == 0, k == 15)
                S.group("pe", [mk(k) for k in range(8)], reads=[sreg, ureg], excl=[BR[1]])
                mid()
                S.group("pe", [mk(k) for k in range(8, 16)], reads=[sreg, ureg], excl=[BR[1]])
            S.op("act", ACTV(f_ga, B[1], AF.Tanh, scale=0.5), excl=[BR[1]], writes=[R["ga"]])
            S.op("act", ACTV(f_sn, B[1], AF.Tanh, scale=-0.5), excl=[BR[1]], writes=[R["sn"]])
            S.op("dve", TS(f_ga, f_ga, a_col[:, h:h + 1], ALU.mult, b_col[:, h:h + 1], ALU.add), reads=[R["ga"]] + CONS, writes=[R["ga"]])

        def s_ln(n):
            S.op("act", ACTV(f_lg, f_ga, AF.Ln), reads=[R["ga"]], writes=[R["lg"]])

        def s_scan(n):
            S.op("dve", SCAN(f_bb, rmask, f_lg), reads=[R["lg"]] + CONS, writes=[R["bb"]])

        def s_exps(n):
            h = its[n][0]
            eb, reb = f_eb2[n % 2], R["eb%d" % (n % 2)]
            S.op("act", ACTV(eb, f_bb, AF.Exp), reads=[R["bb"]], writes=[reb])
            S.op("act", ACTV(f_lg, f_bb, AF.Exp, scale=-1.0, bias=lna_col[:, h:h + 1]), reads=[R["bb"]] + CONS, writes=[R["lg"]])

        def s_kq(n):
            h = its[n][0]
            eb, reb = f_eb2[n % 2], R["eb%d" % (n % 2)]
            S.op("dve", STT(b_kT, f_sn, 1.0, f_lg, ALU.add, ALU.mult), reads=[R["sn"], R["lg"]] + CONS, writes=[R["kT"]])
            if own_(n):
                S.op("dve", STT(b_qT2[n % 2], f_sq, 0.5 * QS, eb, ALU.mult, ALU.mult), reads=[R["sq"], reb], writes=[R["qT%d" % (n % 2)]])
            eb3 = eb.rearrange("p (c t) -> p c t", t=64)
            S.op("dve", TT(b_kh.rearrange("p (c t) -> p c t", t=64), b_kT.rearrange("p (c t) -> p c t", t=64),
                           eb3[:, :, 63:64].broadcast_to([128, 8, 64]), ALU.mult), reads=[R["kT"], reb], writes=[R["kh"]])

        def s_hg(n):
            pe_proj(n, 3, 4)
            sg, rsg = f_sg2[n % 3], R["sg%d" % (n % 3)]
            S.op("act", ACTV(sg, B[4], AF.Tanh, scale=0.5), excl=[BR[4]], writes=[rsg])

        def s_hg_b(n):
            sg, rsg = f_sg2[n % 3], R["sg%d" % (n % 3)]
            S.op("dve", STT(sg, sg, 1.0, B[4], ALU.add, ALU.mult), reads=[rsg], excl=[BR[4]], writes=[rsg])

        def s_hi(n):
            pe_proj(n, 2, 2)
            S.op("act", ACTV(b_vT, B[2], AF.Copy), excl=[BR[2]], writes=[R["vT"]])

        def s_tr(n):
            fns = [TR(BTb[:, j * 128:(j + 1) * 128], b_kh[:, j * 128:(j + 1) * 128], ident_b) for j in range(4)]
            fns += [TR(BTb[:, (4 + j) * 128:(5 + j) * 128], b_vT[:, j * 128:(j + 1) * 128], ident_b) for j in range(4)]
            S.group("pe", fns, reads=[R["kh"], R["vT"]] + CONS, excl=[BR[3]])
            S.op("act", ACTV(b_tok, BTb.rearrange("p (a b) -> p a b", b=128), AF.Copy), excl=[BR[3]], writes=[R["tok"]])

        def s_ms_a(n):
            S.op("pe", MM(B[7], ones_b, b_os), reads=[R["os"]] + CONS, excl=[BR[7]])
            S.op("act", ACTV(f_os, B[7], AF.Ln, bias=eps_c[:, 0:1]), reads=CONS, excl=[BR[7]], writes=[R["os"]])
            S.op("act", ACTV(f_os, f_os, AF.Exp, scale=-0.5), reads=[R["os"]], writes=[R["os"]])

        def s_ms_b(n):
            h, blk = its[n]
            S.op("dve", TT(f_oT, f_oT, f_os, ALU.mult), reads=[R["oT"], R["os"]], writes=[R["oT"]])
            t0 = (blk - 2) * 512
            S.op("dve", STT(yaT[:, h, t0:t0 + 512], f_oT, hgw_h[:, 0:1], f_sg2[n % 3], ALU.mult, ALU.mult),
                 reads=[R["oT"], R["sg%d" % (n % 3)]] + CONS, writes=[R_ya], waw=False)

        def s_u(n):
            fns = []
            for c in range(8):
                j, hh = c // 2, c % 2
                rows = slice(hh * 64, hh * 64 + 64)
                fns.append(MM(B[5 + hh][:, j * 128:(j + 1) * 128], b_tok[rows, j, :], b_tok[rows, 4 + j, :]))
            S.group("pe", fns, reads=[R["tok"]], excl=[BR[5], BR[6]])
            if own_(n):
                q = b_qT2[n % 2]
                fns = [MM(B[4][:, j * 128:(j + 1) * 128], b_kT[:, j * 128:(j + 1) * 128], q[:, j * 128:(j + 1) * 128]) for j in range(4)]
                S.group("pe", fns, reads=[R["kT"], R["qT%d" % (n % 2)]], excl=[BR[4]])
                S.op("dve", TT(b_As, B[4], maskA, ALU.mult), reads=CONS, excl=[BR[4]], writes=[R["As"]])

        def s_chain(n, c0, c1):
            par = its[n][1] % 2
            eb, reb = f_eb2[n % 2], R["eb%d" % (n % 2)]
            for c in range(c0, c1):
                ub = B[5 + c % 2][:, (c // 2) * 128:(c // 2 + 1) * 128]
                sin = Sring[1 - par][:, 8, :] if c == 0 else Sring[par][:, c, :]
                S.op("dve", STT(Sring[par][:, c + 1, :], sin, eb[:, c * 64 + 63:c * 64 + 64], ub, ALU.mult, ALU.add),
                     reads=[R["S"], reb], writes=[R["S"]], excl=[BR[5 + c % 2]])

        def s_cast(n):
            h, blk = its[n]
            par = blk % 2
            if blk >= 2:
                S.op("act", ACTV(Sb[par][:, 1:9, :], Sring[par][:, 1:9, :], AF.Copy), reads=[R["S"]], writes=[RSb[par]], waw=False)
            elif blk == 1:
                S.op("act", ACTV(Sb[par][:, 8, :], Sring[par][:, 8, :], AF.Copy), reads=[R["S"]], writes=[RSb[par]], waw=False)

        def s_o(n):
            par = its[n][1] % 2
            q = b_qT2[n % 2]
            fns = []
            for j in range(4):
                oc = B[7][:, j * 128:(j + 1) * 128]
                fns.append(MM(oc, b_tok[:, 4 + j, :], b_As[:, j * 128:(j + 1) * 128], True, False))
                for c in (2 * j, 2 * j + 1):
                    sprev = Sb[1 - par][:, 8, :] if c == 0 else Sb[par][:, c, :]
                    fns.append(MM(B[7][:, c * 64:(c + 1) * 64], sprev, q[:, c * 64:(c + 1) * 64], False, c == 2 * j + 1))
            S.group("pe", fns, reads=[RSb[0], RSb[1], R["qT%d" % (n % 2)], R["tok"], R["As"]], excl=[BR[7]])
            S.op("act", ACTV(f_oT, B[7], AF.Copy), excl=[BR[7]], writes=[R["oT"]])
            S.op("act", ACTV(b_os, B[7], AF.Square), excl=[BR[7]], writes=[R["os"]])

        if RUN_P1:
            NI = len(its)

            def load_first_half(nx):
                if own_(nx):
                    s_hq(nx)
                s_hf(nx)

            load_first_half(0)
            s_ln(0)
            s_scan(0)
            if own_(0):
                s_hg(0)
            s_hi(0)
            s_exps(0)
            s_kq(0)
            if own_(0):
                s_hg_b(0)
            for n in range(NI):
                h, blk = its[n]
                nx = n + 1 if n + 1 < NI else None
                if blk == 0:
                    S.op("dve", MEMSET(Sring[1][:, 8, :], 0.0), writes=[R["S"]])
                if nx is not None:
                    if own_(nx):
                        s_hq(nx)
                    s_hf(nx, mid=lambda n=n: s_tr(n))
                else:
                    s_tr(n)
                if own_(n - 1):
                    s_ms_a(n - 1)
                if nx is not None:
                    s_ln(nx)
                s_u(n)
                s_chain(n, 0, 8)
                if own_(n - 1):
                    s_ms_b(n - 1)
                if nx is not None:
                    s_scan(nx)
                    if own_(nx):
                        s_hg(nx)
                s_cast(n)
                if nx is not None:
                    if own_(nx):
                        s_hg_b(nx)
                    s_hi(nx)
                    s_exps(nx)
                    s_kq(nx)
                if own_(n):
                    s_o(n)
                if blk == 0 and h >= 1:
                    ws.release(hslot[h - 1])
            if own_(NI - 1):
                s_ms_a(NI - 1)
                s_ms_b(NI - 1)
            ws.release(hslot[its[NI - 1][0]])
        A.pop()
        phase_barrier()
        if DEBUG == 1:
            dbg = nc.dram_tensor("dbg", [128, 8, 1024], BF16, kind="ExternalOutput").ap()
            S.dma("sp", DMA(dbg, yaT), reads=[R_ya, PH])

        A.push()
        rbt = A.alloc([16, 384], F32)
        aqT = A.alloc([1024], BF16)
        akT = A.alloc([1536], BF16)
        Vext = A.alloc([12, 2, 65], BF16)
        tmpS = [A.alloc([384], F32) for _ in range(3)]
        PT = [A.alloc([640], BF16) for _ in range(3)]
        ybt = A.alloc([8, 128], BF16)
        rcp = A.alloc([16], F32)
        avT = A.alloc([1536], BF16)
        R2 = {n: S.reg(n) for n in ["rbt", "aq", "ak", "V", "tmp0", "tmp1", "tmp2", "PT0", "PT1", "PT2", "ybt", "rcp", "avT"]}
        S.dma("sp", DMA(rbt, rbt_d), reads=[PH], writes=[R2["rbt"]])
        S.op("dve", MEMSET(Vext, 1.0), reads=[PH], writes=[R2["V"]])
        for i in range(3):
            S.op("dve", MEMSET(PT[i], 0.0), reads=[PH], writes=[R2["PT%d" % i]])
        BT6 = B[6].bitcast(BF16)
        SC = 64.0 ** -0.5
        unit_i = 0
        for j in range(8 if RUN_P2 else 0):
            sl, sreg = ws.get()
            pb = 0
            for name, dst, nblk, b0, seg in (("aq", aqT, 2, 2, 0), ("ak", akT, 3, 1, 1)):
                for bi_ in range(nblk):
                    uT, ureg, t0 = ublk(b0 + bi_)
                    bk = 6 + (pb % 2)
                    pb += 1
                    fns = [MM(B[bk], sl[:, k, seg * 128:(seg + 1) * 128], uT[:, k, t0:t0 + 512], k == 0, k == 15) for k in range(16)]
                    S.group("pe", fns, reads=[sreg, ureg], excl=[BR[bk]])
                    if pb % 2:
                        S.op("act", ACTV(dst[:, bi_ * 512:(bi_ + 1) * 512], B[bk], AF.Copy), excl=[BR[bk]], writes=[R2[name]], waw=False)
                    else:
                        S.op("dve", COPY(dst[:, bi_ * 512:(bi_ + 1) * 512], B[bk]), excl=[BR[bk]], writes=[R2[name]], waw=False)
            for bi_ in range(3):
                uT, ureg, t0 = ublk(1 + bi_)
                bk = 6 + (pb % 2)
                pb += 1
                fns = [MM(B[bk], sl[:, k, 256:384], uT[:, k, t0:t0 + 512], k == 0, k == 15) for k in range(16)]
                S.group("pe", fns, reads=[sreg, ureg], excl=[BR[bk]])
                if pb % 2:
                    S.op("act", ACTV(avT[:, bi_ * 512:(bi_ + 1) * 512], B[bk], AF.Copy), excl=[BR[bk]], writes=[R2["avT"]], waw=False)
                else:
                    S.op("dve", COPY(avT[:, bi_ * 512:(bi_ + 1) * 512], B[bk]), excl=[BR[bk]], writes=[R2["avT"]], waw=False)
            for (w0, nw) in ((0, 8), (8, 4)):
                bk = 6 + (pb % 2)
                pb += 1
                bt = B[bk].bitcast(BF16)
                fns = [TR(bt[:, i * 128:(i + 1) * 128], avT[:, (w0 + i) * 128:(w0 + i + 1) * 128], ident_b) for i in range(nw)]
                S.group("pe", fns, reads=[R2["avT"]] + CONS, excl=[BR[bk]])
                S.op("act", ACTV(Vext[:, w0:w0 + nw, :, 0:64], bt[:, 0:nw * 128].rearrange("p (a b c) -> p a b c", b=2, c=64), AF.Copy),
                     excl=[BR[bk]], writes=[R2["V"]], waw=False)
            ws.release((sl, sreg))
            def stage_a(qt, hh, u):
                head = 2 * j + hh
                rows = slice(hh * 64, hh * 64 + 64)
                bsA, bsB = 2 * u, 2 * u + 1
                fns = []
                for o in range(5):
                    kt = qt + 4 - o
                    ob_ = B[bsA][:, o * 128:(o + 1) * 128] if o < 3 else B[bsB][:, (o - 3) * 128:(o - 2) * 128]
                    fns.append(MM(ob_, akT[rows, kt * 128:(kt + 1) * 128], aqT[rows, qt * 128:(qt + 1) * 128]))
                S.group("pe", fns, reads=[R2["aq"], R2["ak"]], excl=[BR[bsA], BR[bsB]])
                tr, pr = R2["tmp%d" % u], R2["PT%d" % u]
                S.op("dve", STT(tmpS[u], B[bsA][:, 0:384], SC, rbt[:, head, :], ALU.mult, ALU.add),
                     reads=[R2["rbt"]], excl=[BR[bsA]], writes=[tr])
                o = 0
                while o < 3:
                    hist = (qt + 4 - o) < 4
                    o2 = o
                    while o2 + 1 < 3 and ((qt + 4 - (o2 + 1)) < 4) == hist:
                        o2 += 1
                    c0, c1 = o * 128, (o2 + 1) * 128
                    if hist:
                        S.op("act", ACTV(PT[u][:, c0:c1], tmpS[u][:, c0:c1], AF.Exp, bias=hb[:, 0:1]),
                             reads=[tr] + CONS, writes=[pr], waw=False)
                    else:
                        S.op("act", ACTV(PT[u][:, c0:c1], tmpS[u][:, c0:c1], AF.Exp), reads=[tr], writes=[pr], waw=False)
                    o = o2 + 1
                h3, h4 = (qt + 1) < 4, qt < 4
                if h3 == h4:
                    bias = (cbh if h3 else crep)[:, head:head + 1]
                    S.op("act", ACTV(PT[u][:, 384:640], B[bsB][:, 0:256], AF.Exp, bias=bias, scale=SC),
                         reads=CONS, excl=[BR[bsB]], writes=[pr], waw=False)
                else:
                    for o in (3, 4):
                        hist = (qt + 4 - o) < 4
                        bias = (cbh if hist else crep)[:, head:head + 1]
                        S.op("act", ACTV(PT[u][:, o * 128:(o + 1) * 128], B[bsB][:, (o - 3) * 128:(o - 2) * 128], AF.Exp, bias=bias, scale=SC),
                             reads=CONS, excl=[BR[bsB]], writes=[pr], waw=False)
                S.op("pool", MEMSET(PT[u][0:64, 4 * 128 + 64:5 * 128], 0.0), writes=[pr])

            def stage_b(qt, hh, u, ob):
                pr = R2["PT%d" % u]
                fns = []
                for o in range(5):
                    kt = qt + 4 - o
                    fns.append(MM(B[ob][:, 0:65], PT[u][:, o * 128:(o + 1) * 128], Vext[:, kt, hh, :], o == 0, o == 4))
                S.group("pe", fns, reads=[pr, R2["V"]], excl=[BR[ob]])
                rc = rcp[:, (ob - 6):(ob - 5)]
                S.op("dve", RECIP(rc, B[ob][:, 64:65]), excl=[BR[ob]], writes=[R2["rcp"]])
                S.op("dve", TS(ybt[:, qt, hh * 64:(hh + 1) * 64], B[ob][:, 0:64], rc, ALU.mult),
                     reads=[R2["rcp"]], excl=[BR[ob]], writes=[R2["ybt"]], waw=False)

            ulist = [(qt, hh) for qt in range(8) for hh in range(2)]
            NU = len(ulist)
            for n_ in range(min(2, NU)):
                stage_a(ulist[n_][0], ulist[n_][1], n_ % 3)
            for n_, (qt, hh) in enumerate(ulist):
                if n_ + 2 < NU:
                    stage_a(ulist[n_ + 2][0], ulist[n_ + 2][1], (n_ + 2) % 3)
                stage_b(qt, hh, n_ % 3, 6 + n_ % 2)
            fns = [TR(BT6[:, qt * 128:(qt + 1) * 128], ybt[:, qt, :], ident_b) for qt in range(8)]
            S.group("pe", fns, reads=[R2["ybt"]] + CONS, excl=[BR[6]])
            S.op("act", ACTV(ybT[:, j, :], BT6, AF.Copy), excl=[BR[6]], writes=[R_yb], waw=False)
        A.pop()
        phase_barrier()
        if DEBUG == 2:
            dbg = nc.dram_tensor("dbg", [128, 8, 1024], BF16, kind="ExternalOutput").ap()
            S.dma("sp", DMA(dbg, ybT), reads=[R_yb, PH])

        mergedT = A.at(H_off, [16, 1024], BF16)
        R_mg = S.reg("merged")
        A.push()
        g_a = [A.alloc([512], F32) for _ in range(2)]
        g_b = [A.alloc([512], F32) for _ in range(2)]
        R3 = {n: S.reg(n) for n in ["ga0", "ga1", "gb0", "gb1"]}
        it3 = 0
        for mg in range(4 if RUN_P3 else 0):
            sx = ws.get()
            sy = ws.get()
            sz = ws.get()
            for mi in range(4):
                m = mg * 4 + mi
                cs = slice(mi * 128, (mi + 1) * 128)
                for t2 in range(2):
                    u = it3 % 2
                    it3 += 1
                    bb = 4 * u
                    ts_ = slice(t2 * 512, (t2 + 1) * 512)
                    S.group("pe", [MM(B[bb + 0], sx[0][:, k, cs], uT_own[:, k, ts_], k == 0, k == 15) for k in range(16)],
                            reads=[sx[1], R_uo], excl=[BR[bb + 0]])
                    S.group("pe", [MM(B[bb + 1], sy[0][:, k, cs], uT_own[:, k, ts_], k == 0, k == 15) for k in range(16)],
                            reads=[sy[1], R_uo], excl=[BR[bb + 1]])
                    S.group("pe", [MM(B[bb + 2], sz[0][:, k, cs], yaT[:, k, ts_], k == 0, k == 7) for k in range(8)],
                            reads=[sz[1], R_ya], excl=[BR[bb + 2]])
                    S.group("pe", [MM(B[bb + 3], sz[0][:, 8 + k, cs], ybT[:, k, ts_], k == 0, k == 7) for k in range(8)],
                            reads=[sz[1], R_yb], excl=[BR[bb + 3]])
                    ra, rb_ = R3["ga%d" % u], R3["gb%d" % u]
                    S.op("act", ACTV(g_a[u], B[bb + 0], AF.Sigmoid), excl=[BR[bb + 0]], writes=[ra])
                    S.op("act", ACTV(g_b[u], B[bb + 1], AF.Sigmoid), excl=[BR[bb + 1]], writes=[rb_])
                    S.op("dve", TT(g_a[u], g_a[u], B[bb + 2], ALU.mult), reads=[ra], excl=[BR[bb + 2]], writes=[ra])
                    S.op("dve", TT(g_b[u], g_b[u], B[bb + 3], ALU.mult), reads=[rb_], excl=[BR[bb + 3]], writes=[rb_])
                    S.op("dve", TT(mergedT[:, m, ts_], g_a[u], g_b[u], ALU.add), reads=[ra, rb_, PH], writes=[R_mg], waw=False)
            ws.release(sx)
            ws.release(sy)
            ws.release(sz)
        A.pop()
        A.pop()
        phase_barrier()
        if DEBUG == 3:
            dbg = nc.dram_tensor("dbg", [128, 16, 1024], BF16, kind="ExternalOutput").ap()
            S.dma("sp", DMA(dbg, mergedT), reads=[R_mg, PH])

        hres = A.alloc([8, 2048], F32)
        R_h = [S.reg("h%d" % t) for t in range(8)]
        u2T = uT_own
        R_u2 = S.reg("u2T")
        A.push()
        hnb = [A.alloc([2048], BF16) for _ in range(2)]
        R_hn = [S.reg("hnb%d" % i) for i in range(2)]
        wrep_mlp = A.alloc([2048], F32)
        ssq5 = A.alloc([8], F32)
        sd5 = A.alloc([8], F32)
        rstd5 = A.alloc([8], F32)
        NR5 = S.reg("nr5")
        R_wm = S.reg("wrep_mlp")
        S.dma("sp", DMA(wrep_mlp, wmlp_d), reads=[PH], writes=[R_wm])
        S.op("dve", MEMSET(ssq5, 0.0), reads=[PH], writes=[NR5] + R_hn)
        for tt in range(8):
            S.dma("sp", DMA(hres[:, tt, :], xw[1024 + tt * 128:1024 + (tt + 1) * 128, :]), reads=[PH], writes=[R_h[tt]])
        BT6b, BT7b = B[6].bitcast(BF16), B[7].bitcast(BF16)

        def emit_norm_stats(tt):
            u = tt % 2
            hn, rhn = hnb[u], R_hn[u]
            S.op("act", ACTV(hn, hres[:, tt, :], AF.Square, accum=ssq5[:, tt:tt + 1]), reads=[R_h[tt]], writes=[NR5, rhn])
            S.op("act", ACTV(sd5[:, tt:tt + 1], ssq5[:, tt:tt + 1], AF.Sqrt, bias=eps_c[:, 0:1], scale=1.0 / 2048.0),
                 reads=[NR5] + CONS, writes=[NR5])
            S.op("dve", RECIP(rstd5[:, tt:tt + 1], sd5[:, tt:tt + 1]), reads=[NR5], writes=[NR5])
            S.op("dve", STT(hn, hres[:, tt, :], rstd5[:, tt:tt + 1], wrep_mlp, ALU.mult, ALU.mult),
                 reads=[R_h[tt], NR5, R_wm], writes=[rhn])

        def emit_norm_trans(tt):
            u = tt % 2
            hn, rhn = hnb[u], R_hn[u]
            for half, bt, bk in ((0, BT6b, 6), (1, BT7b, 7)):
                fns = [TR(bt[:, i * 128:(i + 1) * 128], hn[:, (half * 8 + i) * 128:(half * 8 + i + 1) * 128], ident_b) for i in range(8)]
                S.group("pe", fns, reads=[rhn] + CONS, excl=[BR[bk]])
                dst = u2T[:, half * 8:(half + 1) * 8, tt * 128:(tt + 1) * 128]
                srcv = bt.rearrange("p (a b) -> p a b", b=128)
                if half == 0:
                    S.op("act", ACTV(dst, srcv, AF.Copy), excl=[BR[bk]], writes=[R_u2], waw=False)
                else:
                    S.op("dve", COPY(dst, srcv), excl=[BR[bk]], writes=[R_u2], waw=False)

        it4 = 0
        for cg in range(4 if RUN_P4 else 0):
            sl, sreg = ws.get()
            for tt in range(8):
                bk = it4 % 6
                it4 += 1
                S.group("pe", [MM(B[bk], mergedT[:, k, tt * 128:(tt + 1) * 128], sl[:, k, :], k == 0, k == 15) for k in range(16)],
                        reads=[sreg, R_mg], excl=[BR[bk]])
                hv = hres[:, tt, cg * 512:(cg + 1) * 512]
                S.op("dve", TT(hv, hv, B[bk], ALU.add), reads=[R_h[tt]], excl=[BR[bk]], writes=[R_h[tt]])
                if cg == 3 and tt >= 1:
                    emit_norm_stats(tt - 1)
                if cg == 3 and tt >= 2:
                    emit_norm_trans(tt - 2)
            ws.release((sl, sreg))
        if RUN_P4:
            emit_norm_stats(7)
            emit_norm_trans(6)
            emit_norm_trans(7)
        else:
            for tt in range(8):
                emit_norm_stats(tt)
                emit_norm_trans(tt)
        A.pop()
        phase_barrier([s[1] for s in slots])

        A.push()
        slot4 = (A.at(H_off, [16, 512], BF16), S.reg("slot3"))
        hid = [A.at(H_off + 16384 + i * 8192, [4, 1024], BF16) for i in range(2)]
        rr = [A.alloc([512], F32) for _ in range(2)]
        R6 = {n: S.reg(n) for n in ["hid0", "hid1", "rr0", "rr1"]}
        S.op("dve", MEMSET(dummy, 0.0), reads=[PH, slot4[1]], writes=[R6["hid0"], R6["hid1"]])
        ws.add_slot(*slot4)
        upc = [0]

        def emit_up_unit(fb, un, sa):
            fi, half = un // 2, un % 2
            par = fb % 2
            bk = upc[0] % 4
            u = upc[0] % 2
            upc[0] += 1
            S.group("pe", [MM(B[bk], sa[0][:, k, fi * 128:(fi + 1) * 128], u2T[:, k, half * 512:(half + 1) * 512], k == 0, k == 15)
                           for k in range(16)], reads=[sa[1], R_u2], excl=[BR[bk]])
            S.op("act", ACTV(rr[u], B[bk], AF.Relu), excl=[BR[bk]], writes=[R6["rr%d" % u]])
            S.op("dve", TT(hid[par][:, fi, half * 512:(half + 1) * 512], rr[u], rr[u], ALU.mult),
                 reads=[R6["rr%d" % u]], writes=[R6["hid%d" % par]], waw=False)

        def emit_down_tt(fb, tt, sb):
            par = fb % 2
            fns = []
            for k in range(4):
                for cg in range(4):
                    fns.append(MM(B[4 + cg], hid[par][:, k, tt * 128:(tt + 1) * 128], sb[0][:, k * 4 + cg, :], k == 0, k == 3))
            S.group("pe", fns, reads=[sb[1], R6["hid%d" % par]], excl=[BR[4], BR[5], BR[6], BR[7]])
            for cg in range(4):
                hv = hres[:, tt, cg * 512:(cg + 1) * 512]
                S.op("dve", TT(hv, hv, B[4 + cg], ALU.add), reads=[R_h[tt]], excl=[BR[4 + cg]], writes=[R_h[tt]])

        wrep = A.alloc([2048], F32)
        junk = A.alloc([2048], BF16)
        ssq = A.alloc([8], F32)
        sd = A.alloc([8], F32)
        rstd = A.alloc([8], F32)
        R7 = {n: S.reg(n) for n in ["wrep", "nr"]}
        S.dma("sp", DMA(wrep, wfin_d), reads=[PH], writes=[R7["wrep"]])
        S.op("dve", MEMSET(ssq, 0.0), reads=[PH], writes=[R7["nr"]])

        def emit_final(tt):
            S.op("act", ACTV(junk, hres[:, tt, :], AF.Square, accum=ssq[:, tt:tt + 1]), reads=[R_h[tt], PH], writes=[R7["nr"]])
            S.op("act", ACTV(sd[:, tt:tt + 1], ssq[:, tt:tt + 1], AF.Sqrt, bias=eps_c[:, 0:1], scale=1.0 / 2048.0),
                 reads=[R7["nr"]] + CONS, writes=[R7["nr"]])
            S.op("dve", RECIP(rstd[:, tt:tt + 1], sd[:, tt:tt + 1]), reads=[R7["nr"]], writes=[R7["nr"]])
            S.op("dve", STT(hres[:, tt, :], hres[:, tt, :], rstd[:, tt:tt + 1], wrep, ALU.mult, ALU.mult),
                 reads=[R_h[tt], R7["nr"], R7["wrep"]], writes=[R_h[tt]])
            S.dma("sp", DMA(out_d[tt * 128:(tt + 1) * 128, :], hres[:, tt, :]), reads=[R_h[tt]])

        if RUN_P6:
            sa = ws.get()
            for un in range(8):
                emit_up_unit(0, un, sa)
            ws.release(sa)
            for fb in range(16):
                sb = ws.get()
                sa = ws.get() if fb + 1 < 16 else None
                for tt in range(8):
                    emit_down_tt(fb, tt, sb)
                    if sa is not None:
                        emit_up_unit(fb + 1, tt, sa)
                    else:
                        emit_final(tt)
                ws.release(sb)
                if sa is not None:
                    ws.release(sa)
        else:
            for tt in range(8):
                emit_final(tt)
        A.pop()
        S.finish()
        S.replay()
        build_nc.peak = A.peak
        build_nc.log = A.log
    return nc


def _host_prep(inputs):
    x = np.asarray(inputs["x"], dtype=np.float32)
    rb = np.asarray(inputs["rel_bias"], dtype=np.float32)[0]
    lbl = np.ascontiguousarray(np.asarray(inputs["lb_logits"], np.float32).reshape(2, 8, 128).transpose(2, 0, 1).reshape(128, 16))
    hgw = np.ascontiguousarray(np.asarray(inputs["hg_norm_w"], np.float32)[0].reshape(128, 1))
    wc = np.concatenate([np.asarray(inputs["norm_mix_w"], np.float32)[0].reshape(16, 128).T,
                         np.asarray(inputs["norm_mlp_w"], np.float32)[0].reshape(16, 128).T], axis=1)
    wc = np.ascontiguousarray(wc)
    wfin = np.ascontiguousarray(np.broadcast_to(np.asarray(inputs["norm_final_w"], np.float32).reshape(1, 2048), (128, 2048)))
    wmlp = np.ascontiguousarray(np.broadcast_to(np.asarray(inputs["norm_mlp_w"], np.float32)[0].reshape(1, 2048), (128, 2048)))
    k = np.arange(128)[:, None, None]
    o = np.arange(3)[None, :, None]
    t = np.arange(128)[None, None, :]
    idx = np.clip(128 * o + t - k, -256, 256) + 256
    rbt = rb[:, idx]
    invalid = np.broadcast_to((o == 0) & (k >= 64) & (t < 64), idx.shape)
    rbt = np.where(invalid[None], np.float32(NEG), rbt)
    rbt = np.ascontiguousarray(rbt.transpose(1, 0, 2, 3).reshape(128, 16, 384)).astype(np.float32)
    crep = np.ascontiguousarray(np.broadcast_to(rb[:, 512][None, :], (128, 16))).astype(np.float32)
    shared = {
        "w_in": np.ascontiguousarray(np.asarray(inputs["w_in"], np.float32)[0]),
        "w_a": np.ascontiguousarray(np.asarray(inputs["w_branch_a"], np.float32)[0]),
        "w_b": np.ascontiguousarray(np.asarray(inputs["w_branch_b"], np.float32)[0]),
        "w_out": np.ascontiguousarray(np.asarray(inputs["w_out"], np.float32)[0]),
        "w_up": np.ascontiguousarray(np.asarray(inputs["w_up"], np.float32)[0]),
        "w_down": np.ascontiguousarray(np.asarray(inputs["w_down"], np.float32)[0]),
        "lbl": lbl, "hgw": hgw, "wcols": wc, "wfin": wfin, "wmlp": wmlp, "rbt": rbt, "crep": crep,
    }
    in_maps = []
    for c in range(8):
        b, half = c // 2, c % 2
        xwin = np.zeros((2048, 2048), np.float32)
        if half == 1:
            xwin[:] = x[b]
        else:
            xwin[1024:] = x[b, 0:1024]
        d = dict(shared)
        d["xw"] = xwin
        d["hbias"] = np.full((128, 1), 0.0 if half == 1 else NEG, np.float32)
        in_maps.append(d)
    return in_maps


_NC_CACHE = {}


def kernel(**inputs):
    in_maps = _host_prep(inputs)
    if "nc" not in _NC_CACHE:
        _NC_CACHE["nc"] = build_nc()
    nc = _NC_CACHE["nc"]
    res = run_bass_kernel_spmd(nc, in_maps, core_ids=list(range(8)))
    out = np.zeros((4, 2048, 2048), np.float32)
    for c in range(8):
        b, half = c // 2, c % 2
        out[b, half * 1024:(half + 1) * 1024] = res.results[c]["out"]
    return out
```

```python
from contextlib import ExitStack

import numpy as np
import concourse.bass as bass
import concourse.mybir as mybir
from concourse.bass_utils import run_bass_kernel_spmd

F32 = mybir.dt.float32
BF16 = mybir.dt.bfloat16
U8 = mybir.dt.uint8
AF = mybir.ActivationFunctionType
ALU = mybir.AluOpType

NEG = -30000.0
EPS = 1e-6


class Reg:
    __slots__ = ("name", "w", "r")

    def __init__(self, name):
        self.name = name
        self.w = {}
        self.r = {}


class Sched:
    LIMIT = 30000
    NDQ = 6

    def __init__(self, nc, es):
        self.nc = nc
        self.es = es
        self.engs = ("pe", "act", "dve", "pool", "sp")
        self.streams = {e: [] for e in self.engs}
        self.seen = {e: {} for e in self.engs}
        self.cur = {}
        self.nsem = 0
        for e in ("pe", "act", "dve", "pool"):
            self._newsem(e)
        self.dq = {q: [[self._alloc(), 0] for _ in range(self.NDQ)] for q in ("sp", "pool")}
        self.dq_rr = {"sp": 0, "pool": 0}

    def _alloc(self):
        sem = self.es.enter_context(self.nc.semaphore("s%d" % self.nsem))
        self.nsem += 1
        return (self.nsem, sem)

    def _newsem(self, e):
        self.cur[e] = [self._alloc(), 0]

    def reg(self, name="r"):
        return Reg(name)

    def _waits(self, eng, reads, writes, excl, waw):
        need = {}

        def add(tok, same_ok):
            key, sem, val, teng = tok
            if teng == eng and not same_ok:
                return
            if self.seen[eng].get(key, 0) >= val:
                return
            if key in need and need[key][1] >= val:
                return
            need[key] = (sem, val)

        for r in reads:
            for t in r.w.values():
                add(t, True)
        for w in writes:
            if waw:
                for t in w.w.values():
                    add(t, True)
            for t in w.r.values():
                add(t, True)
        for x in excl:
            for t in x.w.values():
                add(t, True)
            for t in x.r.values():
                add(t, True)
        for key, (sem, val) in need.items():
            self.seen[eng][key] = val
            self.streams[eng].append(lambda e, sem=sem, val=val: e.wait_ge(sem, val))

    def _mark(self, tok, reads, writes, excl, waw):
        key = tok[0]
        for r in reads:
            r.r[key] = tok
        for w in writes:
            if waw or w.r:
                w.w = {key: tok}
            else:
                w.w[key] = tok
            w.r = {}
        for x in excl:
            x.w = {key: tok}
            x.r = {}

    def _tick(self, eng):
        cur = self.cur[eng]
        if cur[1] >= self.LIMIT:
            self._newsem(eng)
            cur = self.cur[eng]
        (key, sem) = cur[0]
        cur[1] += 1
        return key, sem, cur[1]

    def op(self, eng, fn, reads=(), writes=(), excl=(), waw=True):
        self.group(eng, [fn], reads, writes, excl, waw)

    def group(self, eng, fns, reads=(), writes=(), excl=(), waw=True):
        self._waits(eng, reads, writes, excl, waw)
        key, sem, val = self._tick(eng)
        for fn in fns[:-1]:
            self.streams[eng].append(lambda e, fn=fn: fn(e))
        fn = fns[-1]
        self.streams[eng].append(lambda e, fn=fn, sem=sem: fn(e).then_inc(sem, 1))
        self._mark((key, sem, val, eng), reads, writes, excl, waw)

    def dma(self, q, fn, reads=(), writes=(), waw=True):
        i = self.dq_rr[q]
        self.dq_rr[q] = (i + 1) % self.NDQ
        slot = self.dq[q][i]
        (key, sem) = slot[0]
        if slot[1] > 0 and self.seen[q].get(key, 0) < 16 * slot[1]:
            v = 16 * slot[1]
            self.seen[q][key] = v
            self.streams[q].append(lambda e, sem=sem, v=v: e.wait_ge(sem, v))
        self._waits(q, reads, writes, (), waw)
        slot[1] += 1
        val = 16 * slot[1]
        self.streams[q].append(lambda e, fn=fn, sem=sem: fn(e).then_inc(sem, 16))
        self._mark((key, sem, val, "dma_" + q), reads, writes, (), waw)

    def barrier(self, engs=("pe", "act", "dve")):
        for e in engs:
            for o in engs:
                if o == e:
                    continue
                (key, sem), cnt = self.cur[o]
                if cnt > 0 and self.seen[e].get(key, 0) < cnt:
                    self.seen[e][key] = cnt
                    self.streams[e].append(lambda en, sem=sem, cnt=cnt: en.wait_ge(sem, cnt))

    def finish(self):
        for q in ("sp", "pool"):
            for (key, sem), cnt in self.dq[q]:
                if cnt > 0:
                    v = 16 * cnt
                    self.streams["sp"].append(lambda e, sem=sem, v=v: e.wait_ge(sem, v))

    def replay(self):
        with self.nc.Block() as block:
            @block.sync
            def _(e):
                for f in self.streams["sp"]:
                    f(e)

            @block.tensor
            def _(e):
                for f in self.streams["pe"]:
                    f(e)

            @block.scalar
            def _(e):
                for f in self.streams["act"]:
                    f(e)

            @block.vector
            def _(e):
                for f in self.streams["dve"]:
                    f(e)

            @block.gpsimd
            def _(e):
                for f in self.streams["pool"]:
                    f(e)


class Arena:
    def __init__(self, nc, es, nbytes, name="arena"):
        self.t = es.enter_context(nc.sbuf_tensor(name, [128, nbytes], U8))
        self.nbytes = nbytes
        self.off = 0
        self.marks = []
        self.peak = 0
        self.log = []

    def alloc(self, shape, dtype):
        esz = 2 if dtype == BF16 else 4
        n = int(np.prod(shape))
        nb = n * esz
        off = self.reserve(nb)
        return self.at(off, shape, dtype)

    def reserve(self, nb):
        off = (self.off + 63) // 64 * 64
        assert off + nb <= self.nbytes, ("arena overflow", off, nb, self.nbytes)
        self.off = off + nb
        self.peak = max(self.peak, self.off)
        return off

    def at(self, off, shape, dtype):
        self.log.append((off, list(shape), "bf16" if dtype == BF16 else "f32"))
        esz = 2 if dtype == BF16 else 4
        nb = int(np.prod(shape)) * esz
        ap = self.t[:, off:off + nb].bitcast(dtype)
        if len(shape) == 2:
            ap = ap.rearrange("p (a b) -> p a b", b=shape[1])
        elif len(shape) == 3:
            ap = ap.rearrange("p (a b c) -> p a b c", b=shape[1], c=shape[2])
        return ap

    def push(self):
        self.marks.append(self.off)

    def pop(self):
        self.off = self.marks.pop()


class WStream:
    def __init__(self, S, units):
        self.S = S
        self.units = units
        self.free = []
        self.nload = 0
        self.nuse = 0
        self.loaded = {}
        self.extra_reads = []
        self.limits = {}

    def add_slot(self, ap, reg, max_unit=None):
        if max_unit is not None:
            self.limits[id(reg)] = max_unit
        self.free.append((ap, reg))
        self.pump()

    def pump(self):
        while self.free and self.nload < len(self.units):
            pick = None
            for i, (ap_, reg_) in enumerate(self.free):
                if self.limits.get(id(reg_), 1 << 30) >= self.nload:
                    pick = i
                    break
            if pick is None:
                break
            ap, reg = self.free.pop(pick)
            for mk in self.units[self.nload]:
                o, i = mk(ap)
                self.S.dma("pool", lambda e, o=o, i=i: e.dma_start(out=o, in_=i), reads=self.extra_reads, writes=[reg], waw=False)
            self.loaded[self.nload] = (ap, reg)
            self.nload += 1

    def get(self):
        assert self.nuse in self.loaded, "weight unit not loaded (no free slot)"
        r = self.loaded.pop(self.nuse)
        self.nuse += 1
        return r

    def release(self, slot):
        self.free.append(slot)
        self.pump()


def MM(out, lhsT, rhs, start=True, stop=True):
    return lambda e: e.matmul(out, lhsT=lhsT, rhs=rhs, start=start, stop=stop)


def TR(out, in_, ident):
    return lambda e: e.transpose(out=out, in_=in_, identity=ident)


def ACTV(out, in_, func, bias=None, scale=None, accum=None):
    kw = {}
    if bias is not None:
        kw["bias"] = bias
    if scale is not None:
        kw["scale"] = scale
    if accum is not None:
        kw["accum_out"] = accum
    return lambda e: e.activation(out=out, in_=in_, func=func, **kw)


def TS(out, in0, s1, op0, s2=None, op1=None):
    if op1 is None:
        return lambda e: e.tensor_scalar(out=out, in0=in0, scalar1=s1, scalar2=None, op0=op0)
    return lambda e: e.tensor_scalar(out=out, in0=in0, scalar1=s1, scalar2=s2, op0=op0, op1=op1)


def TT(out, in0, in1, op):
    return lambda e: e.tensor_tensor(out=out, in0=in0, in1=in1, op=op)


def STT(out, in0, scalar, in1, op0, op1):
    return lambda e: e.scalar_tensor_tensor(out=out, in0=in0, scalar=scalar, in1=in1, op0=op0, op1=op1)


def SCAN(out, d0, d1):
    return lambda e: e.tensor_tensor_scan(out=out, data0=d0, data1=d1, initial=0.0, op0=ALU.mult, op1=ALU.add)


def RECIP(out, in_):
    return lambda e: e.reciprocal(out=out, in_=in_)


def MEMSET(ap, v):
    return lambda e: e.memset(ap, v)


def COPY(out, in_):
    return lambda e: e.tensor_copy(out=out, in_=in_)


def DMA(out, in_):
    return lambda e: e.dma_start(out=out, in_=in_)


def ASEL(out, in_, pattern, op, fill, base, cm):
    return lambda e: e.affine_select(out=out, in_=in_, pattern=pattern, compare_op=op, fill=fill, base=base, channel_multiplier=cm)


ARENA_BYTES = 204 * 1024
DEBUG = 0
RUN_P1 = RUN_P2 = RUN_P3 = RUN_P4 = RUN_P6 = True
P1_NIT = 32
P1_STAGE = 9

def build_nc():
    nc = bass.Bass("TRN2", target_bir_lowering=False)

    def din(name, shape):
        return nc.dram_tensor(name, shape, F32, kind="ExternalInput").ap()

    xw = din("xw", [2048, 2048])
    w_in = din("w_in", [2048, 11264])
    w_a = din("w_a", [1024, 2048])
    w_b = din("w_b", [1024, 2048])
    w_out = din("w_out", [2048, 2048])
    w_up = din("w_up", [2048, 8192])
    w_down = din("w_down", [8192, 2048])
    lbl_d = din("lbl", [128, 16])
    hgw_d = din("hgw", [128, 1])
    wcols_d = din("wcols", [128, 32])
    wfin_d = din("wfin", [128, 2048])
    wmlp_d = din("wmlp", [128, 2048])
    rbt_d = din("rbt", [128, 16, 384])
    crep_d = din("crep", [128, 16])
    hb_d = din("hbias", [128, 1])
    out_d = nc.dram_tensor("out", [1024, 2048], F32, kind="ExternalOutput").ap()

    def wcols_unit(w, nk, c0, ncols, k0, d0):
        def mk(slot):
            return (slot[:, k0:k0 + nk, d0:d0 + ncols],
                    w[0:nk * 128, c0:c0 + ncols].rearrange("(k p) c -> p k c", p=128))
        return mk

    def wdown_unit(fb):
        def mk(slot):
            return (slot.rearrange("p (k g) c -> p k g c", g=4),
                    w_down[fb * 512:(fb + 1) * 512, :].rearrange("(k p) (g c) -> p k g c", p=128, c=512))
        return mk

    units = []
    for h in range(8):
        units.append([wcols_unit(w_in, 16, s * 1024 + h * 128, 128, 0, s * 128) for s in range(4)])
    for j in range(8):
        units.append([wcols_unit(w_in, 16, 4096 + s * 1024 + j * 128, 128, 0, s * 128) for s in range(3)])
    for mg in range(4):
        units.append([wcols_unit(w_in, 16, 7168 + mg * 512, 512, 0, 0)])
        units.append([wcols_unit(w_in, 16, 9216 + mg * 512, 512, 0, 0)])
        units.append([wcols_unit(w_a, 8, mg * 512, 512, 0, 0), wcols_unit(w_b, 8, mg * 512, 512, 8, 0)])
    for cg in range(4):
        units.append([wcols_unit(w_out, 16, cg * 512, 512, 0, 0)])
    for fb in range(16):
        units.append([wcols_unit(w_up, 16, fb * 512, 512, 0, 0)])
        units.append([wdown_unit(fb)])

    with ExitStack() as es:
        S = Sched(nc, es)
        A = Arena(nc, es, ARENA_BYTES)
        banks = [es.enter_context(nc.psum_tensor("bank%d" % i, [128, 512], F32)) for i in range(8)]
        B = [b[:, :] for b in banks]
        BR = [S.reg("bank%d" % i) for i in range(8)]
        PH = S.reg("phase")
        R_dummy = S.reg("dummy")

        ident_f = A.alloc([128], F32)
        ident_b = A.alloc([128], BF16)
        ones_f = A.alloc([128], F32)
        maskA = A.alloc([512], F32)
        ones_b = A.alloc([128], BF16)
        rmask = A.alloc([512], F32)
        eps_c = A.alloc([1], F32)
        dummy = A.alloc([1], F32)
        lbl = A.alloc([16], F32)
        lbw = A.alloc([16], F32)
        lb = A.alloc([8], F32)
        oml = A.alloc([8], F32)
        hgw = A.alloc([1], F32)
        hgw_h = A.alloc([1], F32)
        a_col = A.alloc([8], F32)
        b_col = A.alloc([8], F32)
        lna_col = A.alloc([8], F32)
        wcols = A.alloc([32], F32)
        crep = A.alloc([16], F32)
        cbh = A.alloc([16], F32)
        hb = A.alloc([1], F32)
        CR = S.reg("consts")
        C2 = S.reg("consts2")
        CONS = [CR, C2]

        def phase_barrier(extra_reads=()):
            S.barrier()
            S.op("dve", MEMSET(dummy, 0.0), reads=list(extra_reads), writes=[PH, R_dummy])

        for dst, src in ((lbl, lbl_d), (hgw, hgw_d), (wcols, wcols_d), (crep, crep_d), (hb, hb_d)):
            S.dma("sp", DMA(dst, src), writes=[CR], waw=False)
        S.op("pool", MEMSET(ident_f, 0.0), writes=[C2])
        S.op("pool", ASEL(ident_f, ident_f, [[-1, 128]], ALU.not_equal, 1.0, 0, 1), reads=[C2], writes=[C2])
        S.op("pool", COPY(ident_b, ident_f), reads=[C2], writes=[C2])
        S.op("pool", MEMSET(ones_f, 1.0 / 128.0), writes=[C2])
        S.op("pool", MEMSET(eps_c, EPS), writes=[C2])
        S.op("pool", MEMSET(rmask, 1.0), writes=[C2])
        S.op("pool", MEMSET(rmask.rearrange("p (c t) -> p c t", t=64)[:, :, 0:1], 0.0), reads=[C2], writes=[C2])
        S.op("pool", MEMSET(maskA, 1.0), writes=[C2])
        mlo = maskA[0:64, :].rearrange("p (j t) -> p j t", t=128)
        mhi = maskA[64:128, :].rearrange("p (j t) -> p j t", t=128)
        S.op("pool", ASEL(mlo, mlo, [[0, 4], [1, 128]], ALU.is_ge, 0.0, 0, -1), reads=[C2], writes=[C2])
        S.op("pool", ASEL(mlo, mlo, [[0, 4], [-1, 128]], ALU.is_ge, 0.0, 63, 0), reads=[C2], writes=[C2])
        S.op("pool", ASEL(mhi, mhi, [[0, 4], [1, 128]], ALU.is_ge, 0.0, -64, -1), reads=[C2], writes=[C2])
        S.op("pool", MEMSET(ones_b, 1.0 / 128.0), writes=[C2])
        S.op("dve", TT(lbw[:, 0:8], lbl[:, 8:16], lbl[:, 0:8], ALU.subtract), reads=[CR], writes=[C2])
        S.op("act", ACTV(lbw[:, 8:16], lbw[:, 0:8], AF.Exp, scale=-1.0), reads=[C2], writes=[C2])
        S.op("act", ACTV(lbw[:, 0:8], lbw[:, 0:8], AF.Exp), reads=[C2], writes=[C2])
        S.op("dve", TS(lbw, lbw, 1.0, ALU.add), reads=[C2], writes=[C2])
        S.op("dve", RECIP(lb, lbw[:, 0:8]), reads=[C2], writes=[C2])
        S.op("dve", RECIP(oml, lbw[:, 8:16]), reads=[C2], writes=[C2])
        S.op("dve", TS(cbh, crep, hb[:, 0:1], ALU.add), reads=[CR], writes=[C2])
        S.op("dve", TS(a_col, oml, 0.5, ALU.mult), reads=[C2], writes=[C2])
        S.op("dve", TT(b_col, a_col, lb, ALU.add), reads=[C2], writes=[C2])
        S.op("act", ACTV(lna_col, a_col, AF.Ln), reads=[C2], writes=[C2])
        S.op("dve", TS(hgw_h, hgw, 0.5, ALU.mult), reads=[CR], writes=[C2])

        slots = [(A.alloc([16, 512], BF16), S.reg("slot%d" % i)) for i in range(3)]
        H_off = A.reserve(32768)
        uT_h0 = A.at(H_off, [16, 512], BF16)
        uT_h1 = A.at(H_off + 16384, [16, 512], BF16)
        uT_own = A.alloc([16, 1024], BF16)
        R_uh0, R_uh1, R_uo = S.reg("uh0"), S.reg("uh1"), S.reg("uo")
        ws = WStream(S, units)
        ws.add_slot(*slots[0])

        def ublk(blk):
            if blk == 0:
                return uT_h0, R_uh0, 0
            if blk == 1:
                return uT_h1, R_uh1, 0
            return uT_own, R_uo, (blk - 2) * 512

        def norm_transpose(n_tiles, load_tile, dst_of_group, wc0, xts, xregs, ssq, sd, rstd, junk, NR):
            ng = len(xts) // 4
            NRt = [S.reg("nrt%d" % i) for i in range(n_tiles)]
            RJ = S.reg("junk")

            def bufs(g):
                return xts[(g % ng) * 4:(g % ng) * 4 + 4], xregs[(g % ng) * 4:(g % ng) * 4 + 4]

            def stats(g):
                xg, xrg = bufs(g)
                for j in range(4):
                    i = g * 4 + j
                    xt, xr = xg[j], xrg[j]
                    src, sreg = load_tile(i, xt, xr)
                    S.op("act", ACTV(junk, src, AF.Square, accum=ssq[:, i:i + 1]), reads=[sreg], writes=[NRt[i], RJ])
                    S.op("act", ACTV(sd[:, i:i + 1], ssq[:, i:i + 1], AF.Sqrt, bias=eps_c[:, 0:1], scale=1.0 / 2048.0),
                         reads=[NRt[i], C2], writes=[NRt[i]])
                    S.op("dve", RECIP(rstd[:, i:i + 1], sd[:, i:i + 1]), reads=[NRt[i]], writes=[NRt[i]])
                    S.op("dve", TS(xt, src, rstd[:, i:i + 1], ALU.mult), reads=[NRt[i], sreg], writes=[xr])

            bi = 0
            ngroups = n_tiles // 4
            stats(0)
            for g in range(ngroups):
                if g + 1 < ngroups:
                    stats(g + 1)
                xg, xrg = bufs(g)
                dstT, dreg, tok0 = dst_of_group(g)
                for f in range(16):
                    bk = bi % 8
                    bi += 1
                    fns = [TR(B[bk][:, j * 128:(j + 1) * 128], xg[j][:, f * 128:(f + 1) * 128], ident_f) for j in range(4)]
                    S.group("pe", fns, reads=list(xrg) + CONS, excl=[BR[bk]])
                    sc = wcols[:, wc0 + f:wc0 + f + 1]
                    if f % 2 == 0:
                        S.op("act", ACTV(dstT[:, f, tok0:tok0 + 512], B[bk], AF.Copy, scale=sc),
                             reads=CONS, excl=[BR[bk]], writes=[dreg], waw=False)
                    else:
                        S.op("dve", TS(dstT[:, f, tok0:tok0 + 512], B[bk], sc, ALU.mult),
                             reads=CONS, excl=[BR[bk]], writes=[dreg], waw=False)

        A.push()
        xts = [A.alloc([2048], F32) for _ in range(8)]
        xregs = [S.reg("xt%d" % j) for j in range(8)]
        junk = A.alloc([2048], BF16)
        ssq = A.alloc([16], F32)
        sd = A.alloc([16], F32)
        rstd = A.alloc([16], F32)
        NR = S.reg("nr")

        def load0(i, xt, xr):
            S.dma("sp", DMA(xt, xw[i * 128:(i + 1) * 128, :]), writes=[xr])
            return xt, xr

        norm_transpose(16, load0, ublk, 0, xts, xregs, ssq, sd, rstd, junk, NR)
        ws.extra_reads = list(xregs)
        ws.add_slot(*slots[1])
        ws.add_slot(*slots[2])
        ws.extra_reads = []
        A.pop()
        phase_barrier()

        A.push()
        yaT = A.alloc([8, 1024], BF16)
        ybT = A.alloc([8, 1024], BF16)
        R_ya, R_yb = S.reg("ya"), S.reg("yb")
        A.push()
        f_sq = A.alloc([512], F32)
        f_ga = A.alloc([512], F32)
        f_sn = A.alloc([512], F32)
        f_lg = A.alloc([512], F32)
        f_bb = A.alloc([512], F32)
        f_eb = A.alloc([512], F32)
        f_sg = A.alloc([512], F32)
        f_oT = A.alloc([512], F32)
        f_os = A.alloc([512], F32)
        b_vT = A.alloc([512], BF16)
        b_kT = A.alloc([512], BF16)
        b_qT = A.alloc([512], BF16)
        b_kh = A.alloc([512], BF16)
        b_tok = A.alloc([8, 128], BF16)
        b_As = A.alloc([512], BF16)
        b_os = A.alloc([512], BF16)
        Sring = [A.alloc([9, 128], F32) for _ in range(2)]
        Sb = [A.alloc([9, 128], BF16) for _ in range(2)]
        R = {n: S.reg(n) for n in ["sq", "ga", "sn", "lg", "bb", "eb", "sg", "oT", "os", "vT", "kT", "qT", "kh", "tok", "As", "S",
                                   "Sb0", "Sb1"]}
        RSb = [R["Sb0"], R["Sb1"]]
        BTb = B[3].bitcast(BF16)
        QS = 128.0 ** -0.5
        its = [(h, blk) for h in range(8) for blk in range(4)][:P1_NIT]
        hslot = {}
        f_sg2 = [f_sg, A.alloc([512], F32), A.alloc([512], F32)]
        R["sg0"], R["sg1"], R["sg2"] = S.reg("sg0"), S.reg("sg1"), S.reg("sg2")
        b_qT2 = [b_qT, A.alloc([512], BF16)]
        R["qT0"], R["qT1"] = S.reg("qT0"), S.reg("qT1")

        f_eb2 = [f_eb, A.alloc([512], F32)]
        R["eb0"], R["eb1"] = S.reg("eb0"), S.reg("eb1")

        def slot_of(n):
            h, blk = its[n]
            if h not in hslot:
                hslot[h] = ws.get()
            return hslot[h]

        def pe_proj(n, seg, bk):
            h, blk = its[n]
            sl, sreg = slot_of(n)
            uT, ureg, t0 = ublk(blk)
            fns = [MM(B[bk], sl[:, k, seg * 128:(seg + 1) * 128], uT[:, k, t0:t0 + 512], k == 0, k == 15) for k in range(16)]
            S.group("pe", fns, reads=[sreg, ureg], excl=[BR[bk]])

        def own_(n):
            return 0 <= n < len(its) and its[n][1] >= 2

        def s_hq(n):
            pe_proj(n, 0, 0)
            S.op("act", ACTV(f_sq, B[0], AF.Tanh, scale=0.5), excl=[BR[0]], writes=[R["sq"]])
            S.op("dve", STT(f_sq, f_sq, 1.0, B[0], ALU.add, ALU.mult), reads=[R["sq"]], excl=[BR[0]], writes=[R["sq"]])

        def s_hf(n, mid=None):
            h = its[n][0]
            if mid is None:
                pe_proj(n, 1, 1)
            else:
                hh_, blk_ = its[n]
                sl, sreg = slot_of(n)
                uT, ureg, t0 = ublk(blk_)
                mk = lambda k: MM(B[1], sl[:, k, 128:256], uT[:, k, t0:t0 + 512], k == 0, k == 15)
                S.group("pe", [mk(k) for k in range(8)], reads=[sreg, ureg], excl=[BR[1]])
                mid()
                S.group("pe", [mk(k) for k in range(8, 16)], reads=[sreg, ureg], excl=[BR[1]])
            S.op("act", ACTV(f_ga, B[1], AF.Tanh, scale=0.5), excl=[BR[1]], writes=[R["ga"]])
            S.op("act", ACTV(f_sn, B[1], AF.Tanh, scale=-0.5), excl=[BR[1]], writes=[R["sn"]])
            S.op("dve", TS(f_ga, f_ga, a_col[:, h:h + 1], ALU.mult, b_col[:, h:h + 1], ALU.add), reads=[R["ga"]] + CONS, writes=[R["ga"]])

        def s_ln(n):
            S.op("act", ACTV(f_lg, f_ga, AF.Ln), reads=[R["ga"]], writes=[R["lg"]])

        def s_scan(n):
            S.op("dve", SCAN(f_bb, rmask, f_lg), reads=[R["lg"]] + CONS, writes=[R["bb"]])

        def s_exps(n):
            h = its[n][0]
            eb, reb = f_eb2[n % 2], R["eb%d" % (n % 2)]
            S.op("act", ACTV(eb, f_bb, AF.Exp), reads=[R["bb"]], writes=[reb])
            S.op("act", ACTV(f_lg, f_bb, AF.Exp, scale=-1.0, bias=lna_col[:, h:h + 1]), reads=[R["bb"]] + CONS, writes=[R["lg"]])

        def s_kq(n):
            h = its[n][0]
            eb, reb = f_eb2[n % 2], R["eb%d" % (n % 2)]
            S.op("dve", STT(b_kT, f_sn, 1.0, f_lg, ALU.add, ALU.mult), reads=[R["sn"], R["lg"]] + CONS, writes=[R["kT"]])
            if own_(n):
                S.op("dve", STT(b_qT2[n % 2], f_sq, 0.5 * QS, eb, ALU.mult, ALU.mult), reads=[R["sq"], reb], writes=[R["qT%d" % (n % 2)]])
            eb3 = eb.rearrange("p (c t) -> p c t", t=64)
            S.op("dve", TT(b_kh.rearrange("p (c t) -> p c t", t=64), b_kT.rearrange("p (c t) -> p c t", t=64),
                           eb3[:, :, 63:64].broadcast_to([128, 8, 64]), ALU.mult), reads=[R["kT"], reb], writes=[R["kh"]])

        def s_hg(n):
            pe_proj(n, 3, 4)
            sg, rsg = f_sg2[n % 3], R["sg%d" % (n % 3)]
            S.op("act", ACTV(sg, B[4], AF.Tanh, scale=0.5), excl=[BR[4]], writes=[rsg])

        def s_hg_b(n):
            sg, rsg = f_sg2[n % 3], R["sg%d" % (n % 3)]
            S.op("dve", STT(sg, sg, 1.0, B[4], ALU.add, ALU.mult), reads=[rsg], excl=[BR[4]], writes=[rsg])

        def s_hi(n):
            pe_proj(n, 2, 2)
            S.op("act", ACTV(b_vT, B[2], AF.Copy), excl=[BR[2]], writes=[R["vT"]])

        def s_tr(n):
            fns = [TR(BTb[:, j * 128:(j + 1) * 128], b_kh[:, j * 128:(j + 1) * 128], ident_b) for j in range(4)]
            fns += [TR(BTb[:, (4 + j) * 128:(5 + j) * 128], b_vT[:, j * 128:(j + 1) * 128], ident_b) for j in range(4)]
            S.group("pe", fns, reads=[R["kh"], R["vT"]] + CONS, excl=[BR[3]])
            S.op("act", ACTV(b_tok, BTb.rearrange("p (a b) -> p a b", b=128), AF.Copy), excl=[BR[3]], writes=[R["tok"]])

        def s_ms_a(n):
            S.op("pe", MM(B[7], ones_b, b_os), reads=[R["os"]] + CONS, excl=[BR[7]])
            S.op("act", ACTV(f_os, B[7], AF.Ln, bias=eps_c[:, 0:1]), reads=CONS, excl=[BR[7]], writes=[R["os"]])
            S.op("act", ACTV(f_os, f_os, AF.Exp, scale=-0.5), reads=[R["os"]], writes=[R["os"]])

        def s_ms_b(n):
            h, blk = its[n]
            S.op("dve", TT(f_oT, f_oT, f_os, ALU.mult), reads=[R["oT"], R["os"]], writes=[R["oT"]])
            t0 = (blk - 2) * 512
            S.op("dve", STT(yaT[:, h, t0:t0 + 512], f_oT, hgw_h[:, 0:1], f_sg2[n % 3], ALU.mult, ALU.mult),
                 reads=[R["oT"], R["sg%d" % (n % 3)]] + CONS, writes=[R_ya], waw=False)

        def s_u(n):
            fns = []
            for c in range(8):
                j, hh = c // 2, c % 2
                rows = slice(hh * 64, hh * 64 + 64)
                fns.append(MM(B[5 + hh][:, j * 128:(j + 1) * 128], b_tok[rows, j, :], b_tok[rows, 4 + j, :]))
            S.group("pe", fns, reads=[R["tok"]], excl=[BR[5], BR[6]])
            if own_(n):
                q = b_qT2[n % 2]
                fns = [MM(B[4][:, j * 128:(j + 1) * 128], b_kT[:, j * 128:(j + 1) * 128], q[:, j * 128:(j + 1) * 128]) for j in range(4)]
                S.group("pe", fns, reads=[R["kT"], R["qT%d" % (n % 2)]], excl=[BR[4]])
                S.op("dve", TT(b_As, B[4], maskA, ALU.mult), reads=CONS, excl=[BR[4]], writes=[R["As"]])

        def s_chain(n, c0, c1):
            par = its[n][1] % 2
            eb, reb = f_eb2[n % 2], R["eb%d" % (n % 2)]
            for c in range(c0, c1):
                ub = B[5 + c % 2][:, (c // 2) * 128:(c // 2 + 1) * 128]
                sin = Sring[1 - par][:, 8, :] if c == 0 else Sring[par][:, c, :]
                S.op("dve", STT(Sring[par][:, c + 1, :], sin, eb[:, c * 64 + 63:c * 64 + 64], ub, ALU.mult, ALU.add),
                     reads=[R["S"], reb], writes=[R["S"]], excl=[BR[5 + c % 2]])

        def s_cast(n):
            h, blk = its[n]
            par = blk % 2
            if blk >= 2:
                S.op("act", ACTV(Sb[par][:, 1:9, :], Sring[par][:, 1:9, :], AF.Copy), reads=[R["S"]], writes=[RSb[par]], waw=False)
            elif blk == 1:
                S.op("act", ACTV(Sb[par][:, 8, :], Sring[par][:, 8, :], AF.Copy), reads=[R["S"]], writes=[RSb[par]], waw=False)

        def s_o(n):
            par = its[n][1] % 2
            q = b_qT2[n % 2]
            fns = []
            for j in range(4):
                oc = B[7][:, j * 128:(j + 1) * 128]
                fns.append(MM(oc, b_tok[:, 4 + j, :], b_As[:, j * 128:(j + 1) * 128], True, False))
                for c in (2 * j, 2 * j + 1):
                    sprev = Sb[1 - par][:, 8, :] if c == 0 else Sb[par][:, c, :]
                    fns.append(MM(B[7][:, c * 64:(c + 1) * 64], sprev, q[:, c * 64:(c + 1) * 64], False, c == 2 * j + 1))
            S.group("pe", fns, reads=[RSb[0], RSb[1], R["qT%d" % (n % 2)], R["tok"], R["As"]], excl=[BR[7]])
            S.op("act", ACTV(f_oT, B[7], AF.Copy), excl=[BR[7]], writes=[R["oT"]])
            S.op("act", ACTV(b_os, B[7], AF.Square), excl=[BR[7]], writes=[R["os"]])

        if RUN_P1:
            NI = len(its)

            def load_first_half(nx):
                if own_(nx):
                    s_hq(nx)
                s_hf(nx)

            load_first_half(0)
            s_ln(0)
            s_scan(0)
            if own_(0):
                s_hg(0)
            s_hi(0)
            s_exps(0)
            s_kq(0)
            if own_(0):
                s_hg_b(0)
            for n in range(NI):
                h, blk = its[n]
                nx = n + 1 if n + 1 < NI else None
                if blk == 0:
                    S.op("dve", MEMSET(Sring[1][:, 8, :], 0.0), writes=[R["S"]])
                if nx is not None:
                    if own_(nx):
                        s_hq(nx)
                    s_hf(nx, mid=lambda n=n: s_tr(n))
                else:
                    s_tr(n)
                if own_(n - 1):
                    s_ms_a(n - 1)
                if nx is not None:
                    s_ln(nx)
                s_u(n)
                s_chain(n, 0, 8)
                if own_(n - 1):
                    s_ms_b(n - 1)
                if nx is not None:
                    s_scan(nx)
                    if own_(nx):
                        s_hg(nx)
                s_cast(n)
                if nx is not None:
                    if own_(nx):
                        s_hg_b(nx)
                    s_hi(nx)
                    s_exps(nx)
                    s_kq(nx)
                if own_(n):
                    s_o(n)
                if blk == 0 and h >= 1:
                    ws.release(hslot[h - 1])
            if own_(NI - 1):
                s_ms_a(NI - 1)
                s_ms_b(NI - 1)
            ws.release(hslot[its[NI - 1][0]])
        A.pop()
        phase_barrier()
        if DEBUG == 1:
            dbg = nc.dram_tensor("dbg", [128, 8, 1024], BF16, kind="ExternalOutput").ap()
            S.dma("sp", DMA(dbg, yaT), reads=[R_ya, PH])

        A.push()
        rbt = A.alloc([16, 384], F32)
        aqT = A.alloc([1024], BF16)
        akT = A.alloc([1536], BF16)
        Vext = A.alloc([12, 2, 65], BF16)
        tmpS = [A.alloc([384], F32) for _ in range(3)]
        PT = [A.alloc([640], BF16) for _ in range(3)]
        ybt = A.alloc([8, 128], BF16)
        rcp = A.alloc([16], F32)
        avT = A.alloc([1536], BF16)
        R2 = {n: S.reg(n) for n in ["rbt", "aq", "ak", "V", "tmp0", "tmp1", "tmp2", "PT0", "PT1", "PT2", "ybt", "rcp", "avT"]}
        S.dma("sp", DMA(rbt, rbt_d), reads=[PH], writes=[R2["rbt"]])
        S.op("dve", MEMSET(Vext, 1.0), reads=[PH], writes=[R2["V"]])
        for i in range(3):
            S.op("dve", MEMSET(PT[i], 0.0), reads=[PH], writes=[R2["PT%d" % i]])
        BT6 = B[6].bitcast(BF16)
        SC = 64.0 ** -0.5
        unit_i = 0
        for j in range(8 if RUN_P2 else 0):
            sl, sreg = ws.get()
            pb = 0
            for name, dst, nblk, b0, seg in (("aq", aqT, 2, 2, 0), ("ak", akT, 3, 1, 1)):
                for bi_ in range(nblk):
                    uT, ureg, t0 = ublk(b0 + bi_)
                    bk = 6 + (pb % 2)
                    pb += 1
                    fns = [MM(B[bk], sl[:, k, seg * 128:(seg + 1) * 128], uT[:, k, t0:t0 + 512], k == 0, k == 15) for k in range(16)]
                    S.group("pe", fns, reads=[sreg, ureg], excl=[BR[bk]])
                    if pb % 2:
                        S.op("act", ACTV(dst[:, bi_ * 512:(bi_ + 1) * 512], B[bk], AF.Copy), excl=[BR[bk]], writes=[R2[name]], waw=False)
                    else:
                        S.op("dve", COPY(dst[:, bi_ * 512:(bi_ + 1) * 512], B[bk]), excl=[BR[bk]], writes=[R2[name]], waw=False)
            for bi_ in range(3):
                uT, ureg, t0 = ublk(1 + bi_)
                bk = 6 + (pb % 2)
                pb += 1
                fns = [MM(B[bk], sl[:, k, 256:384], uT[:, k, t0:t0 + 512], k == 0, k == 15) for k in range(16)]
                S.group("pe", fns, reads=[sreg, ureg], excl=[BR[bk]])
                if pb % 2:
                    S.op("act", ACTV(avT[:, bi_ * 512:(bi_ + 1) * 512], B[bk], AF.Copy), excl=[BR[bk]], writes=[R2["avT"]], waw=False)
                else:
                    S.op("dve", COPY(avT[:, bi_ * 512:(bi_ + 1) * 512], B[bk]), excl=[BR[bk]], writes=[R2["avT"]], waw=False)
            for (w0, nw) in ((0, 8), (8, 4)):
                bk = 6 + (pb % 2)
                pb += 1
                bt = B[bk].bitcast(BF16)
                fns = [TR(bt[:, i * 128:(i + 1) * 128], avT[:, (w0 + i) * 128:(w0 + i + 1) * 128], ident_b) for i in range(nw)]
                S.group("pe", fns, reads=[R2["avT"]] + CONS, excl=[BR[bk]])
                S.op("act", ACTV(Vext[:, w0:w0 + nw, :, 0:64], bt[:, 0:nw * 128].rearrange("p (a b c) -> p a b c", b=2, c=64), AF.Copy),
                     excl=[BR[bk]], writes=[R2["V"]], waw=False)
            ws.release((sl, sreg))
            def stage_a(qt, hh, u):
                head = 2 * j + hh
                rows = slice(hh * 64, hh * 64 + 64)
                bsA, bsB = 2 * u, 2 * u + 1
                fns = []
                for o in range(5):
                    kt = qt + 4 - o
                    ob_ = B[bsA][:, o * 128:(o + 1) * 128] if o < 3 else B[bsB][:, (o - 3) * 128:(o - 2) * 128]
                    fns.append(MM(ob_, akT[rows, kt * 128:(kt + 1) * 128], aqT[rows, qt * 128:(qt + 1) * 128]))
                S.group("pe", fns, reads=[R2["aq"], R2["ak"]], excl=[BR[bsA], BR[bsB]])
                tr, pr = R2["tmp%d" % u], R2["PT%d" % u]
                S.op("dve", STT(tmpS[u], B[bsA][:, 0:384], SC, rbt[:, head, :], ALU.mult, ALU.add),
                     reads=[R2["rbt"]], excl=[BR[bsA]], writes=[tr])
                o = 0
                while o < 3:
                    hist = (qt + 4 - o) < 4
                    o2 = o
                    while o2 + 1 < 3 and ((qt + 4 - (o2 + 1)) < 4) == hist:
                        o2 += 1
                    c0, c1 = o * 128, (o2 + 1) * 128
                    if hist:
                        S.op("act", ACTV(PT[u][:, c0:c1], tmpS[u][:, c0:c1], AF.Exp, bias=hb[:, 0:1]),
                             reads=[tr] + CONS, writes=[pr], waw=False)
                    else:
                        S.op("act", ACTV(PT[u][:, c0:c1], tmpS[u][:, c0:c1], AF.Exp), reads=[tr], writes=[pr], waw=False)
                    o = o2 + 1
                h3, h4 = (qt + 1) < 4, qt < 4
                if h3 == h4:
                    bias = (cbh if h3 else crep)[:, head:head + 1]
                    S.op("act", ACTV(PT[u][:, 384:640], B[bsB][:, 0:256], AF.Exp, bias=bias, scale=SC),
                         reads=CONS, excl=[BR[bsB]], writes=[pr], waw=False)
                else:
                    for o in (3, 4):
                        hist = (qt + 4 - o) < 4
                        bias = (cbh if hist else crep)[:, head:head + 1]
                        S.op("act", ACTV(PT[u][:, o * 128:(o + 1) * 128], B[bsB][:, (o - 3) * 128:(o - 2) * 128], AF.Exp, bias=bias, scale=SC),
                             reads=CONS, excl=[BR[bsB]], writes=[pr], waw=False)
                S.op("pool", MEMSET(PT[u][0:64, 4 * 128 + 64:5 * 128], 0.0), writes=[pr])

            def stage_b(qt, hh, u, ob):
                pr = R2["PT%d" % u]
                fns = []
                for o in range(5):
                    kt = qt + 4 - o
                    fns.append(MM(B[ob][:, 0:65], PT[u][:, o * 128:(o + 1) * 128], Vext[:, kt, hh, :], o == 0, o == 4))
                S.group("pe", fns, reads=[pr, R2["V"]], excl=[BR[ob]])
                rc = rcp[:, (ob - 6):(ob - 5)]
                S.op("dve", RECIP(rc, B[ob][:, 64:65]), excl=[BR[ob]], writes=[R2["rcp"]])
                S.op("dve", TS(ybt[:, qt, hh * 64:(hh + 1) * 64], B[ob][:, 0:64], rc, ALU.mult),
                     reads=[R2["rcp"]], excl=[BR[ob]], writes=[R2["ybt"]], waw=False)

            ulist = [(qt, hh) for qt in range(8) for hh in range(2)]
            NU = len(ulist)
            for n_ in range(min(2, NU)):
                stage_a(ulist[n_][0], ulist[n_][1], n_ % 3)
            for n_, (qt, hh) in enumerate(ulist):
                if n_ + 2 < NU:
                    stage_a(ulist[n_ + 2][0], ulist[n_ + 2][1], (n_ + 2) % 3)
                stage_b(qt, hh, n_ % 3, 6 + n_ % 2)
            fns = [TR(BT6[:, qt * 128:(qt + 1) * 128], ybt[:, qt, :], ident_b) for qt in range(8)]
            S.group("pe", fns, reads=[R2["ybt"]] + CONS, excl=[BR[6]])
            S.op("act", ACTV(ybT[:, j, :], BT6, AF.Copy), excl=[BR[6]], writes=[R_yb], waw=False)
        A.pop()
        phase_barrier()
        if DEBUG == 2:
            dbg = nc.dram_tensor("dbg", [128, 8, 1024], BF16, kind="ExternalOutput").ap()
            S.dma("sp", DMA(dbg, ybT), reads=[R_yb, PH])

        mergedT = A.at(H_off, [16, 1024], BF16)
        R_mg = S.reg("merged")
        A.push()
        g_a = [A.alloc([512], F32) for _ in range(2)]
        g_b = [A.alloc([512], F32) for _ in range(2)]
        R3 = {n: S.reg(n) for n in ["ga0", "ga1", "gb0", "gb1"]}
        if RUN_P3:
            for i in range(2):
                xs_ap, xs_reg = A.alloc([16, 512], BF16), S.reg("xslot%d" % i)
                S.op("dve", MEMSET(dummy, 0.0), reads=[PH, xs_reg], writes=[R_dummy])
                ws.add_slot(xs_ap, xs_reg, max_unit=16 + 12 - 1)
        it3 = 0
        for mg in range(4 if RUN_P3 else 0):
            sx = ws.get()
            sy = ws.get()
            sz = ws.get()
            for mi in range(4):
                m = mg * 4 + mi
                cs = slice(mi * 128, (mi + 1) * 128)
                for t2 in range(2):
                    u = it3 % 2
                    it3 += 1
                    bb = 4 * u
                    ts_ = slice(t2 * 512, (t2 + 1) * 512)
                    S.group("pe", [MM(B[bb + 0], sx[0][:, k, cs], uT_own[:, k, ts_], k == 0, k == 15) for k in range(16)],
                            reads=[sx[1], R_uo], excl=[BR[bb + 0]])
                    S.group("pe", [MM(B[bb + 1], sy[0][:, k, cs], uT_own[:, k, ts_], k == 0, k == 15) for k in range(16)],
                            reads=[sy[1], R_uo], excl=[BR[bb + 1]])
                    S.group("pe", [MM(B[bb + 2], sz[0][:, k, cs], yaT[:, k, ts_], k == 0, k == 7) for k in range(8)],
                            reads=[sz[1], R_ya], excl=[BR[bb + 2]])
                    S.group("pe", [MM(B[bb + 3], sz[0][:, 8 + k, cs], ybT[:, k, ts_], k == 0, k == 7) for k in range(8)],
                            reads=[sz[1], R_yb], excl=[BR[bb + 3]])
                    ra, rb_ = R3["ga%d" % u], R3["gb%d" % u]
                    S.op("act", ACTV(g_a[u], B[bb + 0], AF.Sigmoid), excl=[BR[bb + 0]], writes=[ra])
                    S.op("act", ACTV(g_b[u], B[bb + 1], AF.Sigmoid), excl=[BR[bb + 1]], writes=[rb_])
                    S.op("dve", TT(g_a[u], g_a[u], B[bb + 2], ALU.mult), reads=[ra], excl=[BR[bb + 2]], writes=[ra])
                    S.op("dve", TT(g_b[u], g_b[u], B[bb + 3], ALU.mult), reads=[rb_], excl=[BR[bb + 3]], writes=[rb_])
                    S.op("dve", TT(mergedT[:, m, ts_], g_a[u], g_b[u], ALU.add), reads=[ra, rb_, PH], writes=[R_mg], waw=False)
            ws.release(sx)
            ws.release(sy)
            ws.release(sz)
        A.pop()
        A.pop()
        phase_barrier()
        if DEBUG == 3:
            dbg = nc.dram_tensor("dbg", [128, 16, 1024], BF16, kind="ExternalOutput").ap()
            S.dma("sp", DMA(dbg, mergedT), reads=[R_mg, PH])

        hres = A.alloc([8, 2048], F32)
        R_h = [S.reg("h%d" % t) for t in range(8)]
        u2T = uT_own
        R_u2 = S.reg("u2T")
        A.push()
        hnb = [A.alloc([2048], BF16) for _ in range(2)]
        R_hn = [S.reg("hnb%d" % i) for i in range(2)]
        wrep_mlp = A.alloc([2048], F32)
        ssq5 = A.alloc([8], F32)
        sd5 = A.alloc([8], F32)
        rstd5 = A.alloc([8], F32)
        NR5 = S.reg("nr5")
        R_wm = S.reg("wrep_mlp")
        S.dma("sp", DMA(wrep_mlp, wmlp_d), reads=[PH], writes=[R_wm])
        S.op("dve", MEMSET(ssq5, 0.0), reads=[PH], writes=[NR5] + R_hn)
        for tt in range(8):
            S.dma("sp", DMA(hres[:, tt, :], xw[1024 + tt * 128:1024 + (tt + 1) * 128, :]), reads=[PH], writes=[R_h[tt]])
        BT6b, BT7b = B[6].bitcast(BF16), B[7].bitcast(BF16)

        def emit_norm_stats(tt):
            u = tt % 2
            hn, rhn = hnb[u], R_hn[u]
            S.op("act", ACTV(hn, hres[:, tt, :], AF.Square, accum=ssq5[:, tt:tt + 1]), reads=[R_h[tt]], writes=[NR5, rhn])
            S.op("act", ACTV(sd5[:, tt:tt + 1], ssq5[:, tt:tt + 1], AF.Sqrt, bias=eps_c[:, 0:1], scale=1.0 / 2048.0),
                 reads=[NR5] + CONS, writes=[NR5])
            S.op("dve", RECIP(rstd5[:, tt:tt + 1], sd5[:, tt:tt + 1]), reads=[NR5], writes=[NR5])
            S.op("dve", STT(hn, hres[:, tt, :], rstd5[:, tt:tt + 1], wrep_mlp, ALU.mult, ALU.mult),
                 reads=[R_h[tt], NR5, R_wm], writes=[rhn])

        def emit_norm_trans(tt):
            u = tt % 2
            hn, rhn = hnb[u], R_hn[u]
            for half, bt, bk in ((0, BT6b, 6), (1, BT7b, 7)):
                fns = [TR(bt[:, i * 128:(i + 1) * 128], hn[:, (half * 8 + i) * 128:(half * 8 + i + 1) * 128], ident_b) for i in range(8)]
                S.group("pe", fns, reads=[rhn] + CONS, excl=[BR[bk]])
                dst = u2T[:, half * 8:(half + 1) * 8, tt * 128:(tt + 1) * 128]
                srcv = bt.rearrange("p (a b) -> p a b", b=128)
                if half == 0:
                    S.op("act", ACTV(dst, srcv, AF.Copy), excl=[BR[bk]], writes=[R_u2], waw=False)
                else:
                    S.op("dve", COPY(dst, srcv), excl=[BR[bk]], writes=[R_u2], waw=False)

        it4 = 0
        for cg in range(4 if RUN_P4 else 0):
            sl, sreg = ws.get()
            for tt in range(8):
                bk = it4 % 6
                it4 += 1
                S.group("pe", [MM(B[bk], mergedT[:, k, tt * 128:(tt + 1) * 128], sl[:, k, :], k == 0, k == 15) for k in range(16)],
                        reads=[sreg, R_mg], excl=[BR[bk]])
                hv = hres[:, tt, cg * 512:(cg + 1) * 512]
                S.op("dve", TT(hv, hv, B[bk], ALU.add), reads=[R_h[tt]], excl=[BR[bk]], writes=[R_h[tt]])
                if cg == 3 and tt >= 1:
                    emit_norm_stats(tt - 1)
                if cg == 3 and tt >= 2:
                    emit_norm_trans(tt - 2)
            ws.release((sl, sreg))
        if RUN_P4:
            emit_norm_stats(7)
            emit_norm_trans(6)
            emit_norm_trans(7)
        else:
            for tt in range(8):
                emit_norm_stats(tt)
                emit_norm_trans(tt)
        A.pop()
        phase_barrier([s[1] for s in slots])

        A.push()
        slot4 = (A.at(H_off, [16, 512], BF16), S.reg("slot3"))
        hid = [A.at(H_off + 16384 + i * 8192, [4, 1024], BF16) for i in range(2)]
        rr = [A.alloc([512], F32) for _ in range(2)]
        R6 = {n: S.reg(n) for n in ["hid0", "hid1", "rr0", "rr1"]}
        S.op("dve", MEMSET(dummy, 0.0), reads=[PH, slot4[1]], writes=[R6["hid0"], R6["hid1"], R_dummy])
        ws.add_slot(*slot4)
        upc = [0]

        def emit_up_unit(fb, un, sa):
            fi, half = un // 2, un % 2
            par = fb % 2
            bk = upc[0] % 4
            u = upc[0] % 2
            upc[0] += 1
            S.group("pe", [MM(B[bk], sa[0][:, k, fi * 128:(fi + 1) * 128], u2T[:, k, half * 512:(half + 1) * 512], k == 0, k == 15)
                           for k in range(16)], reads=[sa[1], R_u2], excl=[BR[bk]])
            S.op("act", ACTV(rr[u], B[bk], AF.Relu), excl=[BR[bk]], writes=[R6["rr%d" % u]])
            S.op("dve", TT(hid[par][:, fi, half * 512:(half + 1) * 512], rr[u], rr[u], ALU.mult),
                 reads=[R6["rr%d" % u]], writes=[R6["hid%d" % par]], waw=False)

        def emit_down_tt(fb, tt, sb):
            par = fb % 2
            fns = []
            for k in range(4):
                for cg in range(4):
                    fns.append(MM(B[4 + cg], hid[par][:, k, tt * 128:(tt + 1) * 128], sb[0][:, k * 4 + cg, :], k == 0, k == 3))
            S.group("pe", fns, reads=[sb[1], R6["hid%d" % par]], excl=[BR[4], BR[5], BR[6], BR[7]])
            for cg in range(4):
                hv = hres[:, tt, cg * 512:(cg + 1) * 512]
                S.op("dve", TT(hv, hv, B[4 + cg], ALU.add), reads=[R_h[tt]], excl=[BR[4 + cg]], writes=[R_h[tt]])

        wrep = A.alloc([2048], F32)
        junk = A.alloc([2048], BF16)
        ssq = A.alloc([8], F32)
        sd = A.alloc([8], F32)
        rstd = A.alloc([8], F32)
        R7 = {n: S.reg(n) for n in ["wrep", "nr"]}
        S.dma("sp", DMA(wrep, wfin_d), reads=[PH], writes=[R7["wrep"]])
        S.op("dve", MEMSET(ssq, 0.0), reads=[PH], writes=[R7["nr"]])

        def emit_final(tt):
            S.op("act", ACTV(junk, hres[:, tt, :], AF.Square, accum=ssq[:, tt:tt + 1]), reads=[R_h[tt], PH], writes=[R7["nr"]])
            S.op("act", ACTV(sd[:, tt:tt + 1], ssq[:, tt:tt + 1], AF.Sqrt, bias=eps_c[:, 0:1], scale=1.0 / 2048.0),
                 reads=[R7["nr"]] + CONS, writes=[R7["nr"]])
            S.op("dve", RECIP(rstd[:, tt:tt + 1], sd[:, tt:tt + 1]), reads=[R7["nr"]], writes=[R7["nr"]])
            S.op("dve", STT(hres[:, tt, :], hres[:, tt, :], rstd[:, tt:tt + 1], wrep, ALU.mult, ALU.mult),
                 reads=[R_h[tt], R7["nr"], R7["wrep"]], writes=[R_h[tt]])
            S.dma("sp", DMA(out_d[tt * 128:(tt + 1) * 128, :], hres[:, tt, :]), reads=[R_h[tt]])

        if RUN_P6:
            sa = ws.get()
            for un in range(8):
                emit_up_unit(0, un, sa)
            ws.release(sa)
            for fb in range(16):
                sb = ws.get()
                sa = ws.get() if fb + 1 < 16 else None
                for tt in range(8):
                    emit_down_tt(fb, tt, sb)
                    if sa is not None:
                        emit_up_unit(fb + 1, tt, sa)
                    else:
                        emit_final(tt)
                ws.release(sb)
                if sa is not None:
                    ws.release(sa)
        else:
            for tt in range(8):
                emit_final(tt)
        A.pop()
        S.finish()
        S.replay()
        build_nc.peak = A.peak
        build_nc.log = A.log
    return nc


def _host_prep(inputs):
    x = np.asarray(inputs["x"], dtype=np.float32)
    rb = np.asarray(inputs["rel_bias"], dtype=np.float32)[0]
    lbl = np.ascontiguousarray(np.asarray(inputs["lb_logits"], np.float32).reshape(2, 8, 128).transpose(2, 0, 1).reshape(128, 16))
    hgw = np.ascontiguousarray(np.asarray(inputs["hg_norm_w"], np.float32)[0].reshape(128, 1))
    wc = np.concatenate([np.asarray(inputs["norm_mix_w"], np.float32)[0].reshape(16, 128).T,
                         np.asarray(inputs["norm_mlp_w"], np.float32)[0].reshape(16, 128).T], axis=1)
    wc = np.ascontiguousarray(wc)
    wfin = np.ascontiguousarray(np.broadcast_to(np.asarray(inputs["norm_final_w"], np.float32).reshape(1, 2048), (128, 2048)))
    wmlp = np.ascontiguousarray(np.broadcast_to(np.asarray(inputs["norm_mlp_w"], np.float32)[0].reshape(1, 2048), (128, 2048)))
    k = np.arange(128)[:, None, None]
    o = np.arange(3)[None, :, None]
    t = np.arange(128)[None, None, :]
    idx = np.clip(128 * o + t - k, -256, 256) + 256
    rbt = rb[:, idx]
    invalid = np.broadcast_to((o == 0) & (k >= 64) & (t < 64), idx.shape)
    rbt = np.where(invalid[None], np.float32(NEG), rbt)
    rbt = np.ascontiguousarray(rbt.transpose(1, 0, 2, 3).reshape(128, 16, 384)).astype(np.float32)
    crep = np.ascontiguousarray(np.broadcast_to(rb[:, 512][None, :], (128, 16))).astype(np.float32)
    shared = {
        "w_in": np.ascontiguousarray(np.asarray(inputs["w_in"], np.float32)[0]),
        "w_a": np.ascontiguousarray(np.asarray(inputs["w_branch_a"], np.float32)[0]),
        "w_b": np.ascontiguousarray(np.asarray(inputs["w_branch_b"], np.float32)[0]),
        "w_out": np.ascontiguousarray(np.asarray(inputs["w_out"], np.float32)[0]),
        "w_up": np.ascontiguousarray(np.asarray(inputs["w_up"], np.float32)[0]),
        "w_down": np.ascontiguousarray(np.asarray(inputs["w_down"], np.float32)[0]),
        "lbl": lbl, "hgw": hgw, "wcols": wc, "wfin": wfin, "wmlp": wmlp, "rbt": rbt, "crep": crep,
    }
    in_maps = []
    for c in range(8):
        b, half = c // 2, c % 2
        xwin = np.zeros((2048, 2048), np.float32)
        if half == 1:
            xwin[:] = x[b]
        else:
            xwin[1024:] = x[b, 0:1024]
        d = dict(shared)
        d["xw"] = xwin
        d["hbias"] = np.full((128, 1), 0.0 if half == 1 else NEG, np.float32)
        in_maps.append(d)
    return in_maps


_NC_CACHE = {}


def kernel(**inputs):
    in_maps = _host_prep(inputs)
    if "nc" not in _NC_CACHE:
        _NC_CACHE["nc"] = build_nc()
    nc = _NC_CACHE["nc"]
    res = run_bass_kernel_spmd(nc, in_maps, core_ids=list(range(8)))
    out = np.zeros((4, 2048, 2048), np.float32)
    for c in range(8):
        b, half = c // 2, c % 2
        out[b, half * 1024:(half + 1) * 1024] = res.results[c]["out"]
    return out
```

```python
from contextlib import ExitStack

import numpy as np
import concourse.bass as bass
import concourse.mybir as mybir
from concourse.bass_utils import run_bass_kernel_spmd

F32 = mybir.dt.float32
BF16 = mybir.dt.bfloat16
U8 = mybir.dt.uint8
AF = mybir.ActivationFunctionType
ALU = mybir.AluOpType

NEG = -30000.0
EPS = 1e-6


class Reg:
    __slots__ = ("name", "w", "r")

    def __init__(self, name):
        self.name = name
        self.w = {}
        self.r = {}


class Sched:
    LIMIT = 30000
    NDQ = 6

    def __init__(self, nc, es):
        self.nc = nc
        self.es = es
        self.engs = ("pe", "act", "dve", "pool", "sp")
        self.streams = {e: [] for e in self.engs}
        self.seen = {e: {} for e in self.engs}
        self.cur = {}
        self.nsem = 0
        for e in ("pe", "act", "dve", "pool"):
            self._newsem(e)
        self.dq = {q: [[self._alloc(), 0] for _ in range(self.NDQ)] for q in ("sp", "pool")}
        self.dq_rr = {"sp": 0, "pool": 0}

    def _alloc(self):
        sem = self.es.enter_context(self.nc.semaphore("s%d" % self.nsem))
        self.nsem += 1
        return (self.nsem, sem)

    def _newsem(self, e):
        self.cur[e] = [self._alloc(), 0]

    def reg(self, name="r"):
        return Reg(name)

    def _waits(self, eng, reads, writes, excl, waw):
        need = {}

        def add(tok, same_ok):
            key, sem, val, teng = tok
            if teng == eng and not same_ok:
                return
            if self.seen[eng].get(key, 0) >= val:
                return
            if key in need and need[key][1] >= val:
                return
            need[key] = (sem, val)

        for r in reads:
            for t in r.w.values():
                add(t, True)
        for w in writes:
            if waw:
                for t in w.w.values():
                    add(t, True)
            for t in w.r.values():
                add(t, True)
        for x in excl:
            for t in x.w.values():
                add(t, True)
            for t in x.r.values():
                add(t, True)
        for key, (sem, val) in need.items():
            self.seen[eng][key] = val
            self.streams[eng].append(lambda e, sem=sem, val=val: e.wait_ge(sem, val))

    def _mark(self, tok, reads, writes, excl, waw):
        key = tok[0]
        for r in reads:
            r.r[key] = tok
        for w in writes:
            if waw or w.r:
                w.w = {key: tok}
            else:
                w.w[key] = tok
            w.r = {}
        for x in excl:
            x.w = {key: tok}
            x.r = {}

    def _tick(self, eng):
        cur = self.cur[eng]
        if cur[1] >= self.LIMIT:
            self._newsem(eng)
            cur = self.cur[eng]
        (key, sem) = cur[0]
        cur[1] += 1
        return key, sem, cur[1]

    def op(self, eng, fn, reads=(), writes=(), excl=(), waw=True):
        self.group(eng, [fn], reads, writes, excl, waw)

    def group(self, eng, fns, reads=(), writes=(), excl=(), waw=True):
        self._waits(eng, reads, writes, excl, waw)
        key, sem, val = self._tick(eng)
        for fn in fns[:-1]:
            self.streams[eng].append(lambda e, fn=fn: fn(e))
        fn = fns[-1]
        self.streams[eng].append(lambda e, fn=fn, sem=sem: fn(e).then_inc(sem, 1))
        self._mark((key, sem, val, eng), reads, writes, excl, waw)

    def dma(self, q, fn, reads=(), writes=(), waw=True):
        i = self.dq_rr[q]
        self.dq_rr[q] = (i + 1) % self.NDQ
        slot = self.dq[q][i]
        (key, sem) = slot[0]
        if slot[1] > 0 and self.seen[q].get(key, 0) < 16 * slot[1]:
            v = 16 * slot[1]
            self.seen[q][key] = v
            self.streams[q].append(lambda e, sem=sem, v=v: e.wait_ge(sem, v))
        self._waits(q, reads, writes, (), waw)
        slot[1] += 1
        val = 16 * slot[1]
        self.streams[q].append(lambda e, fn=fn, sem=sem: fn(e).then_inc(sem, 16))
        self._mark((key, sem, val, "dma_" + q), reads, writes, (), waw)

    def barrier(self, engs=("pe", "act", "dve")):
        for e in engs:
            for o in engs:
                if o == e:
                    continue
                (key, sem), cnt = self.cur[o]
                if cnt > 0 and self.seen[e].get(key, 0) < cnt:
                    self.seen[e][key] = cnt
                    self.streams[e].append(lambda en, sem=sem, cnt=cnt: en.wait_ge(sem, cnt))

    def finish(self):
        for q in ("sp", "pool"):
            for (key, sem), cnt in self.dq[q]:
                if cnt > 0:
                    v = 16 * cnt
                    self.streams["sp"].append(lambda e, sem=sem, v=v: e.wait_ge(sem, v))

    def replay(self):
        with self.nc.Block() as block:
            @block.sync
            def _(e):
                for f in self.streams["sp"]:
                    f(e)

            @block.tensor
            def _(e):
                for f in self.streams["pe"]:
                    f(e)

            @block.scalar
            def _(e):
                for f in self.streams["act"]:
                    f(e)

            @block.vector
            def _(e):
                for f in self.streams["dve"]:
                    f(e)

            @block.gpsimd
            def _(e):
                for f in self.streams["pool"]:
                    f(e)


class Arena:
    def __init__(self, nc, es, nbytes, name="arena"):
        self.t = es.enter_context(nc.sbuf_tensor(name, [128, nbytes], U8))
        self.nbytes = nbytes
        self.off = 0
        self.marks = []
        self.peak = 0
        self.log = []

    def alloc(self, shape, dtype):
        esz = 2 if dtype == BF16 else 4
        n = int(np.prod(shape))
        nb = n * esz
        off = self.reserve(nb)
        return self.at(off, shape, dtype)

    def reserve(self, nb):
        off = (self.off + 63) // 64 * 64
        assert off + nb <= self.nbytes, ("arena overflow", off, nb, self.nbytes)
        self.off = off + nb
        self.peak = max(self.peak, self.off)
        return off

    def at(self, off, shape, dtype):
        self.log.append((off, list(shape), "bf16" if dtype == BF16 else "f32"))
        esz = 2 if dtype == BF16 else 4
        nb = int(np.prod(shape)) * esz
        ap = self.t[:, off:off + nb].bitcast(dtype)
        if len(shape) == 2:
            ap = ap.rearrange("p (a b) -> p a b", b=shape[1])
        elif len(shape) == 3:
            ap = ap.rearrange("p (a b c) -> p a b c", b=shape[1], c=shape[2])
        return ap

    def push(self):
        self.marks.append(self.off)

    def pop(self):
        self.off = self.marks.pop()


class WStream:
    def __init__(self, S, units):
        self.S = S
        self.units = units
        self.free = []
        self.nload = 0
        self.nuse = 0
        self.loaded = {}
        self.extra_reads = []
        self.limits = {}

    def add_slot(self, ap, reg, max_unit=None):
        if max_unit is not None:
            self.limits[id(reg)] = max_unit
        self.free.append((ap, reg))
        self.pump()

    def pump(self):
        while self.free and self.nload < len(self.units):
            pick = None
            for i, (ap_, reg_) in enumerate(self.free):
                if self.limits.get(id(reg_), 1 << 30) >= self.nload:
                    pick = i
                    break
            if pick is None:
                break
            ap, reg = self.free.pop(pick)
            for mk in self.units[self.nload]:
                o, i = mk(ap)
                self.S.dma("pool", lambda e, o=o, i=i: e.dma_start(out=o, in_=i), reads=self.extra_reads, writes=[reg], waw=False)
            self.loaded[self.nload] = (ap, reg)
            self.nload += 1

    def get(self):
        assert self.nuse in self.loaded, "weight unit not loaded (no free slot)"
        r = self.loaded.pop(self.nuse)
        self.nuse += 1
        return r

    def release(self, slot):
        self.free.append(slot)
        self.pump()


def MM(out, lhsT, rhs, start=True, stop=True):
    return lambda e: e.matmul(out, lhsT=lhsT, rhs=rhs, start=start, stop=stop)


def TR(out, in_, ident):
    return lambda e: e.transpose(out=out, in_=in_, identity=ident)


def ACTV(out, in_, func, bias=None, scale=None, accum=None):
    kw = {}
    if bias is not None:
        kw["bias"] = bias
    if scale is not None:
        kw["scale"] = scale
    if accum is not None:
        kw["accum_out"] = accum
    return lambda e: e.activation(out=out, in_=in_, func=func, **kw)


def TS(out, in0, s1, op0, s2=None, op1=None):
    if op1 is None:
        return lambda e: e.tensor_scalar(out=out, in0=in0, scalar1=s1, scalar2=None, op0=op0)
    return lambda e: e.tensor_scalar(out=out, in0=in0, scalar1=s1, scalar2=s2, op0=op0, op1=op1)


def TT(out, in0, in1, op):
    return lambda e: e.tensor_tensor(out=out, in0=in0, in1=in1, op=op)


def STT(out, in0, scalar, in1, op0, op1):
    return lambda e: e.scalar_tensor_tensor(out=out, in0=in0, scalar=scalar, in1=in1, op0=op0, op1=op1)


def SCAN(out, d0, d1):
    return lambda e: e.tensor_tensor_scan(out=out, data0=d0, data1=d1, initial=0.0, op0=ALU.mult, op1=ALU.add)


def RECIP(out, in_):
    return lambda e: e.reciprocal(out=out, in_=in_)


def MEMSET(ap, v):
    return lambda e: e.memset(ap, v)


def COPY(out, in_):
    return lambda e: e.tensor_copy(out=out, in_=in_)


def DMA(out, in_):
    return lambda e: e.dma_start(out=out, in_=in_)


def ASEL(out, in_, pattern, op, fill, base, cm):
    return lambda e: e.affine_select(out=out, in_=in_, pattern=pattern, compare_op=op, fill=fill, base=base, channel_multiplier=cm)


ARENA_BYTES = 204 * 1024
DEBUG = 0
RUN_P1 = RUN_P2 = RUN_P3 = RUN_P4 = RUN_P6 = True
P1_NIT = 32
P1_STAGE = 9

def build_nc():
    nc = bass.Bass("TRN2", target_bir_lowering=False)

    def din(name, shape):
        return nc.dram_tensor(name, shape, F32, kind="ExternalInput").ap()

    xw = din("xw", [2048, 2048])
    w_in = din("w_in", [2048, 11264])
    w_a = din("w_a", [1024, 2048])
    w_b = din("w_b", [1024, 2048])
    w_out = din("w_out", [2048, 2048])
    w_up = din("w_up", [2048, 8192])
    w_down = din("w_down", [8192, 2048])
    lbl_d = din("lbl", [128, 16])
    hgw_d = din("hgw", [128, 1])
    wcols_d = din("wcols", [128, 32])
    wfin_d = din("wfin", [128, 2048])
    wmlp_d = din("wmlp", [128, 2048])
    rbt_d = din("rbt", [128, 16, 384])
    crep_d = din("crep", [128, 16])
    hb_d = din("hbias", [128, 1])
    out_d = nc.dram_tensor("out", [1024, 2048], F32, kind="ExternalOutput").ap()

    def wcols_unit(w, nk, c0, ncols, k0, d0):
        def mk(slot):
            return (slot[:, k0:k0 + nk, d0:d0 + ncols],
                    w[0:nk * 128, c0:c0 + ncols].rearrange("(k p) c -> p k c", p=128))
        return mk

    def wdown_unit(fb):
        def mk(slot):
            return (slot.rearrange("p (k g) c -> p k g c", g=4),
                    w_down[fb * 512:(fb + 1) * 512, :].rearrange("(k p) (g c) -> p k g c", p=128, c=512))
        return mk

    units = []
    for h in range(8):
        units.append([wcols_unit(w_in, 16, s * 1024 + h * 128, 128, 0, s * 128) for s in range(4)])
    for j in range(8):
        units.append([wcols_unit(w_in, 16, 4096 + s * 1024 + j * 128, 128, 0, s * 128) for s in range(3)])
    for mg in range(4):
        units.append([wcols_unit(w_in, 16, 7168 + mg * 512, 512, 0, 0)])
        units.append([wcols_unit(w_in, 16, 9216 + mg * 512, 512, 0, 0)])
        units.append([wcols_unit(w_a, 8, mg * 512, 512, 0, 0), wcols_unit(w_b, 8, mg * 512, 512, 8, 0)])
    for cg in range(4):
        units.append([wcols_unit(w_out, 16, cg * 512, 512, 0, 0)])
    for fb in range(16):
        units.append([wcols_unit(w_up, 16, fb * 512, 512, 0, 0)])
        units.append([wdown_unit(fb)])

    with ExitStack() as es:
        S = Sched(nc, es)
        A = Arena(nc, es, ARENA_BYTES)
        banks = [es.enter_context(nc.psum_tensor("bank%d" % i, [128, 512], F32)) for i in range(8)]
        B = [b[:, :] for b in banks]
        BR = [S.reg("bank%d" % i) for i in range(8)]
        PH = S.reg("phase")
        R_dummy = S.reg("dummy")

        ident_f = A.alloc([128], F32)
        ident_b = A.alloc([128], BF16)
        ones_f = A.alloc([128], F32)
        maskA = A.alloc([512], F32)
        ones_b = A.alloc([128], BF16)
        rmask = A.alloc([512], F32)
        eps_c = A.alloc([1], F32)
        dummy = A.alloc([1], F32)
        lbl = A.alloc([16], F32)
        lbw = A.alloc([16], F32)
        lb = A.alloc([8], F32)
        oml = A.alloc([8], F32)
        hgw = A.alloc([1], F32)
        hgw_h = A.alloc([1], F32)
        a_col = A.alloc([8], F32)
        b_col = A.alloc([8], F32)
        lna_col = A.alloc([8], F32)
        wcols = A.alloc([32], F32)
        crep = A.alloc([16], F32)
        cbh = A.alloc([16], F32)
        hb = A.alloc([1], F32)
        CR = S.reg("consts")
        C2 = S.reg("consts2")
        CONS = [CR, C2]

        def phase_barrier(extra_reads=()):
            S.barrier()
            S.op("dve", MEMSET(dummy, 0.0), reads=list(extra_reads), writes=[PH, R_dummy])

        for dst, src in ((lbl, lbl_d), (hgw, hgw_d), (wcols, wcols_d), (crep, crep_d), (hb, hb_d)):
            S.dma("sp", DMA(dst, src), writes=[CR], waw=False)
        S.op("pool", MEMSET(ident_f, 0.0), writes=[C2])
        S.op("pool", ASEL(ident_f, ident_f, [[-1, 128]], ALU.not_equal, 1.0, 0, 1), reads=[C2], writes=[C2])
        S.op("pool", COPY(ident_b, ident_f), reads=[C2], writes=[C2])
        S.op("pool", MEMSET(ones_f, 1.0 / 128.0), writes=[C2])
        S.op("pool", MEMSET(eps_c, EPS), writes=[C2])
        S.op("pool", MEMSET(rmask, 1.0), writes=[C2])
        S.op("pool", MEMSET(rmask.rearrange("p (c t) -> p c t", t=64)[:, :, 0:1], 0.0), reads=[C2], writes=[C2])
        S.op("pool", MEMSET(maskA, 1.0), writes=[C2])
        mlo = maskA[0:64, :].rearrange("p (j t) -> p j t", t=128)
        mhi = maskA[64:128, :].rearrange("p (j t) -> p j t", t=128)
        S.op("pool", ASEL(mlo, mlo, [[0, 4], [1, 128]], ALU.is_ge, 0.0, 0, -1), reads=[C2], writes=[C2])
        S.op("pool", ASEL(mlo, mlo, [[0, 4], [-1, 128]], ALU.is_ge, 0.0, 63, 0), reads=[C2], writes=[C2])
        S.op("pool", ASEL(mhi, mhi, [[0, 4], [1, 128]], ALU.is_ge, 0.0, -64, -1), reads=[C2], writes=[C2])
        S.op("pool", MEMSET(ones_b, 1.0 / 128.0), writes=[C2])
        S.op("dve", TT(lbw[:, 0:8], lbl[:, 8:16], lbl[:, 0:8], ALU.subtract), reads=[CR], writes=[C2])
        S.op("act", ACTV(lbw[:, 8:16], lbw[:, 0:8], AF.Exp, scale=-1.0), reads=[C2], writes=[C2])
        S.op("act", ACTV(lbw[:, 0:8], lbw[:, 0:8], AF.Exp), reads=[C2], writes=[C2])
        S.op("dve", TS(lbw, lbw, 1.0, ALU.add), reads=[C2], writes=[C2])
        S.op("dve", RECIP(lb, lbw[:, 0:8]), reads=[C2], writes=[C2])
        S.op("dve", RECIP(oml, lbw[:, 8:16]), reads=[C2], writes=[C2])
        S.op("dve", TS(cbh, crep, hb[:, 0:1], ALU.add), reads=[CR], writes=[C2])
        S.op("dve", TS(a_col, oml, 0.5, ALU.mult), reads=[C2], writes=[C2])
        S.op("dve", TT(b_col, a_col, lb, ALU.add), reads=[C2], writes=[C2])
        S.op("act", ACTV(lna_col, a_col, AF.Ln), reads=[C2], writes=[C2])
        S.op("dve", TS(hgw_h, hgw, 0.5, ALU.mult), reads=[CR], writes=[C2])

        slots = [(A.alloc([16, 512], BF16), S.reg("slot%d" % i)) for i in range(3)]
        H_off = A.reserve(32768)
        uT_h0 = A.at(H_off, [16, 512], BF16)
        uT_h1 = A.at(H_off + 16384, [16, 512], BF16)
        uT_own = A.alloc([16, 1024], BF16)
        R_uh0, R_uh1, R_uo = S.reg("uh0"), S.reg("uh1"), S.reg("uo")
        ws = WStream(S, units)
        ws.add_slot(*slots[0])

        def ublk(blk):
            if blk == 0:
                return uT_h0, R_uh0, 0
            if blk == 1:
                return uT_h1, R_uh1, 0
            return uT_own, R_uo, (blk - 2) * 512

        def norm_transpose(n_tiles, load_tile, dst_of_group, wc0, xts, xregs, ssq, sd, rstd, junk, NR):
            ng = len(xts) // 4
            NRt = [S.reg("nrt%d" % i) for i in range(n_tiles)]
            RJ = S.reg("junk")

            def bufs(g):
                return xts[(g % ng) * 4:(g % ng) * 4 + 4], xregs[(g % ng) * 4:(g % ng) * 4 + 4]

            def stats(g):
                xg, xrg = bufs(g)
                for j in range(4):
                    i = g * 4 + j
                    xt, xr = xg[j], xrg[j]
                    src, sreg = load_tile(i, xt, xr)
                    S.op("act", ACTV(junk, src, AF.Square, accum=ssq[:, i:i + 1]), reads=[sreg], writes=[NRt[i], RJ])
                    S.op("act", ACTV(sd[:, i:i + 1], ssq[:, i:i + 1], AF.Sqrt, bias=eps_c[:, 0:1], scale=1.0 / 2048.0),
                         reads=[NRt[i], C2], writes=[NRt[i]])
                    S.op("dve", RECIP(rstd[:, i:i + 1], sd[:, i:i + 1]), reads=[NRt[i]], writes=[NRt[i]])
                    S.op("dve", TS(xt, src, rstd[:, i:i + 1], ALU.mult), reads=[NRt[i], sreg], writes=[xr])

            bi = 0
            ngroups = n_tiles // 4
            stats(0)
            for g in range(ngroups):
                if g + 1 < ngroups:
                    stats(g + 1)
                xg, xrg = bufs(g)
                dstT, dreg, tok0 = dst_of_group(g)
                for f in range(16):
                    bk = bi % 8
                    bi += 1
                    fns = [TR(B[bk][:, j * 128:(j + 1) * 128], xg[j][:, f * 128:(f + 1) * 128], ident_f) for j in range(4)]
                    S.group("pe", fns, reads=list(xrg) + CONS, excl=[BR[bk]])
                    sc = wcols[:, wc0 + f:wc0 + f + 1]
                    if f % 2 == 0:
                        S.op("act", ACTV(dstT[:, f, tok0:tok0 + 512], B[bk], AF.Copy, scale=sc),
                             reads=CONS, excl=[BR[bk]], writes=[dreg], waw=False)
                    else:
                        S.op("dve", TS(dstT[:, f, tok0:tok0 + 512], B[bk], sc, ALU.mult),
                             reads=CONS, excl=[BR[bk]], writes=[dreg], waw=False)

        A.push()
        xts = [A.alloc([2048], F32) for _ in range(8)]
        xregs = [S.reg("xt%d" % j) for j in range(8)]
        junk = A.alloc([2048], BF16)
        ssq = A.alloc([16], F32)
        sd = A.alloc([16], F32)
        rstd = A.alloc([16], F32)
        NR = S.reg("nr")

        def load0(i, xt, xr):
            S.dma("sp", DMA(xt, xw[i * 128:(i + 1) * 128, :]), writes=[xr])
            return xt, xr

        norm_transpose(16, load0, ublk, 0, xts, xregs, ssq, sd, rstd, junk, NR)
        ws.extra_reads = list(xregs)
        ws.add_slot(*slots[1])
        ws.add_slot(*slots[2])
        ws.extra_reads = []
        A.pop()
        phase_barrier()

        A.push()
        yaT = A.alloc([8, 1024], BF16)
        ybT = A.alloc([8, 1024], BF16)
        R_ya, R_yb = S.reg("ya"), S.reg("yb")
        A.push()
        f_sq = A.alloc([512], F32)
        f_ga = A.alloc([512], F32)
        f_sn = A.alloc([512], F32)
        f_lg = A.alloc([512], F32)
        f_bb = A.alloc([512], F32)
        f_eb = A.alloc([512], F32)
        f_sg = A.alloc([512], F32)
        f_oT = A.alloc([512], F32)
        f_os = A.alloc([512], F32)
        b_vT = A.alloc([512], BF16)
        b_kT = A.alloc([512], BF16)
        b_qT = A.alloc([512], BF16)
        b_kh = A.alloc([512], BF16)
        b_tok = A.alloc([8, 128], BF16)
        b_As = A.alloc([512], BF16)
        b_os = A.alloc([512], BF16)
        Sring = [A.alloc([9, 128], F32) for _ in range(2)]
        Sb = [A.alloc([9, 128], BF16) for _ in range(2)]
        R = {n: S.reg(n) for n in ["sq", "ga", "sn", "lg", "bb", "eb", "sg", "oT", "os", "vT", "kT", "qT", "kh", "tok", "As", "S",
                                   "Sb0", "Sb1"]}
        RSb = [R["Sb0"], R["Sb1"]]
        BTb = B[3].bitcast(BF16)
        QS = 128.0 ** -0.5
        its = [(h, blk) for h in range(8) for blk in range(4)][:P1_NIT]
        hslot = {}
        f_sg2 = [f_sg, A.alloc([512], F32), A.alloc([512], F32)]
        R["sg0"], R["sg1"], R["sg2"] = S.reg("sg0"), S.reg("sg1"), S.reg("sg2")
        b_qT2 = [b_qT, A.alloc([512], BF16)]
        R["qT0"], R["qT1"] = S.reg("qT0"), S.reg("qT1")

        f_eb2 = [f_eb, A.alloc([512], F32)]
        R["eb0"], R["eb1"] = S.reg("eb0"), S.reg("eb1")

        def slot_of(n):
            h, blk = its[n]
            if h not in hslot:
                hslot[h] = ws.get()
            return hslot[h]

        def pe_proj(n, seg, bk):
            h, blk = its[n]
            sl, sreg = slot_of(n)
            uT, ureg, t0 = ublk(blk)
            fns = [MM(B[bk], sl[:, k, seg * 128:(seg + 1) * 128], uT[:, k, t0:t0 + 512], k == 0, k == 15) for k in range(16)]
            S.group("pe", fns, reads=[sreg, ureg], excl=[BR[bk]])

        def own_(n):
            return 0 <= n < len(its) and its[n][1] >= 2

        def s_hq(n):
            pe_proj(n, 0, 0)
            S.op("act", ACTV(f_sq, B[0], AF.Tanh, scale=0.5), excl=[BR[0]], writes=[R["sq"]])
            S.op("dve", STT(f_sq, f_sq, 1.0, B[0], ALU.add, ALU.mult), reads=[R["sq"]], excl=[BR[0]], writes=[R["sq"]])

        def s_hf(n, mid=None):
            h = its[n][0]
            if mid is None:
                pe_proj(n, 1, 1)
            else:
                hh_, blk_ = its[n]
                sl, sreg = slot_of(n)
                uT, ureg, t0 = ublk(blk_)
                mk = lambda k: MM(B[1], sl[:, k, 128:256], uT[:, k, t0:t0 + 512], k == 0, k == 15)
                S.group("pe", [mk(k) for k in range(8)], reads=[sreg, ureg], excl=[BR[1]])
                mid()
                S.group("pe", [mk(k) for k in range(8, 16)], reads=[sreg, ureg], excl=[BR[1]])
            S.op("act", ACTV(f_ga, B[1], AF.Tanh, scale=0.5), excl=[BR[1]], writes=[R["ga"]])
            S.op("act", ACTV(f_sn, B[1], AF.Tanh, scale=-0.5), excl=[BR[1]], writes=[R["sn"]])
            S.op("dve", TS(f_ga, f_ga, a_col[:, h:h + 1], ALU.mult, b_col[:, h:h + 1], ALU.add), reads=[R["ga"]] + CONS, writes=[R["ga"]])

        def s_ln(n):
            S.op("act", ACTV(f_lg, f_ga, AF.Ln), reads=[R["ga"]], writes=[R["lg"]])

        def s_scan(n):
            S.op("dve", SCAN(f_bb, rmask, f_lg), reads=[R["lg"]] + CONS, writes=[R["bb"]])

        def s_exps(n):
            h = its[n][0]
            eb, reb = f_eb2[n % 2], R["eb%d" % (n % 2)]
            S.op("act", ACTV(eb, f_bb, AF.Exp), reads=[R["bb"]], writes=[reb])
            S.op("act", ACTV(f_lg, f_bb, AF.Exp, scale=-1.0, bias=lna_col[:, h:h + 1]), reads=[R["bb"]] + CONS, writes=[R["lg"]])

        def s_kq(n):
            h = its[n][0]
            eb, reb = f_eb2[n % 2], R["eb%d" % (n % 2)]
            S.op("dve", STT(b_kT, f_sn, 1.0, f_lg, ALU.add, ALU.mult), reads=[R["sn"], R["lg"]] + CONS, writes=[R["kT"]])
            if own_(n):
                S.op("dve", STT(b_qT2[n % 2], f_sq, 0.5 * QS, eb, ALU.mult, ALU.mult), reads=[R["sq"], reb], writes=[R["qT%d" % (n % 2)]])
            eb3 = eb.rearrange("p (c t) -> p c t", t=64)
            S.op("dve", TT(b_kh.rearrange("p (c t) -> p c t", t=64), b_kT.rearrange("p (c t) -> p c t", t=64),
                           eb3[:, :, 63:64].broadcast_to([128, 8, 64]), ALU.mult), reads=[R["kT"], reb], writes=[R["kh"]])

        def s_hg(n):
            pe_proj(n, 3, 4)
            sg, rsg = f_sg2[n % 3], R["sg%d" % (n % 3)]
            S.op("act", ACTV(sg, B[4], AF.Tanh, scale=0.5), excl=[BR[4]], writes=[rsg])

        def s_hg_b(n):
            sg, rsg = f_sg2[n % 3], R["sg%d" % (n % 3)]
            S.op("dve", STT(sg, sg, 1.0, B[4], ALU.add, ALU.mult), reads=[rsg], excl=[BR[4]], writes=[rsg])

        def s_hi(n):
            pe_proj(n, 2, 2)
            S.op("act", ACTV(b_vT, B[2], AF.Copy), excl=[BR[2]], writes=[R["vT"]])

        def s_tr(n):
            fns = [TR(BTb[:, j * 128:(j + 1) * 128], b_kh[:, j * 128:(j + 1) * 128], ident_b) for j in range(4)]
            fns += [TR(BTb[:, (4 + j) * 128:(5 + j) * 128], b_vT[:, j * 128:(j + 1) * 128], ident_b) for j in range(4)]
            S.group("pe", fns, reads=[R["kh"], R["vT"]] + CONS, excl=[BR[3]])
            S.op("act", ACTV(b_tok, BTb.rearrange("p (a b) -> p a b", b=128), AF.Copy), excl=[BR[3]], writes=[R["tok"]])

        def s_ms_a(n):
            S.op("pe", MM(B[7], ones_b, b_os), reads=[R["os"]] + CONS, excl=[BR[7]])
            S.op("act", ACTV(f_os, B[7], AF.Ln, bias=eps_c[:, 0:1]), reads=CONS, excl=[BR[7]], writes=[R["os"]])
            S.op("act", ACTV(f_os, f_os, AF.Exp, scale=-0.5), reads=[R["os"]], writes=[R["os"]])

        def s_ms_b(n):
            h, blk = its[n]
            S.op("dve", TT(f_oT, f_oT, f_os, ALU.mult), reads=[R["oT"], R["os"]], writes=[R["oT"]])
            t0 = (blk - 2) * 512
            S.op("dve", STT(yaT[:, h, t0:t0 + 512], f_oT, hgw_h[:, 0:1], f_sg2[n % 3], ALU.mult, ALU.mult),
                 reads=[R["oT"], R["sg%d" % (n % 3)]] + CONS, writes=[R_ya], waw=False)

        def s_u(n):
            fns = []
            for c in range(8):
                j, hh = c // 2, c % 2
                rows = slice(hh * 64, hh * 64 + 64)
                fns.append(MM(B[5 + hh][:, j * 128:(j + 1) * 128], b_tok[rows, j, :], b_tok[rows, 4 + j, :]))
            S.group("pe", fns, reads=[R["tok"]], excl=[BR[5], BR[6]])
            if own_(n):
                q = b_qT2[n % 2]
                fns = [MM(B[4][:, j * 128:(j + 1) * 128], b_kT[:, j * 128:(j + 1) * 128], q[:, j * 128:(j + 1) * 128]) for j in range(4)]
                S.group("pe", fns, reads=[R["kT"], R["qT%d" % (n % 2)]], excl=[BR[4]])
                S.op("dve", TT(b_As, B[4], maskA, ALU.mult), reads=CONS, excl=[BR[4]], writes=[R["As"]])

        def s_chain(n, c0, c1):
            par = its[n][1] % 2
            eb, reb = f_eb2[n % 2], R["eb%d" % (n % 2)]
            for c in range(c0, c1):
                ub = B[5 + c % 2][:, (c // 2) * 128:(c // 2 + 1) * 128]
                sin = Sring[1 - par][:, 8, :] if c == 0 else Sring[par][:, c, :]
                S.op("dve", STT(Sring[par][:, c + 1, :], sin, eb[:, c * 64 + 63:c * 64 + 64], ub, ALU.mult, ALU.add),
                     reads=[R["S"], reb], writes=[R["S"]], excl=[BR[5 + c % 2]])

        def s_cast(n):
            h, blk = its[n]
            par = blk % 2
            if blk >= 2:
                S.op("act", ACTV(Sb[par][:, 1:9, :], Sring[par][:, 1:9, :], AF.Copy), reads=[R["S"]], writes=[RSb[par]], waw=False)
            elif blk == 1:
                S.op("act", ACTV(Sb[par][:, 8, :], Sring[par][:, 8, :], AF.Copy), reads=[R["S"]], writes=[RSb[par]], waw=False)

        def s_o(n):
            par = its[n][1] % 2
            q = b_qT2[n % 2]
            fns = []
            for j in range(4):
                oc = B[7][:, j * 128:(j + 1) * 128]
                fns.append(MM(oc, b_tok[:, 4 + j, :], b_As[:, j * 128:(j + 1) * 128], True, False))
                for c in (2 * j, 2 * j + 1):
                    sprev = Sb[1 - par][:, 8, :] if c == 0 else Sb[par][:, c, :]
                    fns.append(MM(B[7][:, c * 64:(c + 1) * 64], sprev, q[:, c * 64:(c + 1) * 64], False, c == 2 * j + 1))
            S.group("pe", fns, reads=[RSb[0], RSb[1], R["qT%d" % (n % 2)], R["tok"], R["As"]], excl=[BR[7]])
            S.op("act", ACTV(f_oT, B[7], AF.Copy), excl=[BR[7]], writes=[R["oT"]])
            S.op("act", ACTV(b_os, B[7], AF.Square), excl=[BR[7]], writes=[R["os"]])

        if RUN_P1:
            NI = len(its)

            def load_first_half(nx):
                if own_(nx):
                    s_hq(nx)
                s_hf(nx)

            load_first_half(0)
            s_ln(0)
            s_scan(0)
            if own_(0):
                s_hg(0)
            s_hi(0)
            s_exps(0)
            s_kq(0)
            if own_(0):
                s_hg_b(0)
            for n in range(NI):
                h, blk = its[n]
                nx = n + 1 if n + 1 < NI else None
                if blk == 0:
                    S.op("dve", MEMSET(Sring[1][:, 8, :], 0.0), writes=[R["S"]])
                if nx is not None:
                    if own_(nx):
                        s_hq(nx)
                    s_hf(nx, mid=lambda n=n: s_tr(n))
                else:
                    s_tr(n)
                if own_(n - 1):
                    s_ms_a(n - 1)
                if nx is not None:
                    s_ln(nx)
                s_u(n)
                s_chain(n, 0, 8)
                if own_(n - 1):
                    s_ms_b(n - 1)
                if nx is not None:
                    s_scan(nx)
                    if own_(nx):
                        s_hg(nx)
                s_cast(n)
                if nx is not None:
                    if own_(nx):
                        s_hg_b(nx)
                    s_hi(nx)
                    s_exps(nx)
                    s_kq(nx)
                if own_(n):
                    s_o(n)
                if blk == 0 and h >= 1:
                    ws.release(hslot[h - 1])
            if own_(NI - 1):
                s_ms_a(NI - 1)
                s_ms_b(NI - 1)
            ws.release(hslot[its[NI - 1][0]])
        A.pop()
        phase_barrier()
        if DEBUG == 1:
            dbg = nc.dram_tensor("dbg", [128, 8, 1024], BF16, kind="ExternalOutput").ap()
            S.dma("sp", DMA(dbg, yaT), reads=[R_ya, PH])

        A.push()
        rbt = A.alloc([16, 384], F32)
        aqT = A.alloc([1024], BF16)
        akT = A.alloc([1536], BF16)
        Vext = A.alloc([12, 2, 65], BF16)
        tmpS = [A.alloc([384], F32) for _ in range(3)]
        PT = [A.alloc([640], BF16) for _ in range(3)]
        ybt = A.alloc([8, 128], BF16)
        rcp = A.alloc([16], F32)
        avT = A.alloc([1536], BF16)
        R2 = {n: S.reg(n) for n in ["rbt", "aq", "ak", "V", "tmp0", "tmp1", "tmp2", "PT0", "PT1", "PT2", "ybt", "rcp", "avT"]}
        S.dma("sp", DMA(rbt, rbt_d), reads=[PH], writes=[R2["rbt"]])
        S.op("dve", MEMSET(Vext, 1.0), reads=[PH], writes=[R2["V"]])
        for i in range(3):
            S.op("dve", MEMSET(PT[i], 0.0), reads=[PH], writes=[R2["PT%d" % i]])
        BT6 = B[6].bitcast(BF16)
        SC = 64.0 ** -0.5
        unit_i = 0
        for j in range(8 if RUN_P2 else 0):
            sl, sreg = ws.get()
            pb = 0
            for name, dst, nblk, b0, seg in (("aq", aqT, 2, 2, 0), ("ak", akT, 3, 1, 1)):
                for bi_ in range(nblk):
                    uT, ureg, t0 = ublk(b0 + bi_)
                    bk = 6 + (pb % 2)
                    pb += 1
                    fns = [MM(B[bk], sl[:, k, seg * 128:(seg + 1) * 128], uT[:, k, t0:t0 + 512], k == 0, k == 15) for k in range(16)]
                    S.group("pe", fns, reads=[sreg, ureg], excl=[BR[bk]])
                    if pb % 2:
                        S.op("act", ACTV(dst[:, bi_ * 512:(bi_ + 1) * 512], B[bk], AF.Copy), excl=[BR[bk]], writes=[R2[name]], waw=False)
                    else:
                        S.op("dve", COPY(dst[:, bi_ * 512:(bi_ + 1) * 512], B[bk]), excl=[BR[bk]], writes=[R2[name]], waw=False)
            for bi_ in range(3):
                uT, ureg, t0 = ublk(1 + bi_)
                bk = 6 + (pb % 2)
                pb += 1
                fns = [MM(B[bk], sl[:, k, 256:384], uT[:, k, t0:t0 + 512], k == 0, k == 15) for k in range(16)]
                S.group("pe", fns, reads=[sreg, ureg], excl=[BR[bk]])
                if pb % 2:
                    S.op("act", ACTV(avT[:, bi_ * 512:(bi_ + 1) * 512], B[bk], AF.Copy), excl=[BR[bk]], writes=[R2["avT"]], waw=False)
                else:
                    S.op("dve", COPY(avT[:, bi_ * 512:(bi_ + 1) * 512], B[bk]), excl=[BR[bk]], writes=[R2["avT"]], waw=False)
            for (w0, nw) in ((0, 8), (8, 4)):
                bk = 6 + (pb % 2)
                pb += 1
                bt = B[bk].bitcast(BF16)
                fns = [TR(bt[:, i * 128:(i + 1) * 128], avT[:, (w0 + i) * 128:(w0 + i + 1) * 128], ident_b) for i in range(nw)]
                S.group("pe", fns, reads=[R2["avT"]] + CONS, excl=[BR[bk]])
                S.op("act", ACTV(Vext[:, w0:w0 + nw, :, 0:64], bt[:, 0:nw * 128].rearrange("p (a b c) -> p a b c", b=2, c=64), AF.Copy),
                     excl=[BR[bk]], writes=[R2["V"]], waw=False)
            ws.release((sl, sreg))
            def stage_a(qt, hh, u):
                head = 2 * j + hh
                rows = slice(hh * 64, hh * 64 + 64)
                bsA, bsB = 2 * u, 2 * u + 1
                fns = []
                for o in range(5):
                    kt = qt + 4 - o
                    ob_ = B[bsA][:, o * 128:(o + 1) * 128] if o < 3 else B[bsB][:, (o - 3) * 128:(o - 2) * 128]
                    fns.append(MM(ob_, akT[rows, kt * 128:(kt + 1) * 128], aqT[rows, qt * 128:(qt + 1) * 128]))
                S.group("pe", fns, reads=[R2["aq"], R2["ak"]], excl=[BR[bsA], BR[bsB]])
                tr, pr = R2["tmp%d" % u], R2["PT%d" % u]
                S.op("dve", STT(tmpS[u], B[bsA][:, 0:384], SC, rbt[:, head, :], ALU.mult, ALU.add),
                     reads=[R2["rbt"]], excl=[BR[bsA]], writes=[tr])
                o = 0
                while o < 3:
                    hist = (qt + 4 - o) < 4
                    o2 = o
                    while o2 + 1 < 3 and ((qt + 4 - (o2 + 1)) < 4) == hist:
                        o2 += 1
                    c0, c1 = o * 128, (o2 + 1) * 128
                    if hist:
                        S.op("act", ACTV(PT[u][:, c0:c1], tmpS[u][:, c0:c1], AF.Exp, bias=hb[:, 0:1]),
                             reads=[tr] + CONS, writes=[pr], waw=False)
                    else:
                        S.op("act", ACTV(PT[u][:, c0:c1], tmpS[u][:, c0:c1], AF.Exp), reads=[tr], writes=[pr], waw=False)
                    o = o2 + 1
                h3, h4 = (qt + 1) < 4, qt < 4
                if h3 == h4:
                    bias = (cbh if h3 else crep)[:, head:head + 1]
                    S.op("act", ACTV(PT[u][:, 384:640], B[bsB][:, 0:256], AF.Exp, bias=bias, scale=SC),
                         reads=CONS, excl=[BR[bsB]], writes=[pr], waw=False)
                else:
                    for o in (3, 4):
                        hist = (qt + 4 - o) < 4
                        bias = (cbh if hist else crep)[:, head:head + 1]
                        S.op("act", ACTV(PT[u][:, o * 128:(o + 1) * 128], B[bsB][:, (o - 3) * 128:(o - 2) * 128], AF.Exp, bias=bias, scale=SC),
                             reads=CONS, excl=[BR[bsB]], writes=[pr], waw=False)
                S.op("pool", MEMSET(PT[u][0:64, 4 * 128 + 64:5 * 128], 0.0), writes=[pr])

            def stage_b(qt, hh, u, ob):
                pr = R2["PT%d" % u]
                fns = []
                for o in range(5):
                    kt = qt + 4 - o
                    fns.append(MM(B[ob][:, 0:65], PT[u][:, o * 128:(o + 1) * 128], Vext[:, kt, hh, :], o == 0, o == 4))
                S.group("pe", fns, reads=[pr, R2["V"]], excl=[BR[ob]])
                rc = rcp[:, (ob - 6):(ob - 5)]
                S.op("dve", RECIP(rc, B[ob][:, 64:65]), excl=[BR[ob]], writes=[R2["rcp"]])
                S.op("dve", TS(ybt[:, qt, hh * 64:(hh + 1) * 64], B[ob][:, 0:64], rc, ALU.mult),
                     reads=[R2["rcp"]], excl=[BR[ob]], writes=[R2["ybt"]], waw=False)

            ulist = [(qt, hh) for qt in range(8) for hh in range(2)]
            NU = len(ulist)
            for n_ in range(min(2, NU)):
                stage_a(ulist[n_][0], ulist[n_][1], n_ % 3)
            for n_, (qt, hh) in enumerate(ulist):
                if n_ + 2 < NU:
                    stage_a(ulist[n_ + 2][0], ulist[n_ + 2][1], (n_ + 2) % 3)
                stage_b(qt, hh, n_ % 3, 6 + n_ % 2)
            fns = [TR(BT6[:, qt * 128:(qt + 1) * 128], ybt[:, qt, :], ident_b) for qt in range(8)]
            S.group("pe", fns, reads=[R2["ybt"]] + CONS, excl=[BR[6]])
            S.op("act", ACTV(ybT[:, j, :], BT6, AF.Copy), excl=[BR[6]], writes=[R_yb], waw=False)
        A.pop()
        phase_barrier()
        if DEBUG == 2:
            dbg = nc.dram_tensor("dbg", [128, 8, 1024], BF16, kind="ExternalOutput").ap()
            S.dma("sp", DMA(dbg, ybT), reads=[R_yb, PH])

        mergedT = A.at(H_off, [16, 1024], BF16)
        R_mg = S.reg("merged")
        A.push()
        g_a = [A.alloc([512], F32) for _ in range(2)]
        g_b = [A.alloc([512], F32) for _ in range(2)]
        R3 = {n: S.reg(n) for n in ["ga0", "ga1", "gb0", "gb1"]}
        if RUN_P3:
            for i in range(2):
                xs_ap, xs_reg = A.alloc([16, 512], BF16), S.reg("xslot%d" % i)
                S.op("dve", MEMSET(dummy, 0.0), reads=[PH, xs_reg], writes=[R_dummy])
                ws.add_slot(xs_ap, xs_reg, max_unit=16 + 12 - 1)
        it3 = 0
        for mg in range(4 if RUN_P3 else 0):
            sx = ws.get()
            sy = ws.get()
            sz = ws.get()
            for mi in range(4):
                m = mg * 4 + mi
                cs = slice(mi * 128, (mi + 1) * 128)
                for t2 in range(2):
                    u = it3 % 2
                    it3 += 1
                    bb = 4 * u
                    ts_ = slice(t2 * 512, (t2 + 1) * 512)
                    S.group("pe", [MM(B[bb + 0], sx[0][:, k, cs], uT_own[:, k, ts_], k == 0, k == 15) for k in range(16)],
                            reads=[sx[1], R_uo], excl=[BR[bb + 0]])
                    S.group("pe", [MM(B[bb + 1], sy[0][:, k, cs], uT_own[:, k, ts_], k == 0, k == 15) for k in range(16)],
                            reads=[sy[1], R_uo], excl=[BR[bb + 1]])
                    S.group("pe", [MM(B[bb + 2], sz[0][:, k, cs], yaT[:, k, ts_], k == 0, k == 7) for k in range(8)],
                            reads=[sz[1], R_ya], excl=[BR[bb + 2]])
                    S.group("pe", [MM(B[bb + 3], sz[0][:, 8 + k, cs], ybT[:, k, ts_], k == 0, k == 7) for k in range(8)],
                            reads=[sz[1], R_yb], excl=[BR[bb + 3]])
                    ra, rb_ = R3["ga%d" % u], R3["gb%d" % u]
                    S.op("act", ACTV(g_a[u], B[bb + 0], AF.Sigmoid), excl=[BR[bb + 0]], writes=[ra])
                    S.op("act", ACTV(g_b[u], B[bb + 1], AF.Sigmoid), excl=[BR[bb + 1]], writes=[rb_])
                    S.op("dve", TT(g_a[u], g_a[u], B[bb + 2], ALU.mult), reads=[ra], excl=[BR[bb + 2]], writes=[ra])
                    S.op("dve", TT(g_b[u], g_b[u], B[bb + 3], ALU.mult), reads=[rb_], excl=[BR[bb + 3]], writes=[rb_])
                    S.op("dve", TT(mergedT[:, m, ts_], g_a[u], g_b[u], ALU.add), reads=[ra, rb_, PH], writes=[R_mg], waw=False)
            ws.release(sx)
            ws.release(sy)
            ws.release(sz)
        A.pop()
        A.pop()
        phase_barrier()
        if DEBUG == 3:
            dbg = nc.dram_tensor("dbg", [128, 16, 1024], BF16, kind="ExternalOutput").ap()
            S.dma("sp", DMA(dbg, mergedT), reads=[R_mg, PH])

        hres = A.alloc([8, 2048], F32)
        R_h = [S.reg("h%d" % t) for t in range(8)]
        u2T = uT_own
        R_u2 = S.reg("u2T")
        A.push()
        hnb = [A.alloc([2048], BF16) for _ in range(2)]
        R_hn = [S.reg("hnb%d" % i) for i in range(2)]
        wrep_mlp = A.alloc([2048], F32)
        ssq5 = A.alloc([8], F32)
        sd5 = A.alloc([8], F32)
        rstd5 = A.alloc([8], F32)
        NR5 = S.reg("nr5")
        R_wm = S.reg("wrep_mlp")
        S.dma("sp", DMA(wrep_mlp, wmlp_d), reads=[PH], writes=[R_wm])
        S.op("dve", MEMSET(ssq5, 0.0), reads=[PH], writes=[NR5] + R_hn)
        for tt in range(8):
            S.dma("sp", DMA(hres[:, tt, :], xw[1024 + tt * 128:1024 + (tt + 1) * 128, :]), reads=[PH], writes=[R_h[tt]])
        BT6b, BT7b = B[6].bitcast(BF16), B[7].bitcast(BF16)

        def emit_norm_stats(tt):
            u = tt % 2
            hn, rhn = hnb[u], R_hn[u]
            S.op("act", ACTV(hn, hres[:, tt, :], AF.Square, accum=ssq5[:, tt:tt + 1]), reads=[R_h[tt]], writes=[NR5, rhn])
            S.op("act", ACTV(sd5[:, tt:tt + 1], ssq5[:, tt:tt + 1], AF.Sqrt, bias=eps_c[:, 0:1], scale=1.0 / 2048.0),
                 reads=[NR5] + CONS, writes=[NR5])
            S.op("dve", RECIP(rstd5[:, tt:tt + 1], sd5[:, tt:tt + 1]), reads=[NR5], writes=[NR5])
            S.op("dve", STT(hn, hres[:, tt, :], rstd5[:, tt:tt + 1], wrep_mlp, ALU.mult, ALU.mult),
                 reads=[R_h[tt], NR5, R_wm], writes=[rhn])

        def emit_norm_trans(tt):
            u = tt % 2
            hn, rhn = hnb[u], R_hn[u]
            for half, bt, bk in ((0, BT6b, 6), (1, BT7b, 7)):
                fns = [TR(bt[:, i * 128:(i + 1) * 128], hn[:, (half * 8 + i) * 128:(half * 8 + i + 1) * 128], ident_b) for i in range(8)]
                S.group("pe", fns, reads=[rhn] + CONS, excl=[BR[bk]])
                dst = u2T[:, half * 8:(half + 1) * 8, tt * 128:(tt + 1) * 128]
                srcv = bt.rearrange("p (a b) -> p a b", b=128)
                if half == 0:
                    S.op("act", ACTV(dst, srcv, AF.Copy), excl=[BR[bk]], writes=[R_u2], waw=False)
                else:
                    S.op("dve", COPY(dst, srcv), excl=[BR[bk]], writes=[R_u2], waw=False)

        it4 = 0
        for cg in range(4 if RUN_P4 else 0):
            sl, sreg = ws.get()
            for tt in range(8):
                bk = it4 % 6
                it4 += 1
                S.group("pe", [MM(B[bk], mergedT[:, k, tt * 128:(tt + 1) * 128], sl[:, k, :], k == 0, k == 15) for k in range(16)],
                        reads=[sreg, R_mg], excl=[BR[bk]])
                hv = hres[:, tt, cg * 512:(cg + 1) * 512]
                S.op("dve", TT(hv, hv, B[bk], ALU.add), reads=[R_h[tt]], excl=[BR[bk]], writes=[R_h[tt]])
                if cg == 3 and tt >= 1:
                    emit_norm_stats(tt - 1)
                if cg == 3 and tt >= 2:
                    emit_norm_trans(tt - 2)
            ws.release((sl, sreg))
        if RUN_P4:
            emit_norm_stats(7)
            emit_norm_trans(6)
            emit_norm_trans(7)
        else:
            for tt in range(8):
                emit_norm_stats(tt)
                emit_norm_trans(tt)
        A.pop()
        phase_barrier([s[1] for s in slots])

        A.push()
        slot4 = (A.at(H_off, [16, 512], BF16), S.reg("slot3"))
        hid = [A.at(H_off + 16384 + i * 8192, [4, 1024], BF16) for i in range(2)]
        rr = [A.alloc([512], F32) for _ in range(2)]
        R6 = {n: S.reg(n) for n in ["hid0", "hid1", "rr0", "rr1"]}
        S.op("dve", MEMSET(dummy, 0.0), reads=[PH, slot4[1]], writes=[R6["hid0"], R6["hid1"], R_dummy])
        ws.add_slot(*slot4)
        upc = [0]

        def emit_up_unit(fb, un, sa):
            fi, half = un // 2, un % 2
            par = fb % 2
            bk = upc[0] % 4
            u = upc[0] % 2
            upc[0] += 1
            S.group("pe", [MM(B[bk], sa[0][:, k, fi * 128:(fi + 1) * 128], u2T[:, k, half * 512:(half + 1) * 512], k == 0, k == 15)
                           for k in range(16)], reads=[sa[1], R_u2], excl=[BR[bk]])
            S.op("act", ACTV(rr[u], B[bk], AF.Relu), excl=[BR[bk]], writes=[R6["rr%d" % u]])
            S.op("dve", TT(hid[par][:, fi, half * 512:(half + 1) * 512], rr[u], rr[u], ALU.mult),
                 reads=[R6["rr%d" % u]], writes=[R6["hid%d" % par]], waw=False)

        def emit_down_tt(fb, tt, sb):
            par = fb % 2
            bb = 0 if (fb == 15 and tt % 2 == 0) else 4
            fns = []
            for k in range(4):
                for cg in range(4):
                    fns.append(MM(B[bb + cg], hid[par][:, k, tt * 128:(tt + 1) * 128], sb[0][:, k * 4 + cg, :], k == 0, k == 3))
            S.group("pe", fns, reads=[sb[1], R6["hid%d" % par]], excl=[BR[bb], BR[bb + 1], BR[bb + 2], BR[bb + 3]])
            for cg in range(4):
                hv = hres[:, tt, cg * 512:(cg + 1) * 512]
                S.op("dve", TT(hv, hv, B[bb + cg], ALU.add), reads=[R_h[tt]], excl=[BR[bb + cg]], writes=[R_h[tt]])

        wrep = A.alloc([2048], F32)
        junk = A.alloc([2048], BF16)
        ssq = A.alloc([8], F32)
        sd = A.alloc([8], F32)
        rstd = A.alloc([8], F32)
        R7 = {n: S.reg(n) for n in ["wrep", "nr"]}
        S.dma("sp", DMA(wrep, wfin_d), reads=[PH], writes=[R7["wrep"]])
        S.op("dve", MEMSET(ssq, 0.0), reads=[PH], writes=[R7["nr"]])

        def emit_final(tt):
            S.op("act", ACTV(junk, hres[:, tt, :], AF.Square, accum=ssq[:, tt:tt + 1]), reads=[R_h[tt], PH], writes=[R7["nr"]])
            S.op("act", ACTV(sd[:, tt:tt + 1], ssq[:, tt:tt + 1], AF.Sqrt, bias=eps_c[:, 0:1], scale=1.0 / 2048.0),
                 reads=[R7["nr"]] + CONS, writes=[R7["nr"]])
            S.op("dve", RECIP(rstd[:, tt:tt + 1], sd[:, tt:tt + 1]), reads=[R7["nr"]], writes=[R7["nr"]])
            S.op("dve", STT(hres[:, tt, :], hres[:, tt, :], rstd[:, tt:tt + 1], wrep, ALU.mult, ALU.mult),
                 reads=[R_h[tt], R7["nr"], R7["wrep"]], writes=[R_h[tt]])
            S.dma("sp", DMA(out_d[tt * 128:(tt + 1) * 128, :], hres[:, tt, :]), reads=[R_h[tt]])

        if RUN_P6:
            sa = ws.get()
            for un in range(8):
                emit_up_unit(0, un, sa)
            ws.release(sa)
            for fb in range(16):
                sb = ws.get()
                sa = ws.get() if fb + 1 < 16 else None
                for tt in range(8):
                    emit_down_tt(fb, tt, sb)
                    if sa is not None:
                        emit_up_unit(fb + 1, tt, sa)
                    else:
                        emit_final(tt)
                ws.release(sb)
                if sa is not None:
                    ws.release(sa)
        else:
            for tt in range(8):
                emit_final(tt)
        A.pop()
        S.finish()
        S.replay()
        build_nc.peak = A.peak
        build_nc.log = A.log
    return nc


def _host_prep(inputs):
    x = np.asarray(inputs["x"], dtype=np.float32)
    rb = np.asarray(inputs["rel_bias"], dtype=np.float32)[0]
    lbl = np.ascontiguousarray(np.asarray(inputs["lb_logits"], np.float32).reshape(2, 8, 128).transpose(2, 0, 1).reshape(128, 16))
    hgw = np.ascontiguousarray(np.asarray(inputs["hg_norm_w"], np.float32)[0].reshape(128, 1))
    wc = np.concatenate([np.asarray(inputs["norm_mix_w"], np.float32)[0].reshape(16, 128).T,
                         np.asarray(inputs["norm_mlp_w"], np.float32)[0].reshape(16, 128).T], axis=1)
    wc = np.ascontiguousarray(wc)
    wfin = np.ascontiguousarray(np.broadcast_to(np.asarray(inputs["norm_final_w"], np.float32).reshape(1, 2048), (128, 2048)))
    wmlp = np.ascontiguousarray(np.broadcast_to(np.asarray(inputs["norm_mlp_w"], np.float32)[0].reshape(1, 2048), (128, 2048)))
    k = np.arange(128)[:, None, None]
    o = np.arange(3)[None, :, None]
    t = np.arange(128)[None, None, :]
    idx = np.clip(128 * o + t - k, -256, 256) + 256
    rbt = rb[:, idx]
    invalid = np.broadcast_to((o == 0) & (k >= 64) & (t < 64), idx.shape)
    rbt = np.where(invalid[None], np.float32(NEG), rbt)
    rbt = np.ascontiguousarray(rbt.transpose(1, 0, 2, 3).reshape(128, 16, 384)).astype(np.float32)
    crep = np.ascontiguousarray(np.broadcast_to(rb[:, 512][None, :], (128, 16))).astype(np.float32)
    shared = {
        "w_in": np.ascontiguousarray(np.asarray(inputs["w_in"], np.float32)[0]),
        "w_a": np.ascontiguousarray(np.asarray(inputs["w_branch_a"], np.float32)[0]),
        "w_b": np.ascontiguousarray(np.asarray(inputs["w_branch_b"], np.float32)[0]),
        "w_out": np.ascontiguousarray(np.asarray(inputs["w_out"], np.float32)[0]),
        "w_up": np.ascontiguousarray(np.asarray(inputs["w_up"], np.float32)[0]),
        "w_down": np.ascontiguousarray(np.asarray(inputs["w_down"], np.float32)[0]),
        "lbl": lbl, "hgw": hgw, "wcols": wc, "wfin": wfin, "wmlp": wmlp, "rbt": rbt, "crep": crep,
    }
    in_maps = []
    for c in range(8):
        b, half = c // 2, c % 2
        xwin = np.zeros((2048, 2048), np.float32)
        if half == 1:
            xwin[:] = x[b]
        else:
            xwin[1024:] = x[b, 0:1024]
        d = dict(shared)
        d["xw"] = xwin
        d["hbias"] = np.full((128, 1), 0.0 if half == 1 else NEG, np.float32)
        in_maps.append(d)
    return in_maps


_NC_CACHE = {}


def kernel(**inputs):
    in_maps = _host_prep(inputs)
    if "nc" not in _NC_CACHE:
        _NC_CACHE["nc"] = build_nc()
    nc = _NC_CACHE["nc"]
    res = run_bass_kernel_spmd(nc, in_maps, core_ids=list(range(8)))
    out = np.zeros((4, 2048, 2048), np.float32)
    for c in range(8):
        b, half = c // 2, c % 2
        out[b, half * 1024:(half + 1) * 1024] = res.results[c]["out"]
    return out
```

```python
from contextlib import ExitStack

import numpy as np
import concourse.bass as bass
import concourse.mybir as mybir
from concourse.bass_utils import run_bass_kernel_spmd

F32 = mybir.dt.float32
BF16 = mybir.dt.bfloat16
U8 = mybir.dt.uint8
AF = mybir.ActivationFunctionType
ALU = mybir.AluOpType

NEG = -30000.0
EPS = 1e-6


class Reg:
    __slots__ = ("name", "w", "r")

    def __init__(self, name):
        self.name = name
        self.w = {}
        self.r = {}


class Sched:
    LIMIT = 30000
    NDQ = 6

    def __init__(self, nc, es):
        self.nc = nc
        self.es = es
        self.engs = ("pe", "act", "dve", "pool", "sp")
        self.streams = {e: [] for e in self.engs}
        self.seen = {e: {} for e in self.engs}
        self.cur = {}
        self.nsem = 0
        for e in ("pe", "act", "dve", "pool"):
            self._newsem(e)
        self.dq = {q: [[self._alloc(), 0] for _ in range(self.NDQ)] for q in ("sp", "pool")}
        self.dq_rr = {"sp": 0, "pool": 0}

    def _alloc(self):
        sem = self.es.enter_context(self.nc.semaphore("s%d" % self.nsem))
        self.nsem += 1
        return (self.nsem, sem)

    def _newsem(self, e):
        self.cur[e] = [self._alloc(), 0]

    def reg(self, name="r"):
        return Reg(name)

    def _waits(self, eng, reads, writes, excl, waw):
        need = {}

        def add(tok, same_ok):
            key, sem, val, teng = tok
            if teng == eng and not same_ok:
                return
            if self.seen[eng].get(key, 0) >= val:
                return
            if key in need and need[key][1] >= val:
                return
            need[key] = (sem, val)

        for r in reads:
            for t in r.w.values():
                add(t, True)
        for w in writes:
            if waw:
                for t in w.w.values():
                    add(t, True)
            for t in w.r.values():
                add(t, True)
        for x in excl:
            for t in x.w.values():
                add(t, True)
            for t in x.r.values():
                add(t, True)
        for key, (sem, val) in need.items():
            self.seen[eng][key] = val
            self.streams[eng].append(lambda e, sem=sem, val=val: e.wait_ge(sem, val))

    def _mark(self, tok, reads, writes, excl, waw):
        key = tok[0]
        for r in reads:
            r.r[key] = tok
        for w in writes:
            if waw or w.r:
                w.w = {key: tok}
            else:
                w.w[key] = tok
            w.r = {}
        for x in excl:
            x.w = {key: tok}
            x.r = {}

    def _tick(self, eng):
        cur = self.cur[eng]
        if cur[1] >= self.LIMIT:
            self._newsem(eng)
            cur = self.cur[eng]
        (key, sem) = cur[0]
        cur[1] += 1
        return key, sem, cur[1]

    def op(self, eng, fn, reads=(), writes=(), excl=(), waw=True):
        self.group(eng, [fn], reads, writes, excl, waw)

    def group(self, eng, fns, reads=(), writes=(), excl=(), waw=True):
        self._waits(eng, reads, writes, excl, waw)
        key, sem, val = self._tick(eng)
        for fn in fns[:-1]:
            self.streams[eng].append(lambda e, fn=fn: fn(e))
        fn = fns[-1]
        self.streams[eng].append(lambda e, fn=fn, sem=sem: fn(e).then_inc(sem, 1))
        self._mark((key, sem, val, eng), reads, writes, excl, waw)

    def dma(self, q, fn, reads=(), writes=(), waw=True):
        i = self.dq_rr[q]
        self.dq_rr[q] = (i + 1) % self.NDQ
        slot = self.dq[q][i]
        (key, sem) = slot[0]
        if slot[1] > 0 and self.seen[q].get(key, 0) < 16 * slot[1]:
            v = 16 * slot[1]
            self.seen[q][key] = v
            self.streams[q].append(lambda e, sem=sem, v=v: e.wait_ge(sem, v))
        self._waits(q, reads, writes, (), waw)
        slot[1] += 1
        val = 16 * slot[1]
        self.streams[q].append(lambda e, fn=fn, sem=sem: fn(e).then_inc(sem, 16))
        self._mark((key, sem, val, "dma_" + q), reads, writes, (), waw)

    def barrier(self, engs=("pe", "act", "dve")):
        for e in engs:
            for o in engs:
                if o == e:
                    continue
                (key, sem), cnt = self.cur[o]
                if cnt > 0 and self.seen[e].get(key, 0) < cnt:
                    self.seen[e][key] = cnt
                    self.streams[e].append(lambda en, sem=sem, cnt=cnt: en.wait_ge(sem, cnt))

    def finish(self):
        for q in ("sp", "pool"):
            for (key, sem), cnt in self.dq[q]:
                if cnt > 0:
                    v = 16 * cnt
                    self.streams["sp"].append(lambda e, sem=sem, v=v: e.wait_ge(sem, v))

    def replay(self):
        with self.nc.Block() as block:
            @block.sync
            def _(e):
                for f in self.streams["sp"]:
                    f(e)

            @block.tensor
            def _(e):
                for f in self.streams["pe"]:
                    f(e)

            @block.scalar
            def _(e):
                for f in self.streams["act"]:
                    f(e)

            @block.vector
            def _(e):
                for f in self.streams["dve"]:
                    f(e)

            @block.gpsimd
            def _(e):
                for f in self.streams["pool"]:
                    f(e)


class Arena:
    def __init__(self, nc, es, nbytes, name="arena"):
        self.t = es.enter_context(nc.sbuf_tensor(name, [128, nbytes], U8))
        self.nbytes = nbytes
        self.off = 0
        self.marks = []
        self.peak = 0
        self.log = []

    def alloc(self, shape, dtype):
        esz = 2 if dtype == BF16 else 4
        n = int(np.prod(shape))
        nb = n * esz
        off = self.reserve(nb)
        return self.at(off, shape, dtype)

    def reserve(self, nb):
        off = (self.off + 63) // 64 * 64
        assert off + nb <= self.nbytes, ("arena overflow", off, nb, self.nbytes)
        self.off = off + nb
        self.peak = max(self.peak, self.off)
        return off

    def at(self, off, shape, dtype):
        self.log.append((off, list(shape), "bf16" if dtype == BF16 else "f32"))
        esz = 2 if dtype == BF16 else 4
        nb = int(np.prod(shape)) * esz
        ap = self.t[:, off:off + nb].bitcast(dtype)
        if len(shape) == 2:
            ap = ap.rearrange("p (a b) -> p a b", b=shape[1])
        elif len(shape) == 3:
            ap = ap.rearrange("p (a b c) -> p a b c", b=shape[1], c=shape[2])
        return ap

    def push(self):
        self.marks.append(self.off)

    def pop(self):
        self.off = self.marks.pop()


class WStream:
    def __init__(self, S, units):
        self.S = S
        self.units = units
        self.free = []
        self.nload = 0
        self.nuse = 0
        self.loaded = {}
        self.extra_reads = []
        self.limits = {}

    def add_slot(self, ap, reg, max_unit=None):
        if max_unit is not None:
            self.limits[id(reg)] = max_unit
        self.free.append((ap, reg))
        self.pump()

    def pump(self):
        while self.free and self.nload < len(self.units):
            pick = None
            for i, (ap_, reg_) in enumerate(self.free):
                if self.limits.get(id(reg_), 1 << 30) >= self.nload:
                    pick = i
                    break
            if pick is None:
                break
            ap, reg = self.free.pop(pick)
            for mk in self.units[self.nload]:
                o, i = mk(ap)
                self.S.dma("pool", lambda e, o=o, i=i: e.dma_start(out=o, in_=i), reads=self.extra_reads, writes=[reg], waw=False)
            self.loaded[self.nload] = (ap, reg)
            self.nload += 1

    def get(self):
        assert self.nuse in self.loaded, "weight unit not loaded (no free slot)"
        r = self.loaded.pop(self.nuse)
        self.nuse += 1
        return r

    def release(self, slot):
        self.free.append(slot)
        self.pump()


def MM(out, lhsT, rhs, start=True, stop=True):
    return lambda e: e.matmul(out, lhsT=lhsT, rhs=rhs, start=start, stop=stop)


def TR(out, in_, ident):
    return lambda e: e.transpose(out=out, in_=in_, identity=ident)


def ACTV(out, in_, func, bias=None, scale=None, accum=None):
    kw = {}
    if bias is not None:
        kw["bias"] = bias
    if scale is not None:
        kw["scale"] = scale
    if accum is not None:
        kw["accum_out"] = accum
    return lambda e: e.activation(out=out, in_=in_, func=func, **kw)


def TS(out, in0, s1, op0, s2=None, op1=None):
    if op1 is None:
        return lambda e: e.tensor_scalar(out=out, in0=in0, scalar1=s1, scalar2=None, op0=op0)
    return lambda e: e.tensor_scalar(out=out, in0=in0, scalar1=s1, scalar2=s2, op0=op0, op1=op1)


def TT(out, in0, in1, op):
    return lambda e: e.tensor_tensor(out=out, in0=in0, in1=in1, op=op)


def STT(out, in0, scalar, in1, op0, op1):
    return lambda e: e.scalar_tensor_tensor(out=out, in0=in0, scalar=scalar, in1=in1, op0=op0, op1=op1)


def SCAN(out, d0, d1):
    return lambda e: e.tensor_tensor_scan(out=out, data0=d0, data1=d1, initial=0.0, op0=ALU.mult, op1=ALU.add)


def RECIP(out, in_):
    return lambda e: e.reciprocal(out=out, in_=in_)


def MEMSET(ap, v):
    return lambda e: e.memset(ap, v)


def COPY(out, in_):
    return lambda e: e.tensor_copy(out=out, in_=in_)


def DMA(out, in_):
    return lambda e: e.dma_start(out=out, in_=in_)


def ASEL(out, in_, pattern, op, fill, base, cm):
    return lambda e: e.affine_select(out=out, in_=in_, pattern=pattern, compare_op=op, fill=fill, base=base, channel_multiplier=cm)


ARENA_BYTES = 204 * 1024
DEBUG = 0
RUN_P1 = RUN_P2 = RUN_P3 = RUN_P4 = RUN_P6 = True
P1_NIT = 32
P1_STAGE = 9

def build_nc():
    nc = bass.Bass("TRN2", target_bir_lowering=False)

    def din(name, shape):
        return nc.dram_tensor(name, shape, F32, kind="ExternalInput").ap()

    xw = din("xw", [2048, 2048])
    w_in = din("w_in", [2048, 11264])
    w_a = din("w_a", [1024, 2048])
    w_b = din("w_b", [1024, 2048])
    w_out = din("w_out", [2048, 2048])
    w_up = din("w_up", [2048, 8192])
    w_down = din("w_down", [8192, 2048])
    lbl_d = din("lbl", [128, 16])
    hgw_d = din("hgw", [128, 1])
    wcols_d = din("wcols", [128, 32])
    wfin_d = din("wfin", [128, 2048])
    wmlp_d = din("wmlp", [128, 2048])
    rbt_d = din("rbt", [128, 16, 384])
    crep_d = din("crep", [128, 16])
    hb_d = din("hbias", [128, 1])
    out_d = nc.dram_tensor("out", [1024, 2048], F32, kind="ExternalOutput").ap()

    def wcols_unit(w, nk, c0, ncols, k0, d0):
        def mk(slot):
            return (slot[:, k0:k0 + nk, d0:d0 + ncols],
                    w[0:nk * 128, c0:c0 + ncols].rearrange("(k p) c -> p k c", p=128))
        return mk

    def wdown_unit(fb):
        def mk(slot):
            return (slot.rearrange("p (k g) c -> p k g c", g=4),
                    w_down[fb * 512:(fb + 1) * 512, :].rearrange("(k p) (g c) -> p k g c", p=128, c=512))
        return mk

    units = []
    for h in range(8):
        units.append([wcols_unit(w_in, 16, s * 1024 + h * 128, 128, 0, s * 128) for s in range(4)])
    for j in range(8):
        units.append([wcols_unit(w_in, 16, 4096 + s * 1024 + j * 128, 128, 0, s * 128) for s in range(3)])
    for mg in range(4):
        units.append([wcols_unit(w_in, 16, 7168 + mg * 512, 512, 0, 0)])
        units.append([wcols_unit(w_in, 16, 9216 + mg * 512, 512, 0, 0)])
        units.append([wcols_unit(w_a, 8, mg * 512, 512, 0, 0), wcols_unit(w_b, 8, mg * 512, 512, 8, 0)])
    for cg in range(4):
        units.append([wcols_unit(w_out, 16, cg * 512, 512, 0, 0)])
    for fb in range(16):
        units.append([wcols_unit(w_up, 16, fb * 512, 512, 0, 0)])
        units.append([wdown_unit(fb)])

    with ExitStack() as es:
        S = Sched(nc, es)
        A = Arena(nc, es, ARENA_BYTES)
        banks = [es.enter_context(nc.psum_tensor("bank%d" % i, [128, 512], F32)) for i in range(8)]
        B = [b[:, :] for b in banks]
        BR = [S.reg("bank%d" % i) for i in range(8)]
        PH = S.reg("phase")
        R_dummy = S.reg("dummy")

        ident_f = A.alloc([128], F32)
        ident_b = A.alloc([128], BF16)
        ones_f = A.alloc([128], F32)
        maskA = A.alloc([512], F32)
        ones_b = A.alloc([128], BF16)
        rmask = A.alloc([512], F32)
        eps_c = A.alloc([1], F32)
        dummy = A.alloc([1], F32)
        lbl = A.alloc([16], F32)
        lbw = A.alloc([16], F32)
        lb = A.alloc([8], F32)
        oml = A.alloc([8], F32)
        hgw = A.alloc([1], F32)
        hgw_h = A.alloc([1], F32)
        a_col = A.alloc([8], F32)
        b_col = A.alloc([8], F32)
        lna_col = A.alloc([8], F32)
        wcols = A.alloc([32], F32)
        crep = A.alloc([16], F32)
        cbh = A.alloc([16], F32)
        hb = A.alloc([1], F32)
        CR = S.reg("consts")
        C2 = S.reg("consts2")
        CONS = [CR, C2]

        def phase_barrier(extra_reads=()):
            S.barrier()
            S.op("dve", MEMSET(dummy, 0.0), reads=list(extra_reads), writes=[PH, R_dummy])

        for dst, src in ((lbl, lbl_d), (hgw, hgw_d), (wcols, wcols_d), (crep, crep_d), (hb, hb_d)):
            S.dma("sp", DMA(dst, src), writes=[CR], waw=False)
        S.op("pool", MEMSET(ident_f, 0.0), writes=[C2])
        S.op("pool", ASEL(ident_f, ident_f, [[-1, 128]], ALU.not_equal, 1.0, 0, 1), reads=[C2], writes=[C2])
        S.op("pool", COPY(ident_b, ident_f), reads=[C2], writes=[C2])
        S.op("pool", MEMSET(ones_f, 1.0 / 128.0), writes=[C2])
        S.op("pool", MEMSET(eps_c, EPS), writes=[C2])
        S.op("pool", MEMSET(rmask, 1.0), writes=[C2])
        S.op("pool", MEMSET(rmask.rearrange("p (c t) -> p c t", t=64)[:, :, 0:1], 0.0), reads=[C2], writes=[C2])
        S.op("pool", MEMSET(maskA, 1.0), writes=[C2])
        mlo = maskA[0:64, :].rearrange("p (j t) -> p j t", t=128)
        mhi = maskA[64:128, :].rearrange("p (j t) -> p j t", t=128)
        S.op("pool", ASEL(mlo, mlo, [[0, 4], [1, 128]], ALU.is_ge, 0.0, 0, -1), reads=[C2], writes=[C2])
        S.op("pool", ASEL(mlo, mlo, [[0, 4], [-1, 128]], ALU.is_ge, 0.0, 63, 0), reads=[C2], writes=[C2])
        S.op("pool", ASEL(mhi, mhi, [[0, 4], [1, 128]], ALU.is_ge, 0.0, -64, -1), reads=[C2], writes=[C2])
        S.op("pool", MEMSET(ones_b, 1.0 / 128.0), writes=[C2])
        S.op("dve", TT(lbw[:, 0:8], lbl[:, 8:16], lbl[:, 0:8], ALU.subtract), reads=[CR], writes=[C2])
        S.op("act", ACTV(lbw[:, 8:16], lbw[:, 0:8], AF.Exp, scale=-1.0), reads=[C2], writes=[C2])
        S.op("act", ACTV(lbw[:, 0:8], lbw[:, 0:8], AF.Exp), reads=[C2], writes=[C2])
        S.op("dve", TS(lbw, lbw, 1.0, ALU.add), reads=[C2], writes=[C2])
        S.op("dve", RECIP(lb, lbw[:, 0:8]), reads=[C2], writes=[C2])
        S.op("dve", RECIP(oml, lbw[:, 8:16]), reads=[C2], writes=[C2])
        S.op("dve", TS(cbh, crep, hb[:, 0:1], ALU.add), reads=[CR], writes=[C2])
        S.op("dve", TS(a_col, oml, 0.5, ALU.mult), reads=[C2], writes=[C2])
        S.op("dve", TT(b_col, a_col, lb, ALU.add), reads=[C2], writes=[C2])
        S.op("act", ACTV(lna_col, a_col, AF.Ln), reads=[C2], writes=[C2])
        S.op("dve", TS(hgw_h, hgw, 0.5, ALU.mult), reads=[CR], writes=[C2])

        slots = [(A.alloc([16, 512], BF16), S.reg("slot%d" % i)) for i in range(3)]
        H_off = A.reserve(32768)
        uT_h0 = A.at(H_off, [16, 512], BF16)
        uT_h1 = A.at(H_off + 16384, [16, 512], BF16)
        uT_own = A.alloc([16, 1024], BF16)
        R_uh0, R_uh1, R_uo = S.reg("uh0"), S.reg("uh1"), S.reg("uo")
        ws = WStream(S, units)
        ws.add_slot(*slots[0])

        def ublk(blk):
            if blk == 0:
                return uT_h0, R_uh0, 0
            if blk == 1:
                return uT_h1, R_uh1, 0
            return uT_own, R_uo, (blk - 2) * 512

        def norm_transpose(n_tiles, load_tile, dst_of_group, wc0, xts, xregs, ssq, sd, rstd, junk, NR):
            ng = len(xts) // 4
            NRt = [S.reg("nrt%d" % i) for i in range(n_tiles)]
            RJ = S.reg("junk")

            def bufs(g):
                return xts[(g % ng) * 4:(g % ng) * 4 + 4], xregs[(g % ng) * 4:(g % ng) * 4 + 4]

            def stats(g, js=(0, 1, 2, 3)):
                xg, xrg = bufs(g)
                for j in js:
                    i = g * 4 + j
                    xt, xr = xg[j], xrg[j]
                    src, sreg = load_tile(i, xt, xr)
                    S.op("act", ACTV(junk, src, AF.Square, accum=ssq[:, i:i + 1]), reads=[sreg], writes=[NRt[i], RJ])
                    S.op("act", ACTV(sd[:, i:i + 1], ssq[:, i:i + 1], AF.Sqrt, bias=eps_c[:, 0:1], scale=1.0 / 2048.0),
                         reads=[NRt[i], C2], writes=[NRt[i]])
                    S.op("dve", RECIP(rstd[:, i:i + 1], sd[:, i:i + 1]), reads=[NRt[i]], writes=[NRt[i]])
                    S.op("dve", TS(xt, src, rstd[:, i:i + 1], ALU.mult), reads=[NRt[i], sreg], writes=[xr])

            bi = 0
            ngroups = n_tiles // 4
            stats(0)
            for g in range(ngroups):
                xg, xrg = bufs(g)
                dstT, dreg, tok0 = dst_of_group(g)
                for f in range(16):
                    bk = bi % 8
                    bi += 1
                    fns = [TR(B[bk][:, j * 128:(j + 1) * 128], xg[j][:, f * 128:(f + 1) * 128], ident_f) for j in range(4)]
                    S.group("pe", fns, reads=list(xrg) + CONS, excl=[BR[bk]])
                    sc = wcols[:, wc0 + f:wc0 + f + 1]
                    if f % 2 == 0:
                        S.op("act", ACTV(dstT[:, f, tok0:tok0 + 512], B[bk], AF.Copy, scale=sc),
                             reads=CONS, excl=[BR[bk]], writes=[dreg], waw=False)
                    else:
                        S.op("dve", TS(dstT[:, f, tok0:tok0 + 512], B[bk], sc, ALU.mult),
                             reads=CONS, excl=[BR[bk]], writes=[dreg], waw=False)
                    if f % 4 == 3 and g + 1 < ngroups:
                        stats(g + 1, (f // 4,))

        A.push()
        xts = [A.alloc([2048], F32) for _ in range(8)]
        xregs = [S.reg("xt%d" % j) for j in range(8)]
        junk = A.alloc([2048], BF16)
        ssq = A.alloc([16], F32)
        sd = A.alloc([16], F32)
        rstd = A.alloc([16], F32)
        NR = S.reg("nr")

        def load0(i, xt, xr):
            S.dma("sp", DMA(xt, xw[i * 128:(i + 1) * 128, :]), writes=[xr])
            return xt, xr

        norm_transpose(16, load0, ublk, 0, xts, xregs, ssq, sd, rstd, junk, NR)
        ws.extra_reads = list(xregs)
        ws.add_slot(*slots[1])
        ws.add_slot(*slots[2])
        ws.extra_reads = []
        A.pop()
        phase_barrier()

        A.push()
        yaT = A.alloc([8, 1024], BF16)
        ybT = A.alloc([8, 1024], BF16)
        R_ya, R_yb = S.reg("ya"), S.reg("yb")
        A.push()
        f_sq = A.alloc([512], F32)
        f_ga = A.alloc([512], F32)
        f_sn = A.alloc([512], F32)
        f_lg = A.alloc([512], F32)
        f_bb = A.alloc([512], F32)
        f_eb = A.alloc([512], F32)
        f_sg = A.alloc([512], F32)
        f_oT = A.alloc([512], F32)
        f_os = A.alloc([512], F32)
        b_vT = A.alloc([512], BF16)
        b_kT = A.alloc([512], BF16)
        b_qT = A.alloc([512], BF16)
        b_kh = A.alloc([512], BF16)
        b_tok = A.alloc([8, 128], BF16)
        b_As = A.alloc([512], BF16)
        b_os = A.alloc([512], BF16)
        Sring = [A.alloc([9, 128], F32) for _ in range(2)]
        Sb = [A.alloc([9, 128], BF16) for _ in range(2)]
        R = {n: S.reg(n) for n in ["sq", "ga", "sn", "lg", "bb", "eb", "sg", "oT", "os", "vT", "kT", "qT", "kh", "tok", "As", "S",
                                   "Sb0", "Sb1"]}
        RSb = [R["Sb0"], R["Sb1"]]
        BTb = B[3].bitcast(BF16)
        QS = 128.0 ** -0.5
        its = [(h, blk) for h in range(8) for blk in range(4)][:P1_NIT]
        hslot = {}
        f_sg2 = [f_sg, A.alloc([512], F32), A.alloc([512], F32)]
        R["sg0"], R["sg1"], R["sg2"] = S.reg("sg0"), S.reg("sg1"), S.reg("sg2")
        b_qT2 = [b_qT, A.alloc([512], BF16)]
        R["qT0"], R["qT1"] = S.reg("qT0"), S.reg("qT1")

        f_eb2 = [f_eb, A.alloc([512], F32)]
        R["eb0"], R["eb1"] = S.reg("eb0"), S.reg("eb1")

        def slot_of(n):
            h, blk = its[n]
            if h not in hslot:
                hslot[h] = ws.get()
            return hslot[h]

        def pe_proj(n, seg, bk):
            h, blk = its[n]
            sl, sreg = slot_of(n)
            uT, ureg, t0 = ublk(blk)
            fns = [MM(B[bk], sl[:, k, seg * 128:(seg + 1) * 128], uT[:, k, t0:t0 + 512], k == 0, k == 15) for k in range(16)]
            S.group("pe", fns, reads=[sreg, ureg], excl=[BR[bk]])

        def own_(n):
            return 0 <= n < len(its) and its[n][1] >= 2

        def s_hq(n):
            pe_proj(n, 0, 0)
            S.op("act", ACTV(f_sq, B[0], AF.Tanh, scale=0.5), excl=[BR[0]], writes=[R["sq"]])
            S.op("dve", STT(f_sq, f_sq, 1.0, B[0], ALU.add, ALU.mult), reads=[R["sq"]], excl=[BR[0]], writes=[R["sq"]])

        def s_hf(n, mid=None):
            h = its[n][0]
            if mid is None:
                pe_proj(n, 1, 1)
            else:
                hh_, blk_ = its[n]
                sl, sreg = slot_of(n)
                uT, ureg, t0 = ublk(blk_)
                mk = lambda k: MM(B[1], sl[:, k, 128:256], uT[:, k, t0:t0 + 512], k == 0, k == 15)
                S.group("pe", [mk(k) for k in range(8)], reads=[sreg, ureg], excl=[BR[1]])
                mid()
                S.group("pe", [mk(k) for k in range(8, 16)], reads=[sreg, ureg], excl=[BR[1]])
            S.op("act", ACTV(f_ga, B[1], AF.Tanh, scale=0.5), excl=[BR[1]], writes=[R["ga"]])
            S.op("act", ACTV(f_sn, B[1], AF.Tanh, scale=-0.5), excl=[BR[1]], writes=[R["sn"]])
            S.op("dve", TS(f_ga, f_ga, a_col[:, h:h + 1], ALU.mult, b_col[:, h:h + 1], ALU.add), reads=[R["ga"]] + CONS, writes=[R["ga"]])

        def s_ln(n):
            S.op("act", ACTV(f_lg, f_ga, AF.Ln), reads=[R["ga"]], writes=[R["lg"]])

        def s_scan(n):
            S.op("dve", SCAN(f_bb, rmask, f_lg), reads=[R["lg"]] + CONS, writes=[R["bb"]])

        def s_exps(n):
            h = its[n][0]
            eb, reb = f_eb2[n % 2], R["eb%d" % (n % 2)]
            S.op("act", ACTV(eb, f_bb, AF.Exp), reads=[R["bb"]], writes=[reb])
            S.op("act", ACTV(f_lg, f_bb, AF.Exp, scale=-1.0, bias=lna_col[:, h:h + 1]), reads=[R["bb"]] + CONS, writes=[R["lg"]])

        def s_kq(n):
            h = its[n][0]
            eb, reb = f_eb2[n % 2], R["eb%d" % (n % 2)]
            S.op("dve", STT(b_kT, f_sn, 1.0, f_lg, ALU.add, ALU.mult), reads=[R["sn"], R["lg"]] + CONS, writes=[R["kT"]])
            if own_(n):
                S.op("dve", STT(b_qT2[n % 2], f_sq, 0.5 * QS, eb, ALU.mult, ALU.mult), reads=[R["sq"], reb], writes=[R["qT%d" % (n % 2)]])
            eb3 = eb.rearrange("p (c t) -> p c t", t=64)
            S.op("dve", TT(b_kh.rearrange("p (c t) -> p c t", t=64), b_kT.rearrange("p (c t) -> p c t", t=64),
                           eb3[:, :, 63:64].broadcast_to([128, 8, 64]), ALU.mult), reads=[R["kT"], reb], writes=[R["kh"]])

        def s_hg(n):
            pe_proj(n, 3, 4)
            sg, rsg = f_sg2[n % 3], R["sg%d" % (n % 3)]
            S.op("act", ACTV(sg, B[4], AF.Tanh, scale=0.5), excl=[BR[4]], writes=[rsg])

        def s_hg_b(n):
            sg, rsg = f_sg2[n % 3], R["sg%d" % (n % 3)]
            S.op("dve", STT(sg, sg, 1.0, B[4], ALU.add, ALU.mult), reads=[rsg], excl=[BR[4]], writes=[rsg])

        def s_hi(n):
            pe_proj(n, 2, 2)
            S.op("act", ACTV(b_vT, B[2], AF.Copy), excl=[BR[2]], writes=[R["vT"]])

        def s_tr(n):
            fns = [TR(BTb[:, j * 128:(j + 1) * 128], b_kh[:, j * 128:(j + 1) * 128], ident_b) for j in range(4)]
            fns += [TR(BTb[:, (4 + j) * 128:(5 + j) * 128], b_vT[:, j * 128:(j + 1) * 128], ident_b) for j in range(4)]
            S.group("pe", fns, reads=[R["kh"], R["vT"]] + CONS, excl=[BR[3]])
            S.op("act", ACTV(b_tok, BTb.rearrange("p (a b) -> p a b", b=128), AF.Copy), excl=[BR[3]], writes=[R["tok"]])

        def s_ms_a(n):
            S.op("pe", MM(B[7], ones_b, b_os), reads=[R["os"]] + CONS, excl=[BR[7]])
            S.op("act", ACTV(f_os, B[7], AF.Ln, bias=eps_c[:, 0:1]), reads=CONS, excl=[BR[7]], writes=[R["os"]])
            S.op("act", ACTV(f_os, f_os, AF.Exp, scale=-0.5), reads=[R["os"]], writes=[R["os"]])

        def s_ms_b(n):
            h, blk = its[n]
            S.op("dve", TT(f_oT, f_oT, f_os, ALU.mult), reads=[R["oT"], R["os"]], writes=[R["oT"]])
            t0 = (blk - 2) * 512
            S.op("dve", STT(yaT[:, h, t0:t0 + 512], f_oT, hgw_h[:, 0:1], f_sg2[n % 3], ALU.mult, ALU.mult),
                 reads=[R["oT"], R["sg%d" % (n % 3)]] + CONS, writes=[R_ya], waw=False)

        def s_u(n):
            fns = []
            for c in range(8):
                j, hh = c // 2, c % 2
                rows = slice(hh * 64, hh * 64 + 64)
                fns.append(MM(B[5 + hh][:, j * 128:(j + 1) * 128], b_tok[rows, j, :], b_tok[rows, 4 + j, :]))
            S.group("pe", fns, reads=[R["tok"]], excl=[BR[5], BR[6]])
            if own_(n):
                q = b_qT2[n % 2]
                fns = [MM(B[4][:, j * 128:(j + 1) * 128], b_kT[:, j * 128:(j + 1) * 128], q[:, j * 128:(j + 1) * 128]) for j in range(4)]
                S.group("pe", fns, reads=[R["kT"], R["qT%d" % (n % 2)]], excl=[BR[4]])
                S.op("dve", TT(b_As, B[4], maskA, ALU.mult), reads=CONS, excl=[BR[4]], writes=[R["As"]])

        def s_chain(n, c0, c1):
            par = its[n][1] % 2
            eb, reb = f_eb2[n % 2], R["eb%d" % (n % 2)]
            for c in range(c0, c1):
                ub = B[5 + c % 2][:, (c // 2) * 128:(c // 2 + 1) * 128]
                sin = Sring[1 - par][:, 8, :] if c == 0 else Sring[par][:, c, :]
                S.op("dve", STT(Sring[par][:, c + 1, :], sin, eb[:, c * 64 + 63:c * 64 + 64], ub, ALU.mult, ALU.add),
                     reads=[R["S"], reb], writes=[R["S"]], excl=[BR[5 + c % 2]])

        def s_cast(n):
            h, blk = its[n]
            par = blk % 2
            if blk >= 2:
                S.op("act", ACTV(Sb[par][:, 1:9, :], Sring[par][:, 1:9, :], AF.Copy), reads=[R["S"]], writes=[RSb[par]], waw=False)
            elif blk == 1:
                S.op("act", ACTV(Sb[par][:, 8, :], Sring[par][:, 8, :], AF.Copy), reads=[R["S"]], writes=[RSb[par]], waw=False)

        def s_o(n):
            par = its[n][1] % 2
            q = b_qT2[n % 2]
            fns = []
            for j in range(4):
                oc = B[7][:, j * 128:(j + 1) * 128]
                fns.append(MM(oc, b_tok[:, 4 + j, :], b_As[:, j * 128:(j + 1) * 128], True, False))
                for c in (2 * j, 2 * j + 1):
                    sprev = Sb[1 - par][:, 8, :] if c == 0 else Sb[par][:, c, :]
                    fns.append(MM(B[7][:, c * 64:(c + 1) * 64], sprev, q[:, c * 64:(c + 1) * 64], False, c == 2 * j + 1))
            S.group("pe", fns, reads=[RSb[0], RSb[1], R["qT%d" % (n % 2)], R["tok"], R["As"]], excl=[BR[7]])
            S.op("act", ACTV(f_oT, B[7], AF.Copy), excl=[BR[7]], writes=[R["oT"]])
            S.op("act", ACTV(b_os, B[7], AF.Square), excl=[BR[7]], writes=[R["os"]])

        if RUN_P1:
            NI = len(its)

            def load_first_half(nx):
                if own_(nx):
                    s_hq(nx)
                s_hf(nx)

            load_first_half(0)
            s_ln(0)
            s_scan(0)
            if own_(0):
                s_hg(0)
            s_hi(0)
            s_exps(0)
            s_kq(0)
            if own_(0):
                s_hg_b(0)
            for n in range(NI):
                h, blk = its[n]
                nx = n + 1 if n + 1 < NI else None
                if blk == 0:
                    S.op("dve", MEMSET(Sring[1][:, 8, :], 0.0), writes=[R["S"]])
                if nx is not None:
                    if own_(nx):
                        s_hq(nx)
                    s_hf(nx, mid=lambda n=n: s_tr(n))
                else:
                    s_tr(n)
                if own_(n - 1):
                    s_ms_a(n - 1)
                if nx is not None:
                    s_ln(nx)
                s_u(n)
                s_chain(n, 0, 8)
                if own_(n - 1):
                    s_ms_b(n - 1)
                if nx is not None:
                    s_scan(nx)
                    if own_(nx):
                        s_hg(nx)
                s_cast(n)
                if nx is not None:
                    if own_(nx):
                        s_hg_b(nx)
                    s_hi(nx)
                    s_exps(nx)
                    s_kq(nx)
                if own_(n):
                    s_o(n)
                if blk == 0 and h >= 1:
                    ws.release(hslot[h - 1])
            if own_(NI - 1):
                s_ms_a(NI - 1)
                s_ms_b(NI - 1)
            ws.release(hslot[its[NI - 1][0]])
        A.pop()
        phase_barrier()
        if DEBUG == 1:
            dbg = nc.dram_tensor("dbg", [128, 8, 1024], BF16, kind="ExternalOutput").ap()
            S.dma("sp", DMA(dbg, yaT), reads=[R_ya, PH])

        A.push()
        rbt = A.alloc([16, 384], F32)
        o1 = H_off
        aqT2 = [A.alloc([1024], BF16), A.at(o1, [1024], BF16)]
        akT2 = [A.alloc([1536], BF16), A.at(o1 + 2048, [1536], BF16)]
        avT2 = [A.alloc([1536], BF16), A.at(o1 + 2048 + 3072, [1536], BF16)]
        Vext2 = [A.alloc([12, 2, 65], BF16), A.at(o1 + 2048 + 3072 + 3072, [12, 2, 65], BF16)]
        tmpS = [A.alloc([384], F32) for _ in range(3)]
        PT = [A.alloc([640], BF16) for _ in range(3)]
        ybt = A.alloc([8, 128], BF16)
        rcp = A.alloc([16], F32)
        R2 = {n: S.reg(n) for n in ["rbt", "tmp0", "tmp1", "tmp2", "PT0", "PT1", "PT2", "ybt", "rcp"]}
        Rp = [{n: S.reg(n + str(i)) for n in ["aq", "ak", "avT", "V"]} for i in range(2)]
        S.dma("sp", DMA(rbt, rbt_d), reads=[PH], writes=[R2["rbt"]])
        for i in range(2):
            S.op("dve", MEMSET(Vext2[i], 1.0), reads=[PH, R_uh0], writes=[Rp[i]["V"]])
        for i in range(3):
            S.op("dve", MEMSET(PT[i], 0.0), reads=[PH], writes=[R2["PT%d" % i]])
        SC = 64.0 ** -0.5
        pbc = [0]

        def proj_items(j):
            jb = j % 2
            st = {}
            items = []

            def proj_item(name, dst, bi_, blk, seg, release=False):
                def f():
                    if "s" not in st:
                        st["s"] = ws.get()
                    sl, sreg = st["s"]
                    uT, ureg, t0 = ublk(blk)
                    bk = 6 + (pbc[0] % 2)
                    pbc[0] += 1
                    fns = [MM(B[bk], sl[:, k, seg * 128:(seg + 1) * 128], uT[:, k, t0:t0 + 512], k == 0, k == 15) for k in range(16)]
                    S.group("pe", fns, reads=[sreg, ureg], excl=[BR[bk]])
                    if pbc[0] % 2:
                        S.op("act", ACTV(dst[:, bi_ * 512:(bi_ + 1) * 512], B[bk], AF.Copy), excl=[BR[bk]], writes=[Rp[jb][name]], waw=False)
                    else:
                        S.op("dve", COPY(dst[:, bi_ * 512:(bi_ + 1) * 512], B[bk]), excl=[BR[bk]], writes=[Rp[jb][name]], waw=False)
                    if release:
                        ws.release(st["s"])
                return f

            for bi_ in range(2):
                items.append(proj_item("aq", aqT2[jb], bi_, 2 + bi_, 0))
            for bi_ in range(3):
                items.append(proj_item("ak", akT2[jb], bi_, 1 + bi_, 1))
            for bi_ in range(3):
                items.append(proj_item("avT", avT2[jb], bi_, 1 + bi_, 2, release=(bi_ == 2)))

            def vtr(w0, nw):
                def f():
                    bk = 6 + (pbc[0] % 2)
                    pbc[0] += 1
                    bt = B[bk].bitcast(BF16)
                    fns = [TR(bt[:, i * 128:(i + 1) * 128], avT2[jb][:, (w0 + i) * 128:(w0 + i + 1) * 128], ident_b) for i in range(nw)]
                    S.group("pe", fns, reads=[Rp[jb]["avT"]] + CONS, excl=[BR[bk]])
                    S.op("act", ACTV(Vext2[jb][:, w0:w0 + nw, :, 0:64], bt[:, 0:nw * 128].rearrange("p (a b c) -> p a b c", b=2, c=64), AF.Copy),
                         excl=[BR[bk]], writes=[Rp[jb]["V"]], waw=False)
                return f

            items.append(vtr(0, 8))
            items.append(vtr(8, 4))
            return items

        def stage_a(j, qt, hh, n_):
            jb = j % 2
            aqT, akT = aqT2[jb], akT2[jb]
            head = 2 * j + hh
            rows = slice(hh * 64, hh * 64 + 64)
            ub, u = n_ % 2, n_ % 3
            bsA, bsB = 2 * ub, 2 * ub + 1
            fns = []
            for o in range(5):
                kt = qt + 4 - o
                ob_ = B[bsA][:, o * 128:(o + 1) * 128] if o < 3 else B[bsB][:, (o - 3) * 128:(o - 2) * 128]
                fns.append(MM(ob_, akT[rows, kt * 128:(kt + 1) * 128], aqT[rows, qt * 128:(qt + 1) * 128]))
            S.group("pe", fns, reads=[Rp[jb]["aq"], Rp[jb]["ak"]], excl=[BR[bsA], BR[bsB]])
            tr, pr = R2["tmp%d" % u], R2["PT%d" % u]
            S.op("dve", STT(tmpS[u], B[bsA][:, 0:384], SC, rbt[:, head, :], ALU.mult, ALU.add),
                 reads=[R2["rbt"]], excl=[BR[bsA]], writes=[tr])
            o = 0
            while o < 3:
                hist = (qt + 4 - o) < 4
                o2 = o
                while o2 + 1 < 3 and ((qt + 4 - (o2 + 1)) < 4) == hist:
                    o2 += 1
                c0, c1 = o * 128, (o2 + 1) * 128
                if hist:
                    S.op("act", ACTV(PT[u][:, c0:c1], tmpS[u][:, c0:c1], AF.Exp, bias=hb[:, 0:1]),
                         reads=[tr] + CONS, writes=[pr], waw=False)
                else:
                    S.op("act", ACTV(PT[u][:, c0:c1], tmpS[u][:, c0:c1], AF.Exp), reads=[tr], writes=[pr], waw=False)
                o = o2 + 1
            h3, h4 = (qt + 1) < 4, qt < 4
            if h3 == h4:
                bias = (cbh if h3 else crep)[:, head:head + 1]
                S.op("act", ACTV(PT[u][:, 384:640], B[bsB][:, 0:256], AF.Exp, bias=bias, scale=SC),
                     reads=CONS, excl=[BR[bsB]], writes=[pr], waw=False)
            else:
                for o in (3, 4):
                    hist = (qt + 4 - o) < 4
                    bias = (cbh if hist else crep)[:, head:head + 1]
                    S.op("act", ACTV(PT[u][:, o * 128:(o + 1) * 128], B[bsB][:, (o - 3) * 128:(o - 2) * 128], AF.Exp, bias=bias, scale=SC),
                         reads=CONS, excl=[BR[bsB]], writes=[pr], waw=False)
            S.op("pool", MEMSET(PT[u][0:64, 4 * 128 + 64:5 * 128], 0.0), writes=[pr])

        def stage_b(j, qt, hh, n_):
            jb = j % 2
            u = n_ % 3
            ob = 4 + n_ % 2
            pr = R2["PT%d" % u]
            fns = []
            for o in range(5):
                kt = qt + 4 - o
                fns.append(MM(B[ob][:, 0:65], PT[u][:, o * 128:(o + 1) * 128], Vext2[jb][:, kt, hh, :], o == 0, o == 4))
            S.group("pe", fns, reads=[pr, Rp[jb]["V"]], excl=[BR[ob]])
            rc = rcp[:, (ob - 4):(ob - 3)]
            S.op("dve", RECIP(rc, B[ob][:, 64:65]), excl=[BR[ob]], writes=[R2["rcp"]])
            S.op("dve", TS(ybt[:, qt, hh * 64:(hh + 1) * 64], B[ob][:, 0:64], rc, ALU.mult),
                 reads=[R2["rcp"]], excl=[BR[ob]], writes=[R2["ybt"]], waw=False)

        NP_ = 8 if RUN_P2 else 0
        ulist = [(qt, hh) for qt in range(8) for hh in range(2)]
        NU = len(ulist)
        if NP_:
            for it in proj_items(0):
                it()
        gn = 0
        for j in range(NP_):
            nxt = proj_items(j + 1) if j + 1 < NP_ else []
            slots_at = {0, 2, 3, 5, 6, 8, 9, 11, 12, 14}
            for n_ in range(min(2, NU)):
                stage_a(j, ulist[n_][0], ulist[n_][1], gn + n_)
            for n_, (qt, hh) in enumerate(ulist):
                if n_ + 2 < NU:
                    stage_a(j, ulist[n_ + 2][0], ulist[n_ + 2][1], gn + n_ + 2)
                stage_b(j, qt, hh, gn + n_)
                if nxt and n_ in slots_at:
                    nxt.pop(0)()
            while nxt:
                nxt.pop(0)()
            gn += NU
            bkT = 6 + (pbc[0] % 2)
            pbc[0] += 1
            btT = B[bkT].bitcast(BF16)
            fns = [TR(btT[:, qt * 128:(qt + 1) * 128], ybt[:, qt, :], ident_b) for qt in range(8)]
            S.group("pe", fns, reads=[R2["ybt"]] + CONS, excl=[BR[bkT]])
            S.op("act", ACTV(ybT[:, j, :], btT, AF.Copy), excl=[BR[bkT]], writes=[R_yb], waw=False)
        A.pop()
        phase_barrier()
        if DEBUG == 2:
            dbg = nc.dram_tensor("dbg", [128, 8, 1024], BF16, kind="ExternalOutput").ap()
            S.dma("sp", DMA(dbg, ybT), reads=[R_yb, PH])

        mergedT = A.at(H_off, [16, 1024], BF16)
        R_mg = S.reg("merged")
        A.push()
        g_a = [A.alloc([512], F32) for _ in range(2)]
        g_b = [A.alloc([512], F32) for _ in range(2)]
        R3 = {n: S.reg(n) for n in ["ga0", "ga1", "gb0", "gb1"]}
        if RUN_P3:
            for i in range(2):
                xs_ap, xs_reg = A.alloc([16, 512], BF16), S.reg("xslot%d" % i)
                S.op("dve", MEMSET(dummy, 0.0), reads=[PH, xs_reg], writes=[R_dummy])
                ws.add_slot(xs_ap, xs_reg, max_unit=16 + 12 - 1)
        it3 = 0
        for mg in range(4 if RUN_P3 else 0):
            sx = ws.get()
            sy = ws.get()
            sz = ws.get()
            for mi in range(4):
                m = mg * 4 + mi
                cs = slice(mi * 128, (mi + 1) * 128)
                for t2 in range(2):
                    u = it3 % 2
                    it3 += 1
                    bb = 4 * u
                    ts_ = slice(t2 * 512, (t2 + 1) * 512)
                    S.group("pe", [MM(B[bb + 0], sx[0][:, k, cs], uT_own[:, k, ts_], k == 0, k == 15) for k in range(16)],
                            reads=[sx[1], R_uo], excl=[BR[bb + 0]])
                    S.group("pe", [MM(B[bb + 1], sy[0][:, k, cs], uT_own[:, k, ts_], k == 0, k == 15) for k in range(16)],
                            reads=[sy[1], R_uo], excl=[BR[bb + 1]])
                    S.group("pe", [MM(B[bb + 2], sz[0][:, k, cs], yaT[:, k, ts_], k == 0, k == 7) for k in range(8)],
                            reads=[sz[1], R_ya], excl=[BR[bb + 2]])
                    S.group("pe", [MM(B[bb + 3], sz[0][:, 8 + k, cs], ybT[:, k, ts_], k == 0, k == 7) for k in range(8)],
                            reads=[sz[1], R_yb], excl=[BR[bb + 3]])
                    ra, rb_ = R3["ga%d" % u], R3["gb%d" % u]
                    S.op("act", ACTV(g_a[u], B[bb + 0], AF.Sigmoid), excl=[BR[bb + 0]], writes=[ra])
                    S.op("act", ACTV(g_b[u], B[bb + 1], AF.Sigmoid), excl=[BR[bb + 1]], writes=[rb_])
                    S.op("dve", TT(g_a[u], g_a[u], B[bb + 2], ALU.mult), reads=[ra], excl=[BR[bb + 2]], writes=[ra])
                    S.op("dve", TT(g_b[u], g_b[u], B[bb + 3], ALU.mult), reads=[rb_], excl=[BR[bb + 3]], writes=[rb_])
                    S.op("dve", TT(mergedT[:, m, ts_], g_a[u], g_b[u], ALU.add), reads=[ra, rb_, PH], writes=[R_mg], waw=False)
            ws.release(sx)
            ws.release(sy)
            ws.release(sz)
        A.pop()
        A.pop()
        phase_barrier()
        if DEBUG == 3:
            dbg = nc.dram_tensor("dbg", [128, 16, 1024], BF16, kind="ExternalOutput").ap()
            S.dma("sp", DMA(dbg, mergedT), reads=[R_mg, PH])

        hres = A.alloc([8, 2048], F32)
        R_h = [S.reg("h%d" % t) for t in range(8)]
        u2T = uT_own
        R_u2 = S.reg("u2T")
        A.push()
        hnb = [A.alloc([2048], BF16) for _ in range(2)]
        R_hn = [S.reg("hnb%d" % i) for i in range(2)]
        wrep_mlp = A.alloc([2048], F32)
        ssq5 = A.alloc([8], F32)
        sd5 = A.alloc([8], F32)
        rstd5 = A.alloc([8], F32)
        NR5 = S.reg("nr5")
        R_wm = S.reg("wrep_mlp")
        S.dma("sp", DMA(wrep_mlp, wmlp_d), reads=[PH], writes=[R_wm])
        S.op("dve", MEMSET(ssq5, 0.0), reads=[PH], writes=[NR5] + R_hn)
        for tt in range(8):
            S.dma("sp", DMA(hres[:, tt, :], xw[1024 + tt * 128:1024 + (tt + 1) * 128, :]), reads=[PH], writes=[R_h[tt]])
        BT6b, BT7b = B[6].bitcast(BF16), B[7].bitcast(BF16)

        def emit_norm_stats(tt):
            u = tt % 2
            hn, rhn = hnb[u], R_hn[u]
            S.op("act", ACTV(hn, hres[:, tt, :], AF.Square, accum=ssq5[:, tt:tt + 1]), reads=[R_h[tt]], writes=[NR5, rhn])
            S.op("act", ACTV(sd5[:, tt:tt + 1], ssq5[:, tt:tt + 1], AF.Sqrt, bias=eps_c[:, 0:1], scale=1.0 / 2048.0),
                 reads=[NR5] + CONS, writes=[NR5])
            S.op("dve", RECIP(rstd5[:, tt:tt + 1], sd5[:, tt:tt + 1]), reads=[NR5], writes=[NR5])
            S.op("dve", STT(hn, hres[:, tt, :], rstd5[:, tt:tt + 1], wrep_mlp, ALU.mult, ALU.mult),
                 reads=[R_h[tt], NR5, R_wm], writes=[rhn])

        def emit_norm_trans(tt):
            u = tt % 2
            hn, rhn = hnb[u], R_hn[u]
            for half, bt, bk in ((0, BT6b, 6), (1, BT7b, 7)):
                fns = [TR(bt[:, i * 128:(i + 1) * 128], hn[:, (half * 8 + i) * 128:(half * 8 + i + 1) * 128], ident_b) for i in range(8)]
                S.group("pe", fns, reads=[rhn] + CONS, excl=[BR[bk]])
                dst = u2T[:, half * 8:(half + 1) * 8, tt * 128:(tt + 1) * 128]
                srcv = bt.rearrange("p (a b) -> p a b", b=128)
                if half == 0:
                    S.op("act", ACTV(dst, srcv, AF.Copy), excl=[BR[bk]], writes=[R_u2], waw=False)
                else:
                    S.op("dve", COPY(dst, srcv), excl=[BR[bk]], writes=[R_u2], waw=False)

        it4 = 0
        for cg in range(4 if RUN_P4 else 0):
            sl, sreg = ws.get()
            for tt in range(8):
                bk = it4 % 6
                it4 += 1
                S.group("pe", [MM(B[bk], mergedT[:, k, tt * 128:(tt + 1) * 128], sl[:, k, :], k == 0, k == 15) for k in range(16)],
                        reads=[sreg, R_mg], excl=[BR[bk]])
                hv = hres[:, tt, cg * 512:(cg + 1) * 512]
                S.op("dve", TT(hv, hv, B[bk], ALU.add), reads=[R_h[tt]], excl=[BR[bk]], writes=[R_h[tt]])
                if cg == 3 and tt >= 1:
                    emit_norm_stats(tt - 1)
                if cg == 3 and tt >= 2:
                    emit_norm_trans(tt - 2)
            ws.release((sl, sreg))
        if RUN_P4:
            emit_norm_stats(7)
            emit_norm_trans(6)
            emit_norm_trans(7)
        else:
            for tt in range(8):
                emit_norm_stats(tt)
                emit_norm_trans(tt)
        A.pop()
        phase_barrier([s[1] for s in slots])

        A.push()
        slot4 = (A.at(H_off, [16, 512], BF16), S.reg("slot3"))
        hid = [A.at(H_off + 16384 + i * 8192, [4, 1024], BF16) for i in range(2)]
        rr = [A.alloc([512], F32) for _ in range(2)]
        R6 = {n: S.reg(n) for n in ["hid0", "hid1", "rr0", "rr1"]}
        S.op("dve", MEMSET(dummy, 0.0), reads=[PH, slot4[1]], writes=[R6["hid0"], R6["hid1"], R_dummy])
        ws.add_slot(*slot4)
        upc = [0]

        def emit_up_unit(fb, un, sa):
            fi, half = un // 2, un % 2
            par = fb % 2
            bk = upc[0] % 4
            u = upc[0] % 2
            upc[0] += 1
            S.group("pe", [MM(B[bk], sa[0][:, k, fi * 128:(fi + 1) * 128], u2T[:, k, half * 512:(half + 1) * 512], k == 0, k == 15)
                           for k in range(16)], reads=[sa[1], R_u2], excl=[BR[bk]])
            S.op("act", ACTV(rr[u], B[bk], AF.Relu), excl=[BR[bk]], writes=[R6["rr%d" % u]])
            S.op("dve", TT(hid[par][:, fi, half * 512:(half + 1) * 512], rr[u], rr[u], ALU.mult),
                 reads=[R6["rr%d" % u]], writes=[R6["hid%d" % par]], waw=False)

        def emit_down_tt(fb, tt, sb):
            par = fb % 2
            bb = 0 if (fb == 15 and tt % 2 == 0) else 4
            fns = []
            for k in range(4):
                for cg in range(4):
                    fns.append(MM(B[bb + cg], hid[par][:, k, tt * 128:(tt + 1) * 128], sb[0][:, k * 4 + cg, :], k == 0, k == 3))
            S.group("pe", fns, reads=[sb[1], R6["hid%d" % par]], excl=[BR[bb], BR[bb + 1], BR[bb + 2], BR[bb + 3]])
            for cg in range(4):
                hv = hres[:, tt, cg * 512:(cg + 1) * 512]
                S.op("dve", TT(hv, hv, B[bb + cg], ALU.add), reads=[R_h[tt]], excl=[BR[bb + cg]], writes=[R_h[tt]])

        wrep = A.alloc([2048], F32)
        junk = A.alloc([2048], BF16)
        ssq = A.alloc([8], F32)
        sd = A.alloc([8], F32)
        rstd = A.alloc([8], F32)
        R7 = {n: S.reg(n) for n in ["wrep", "nr"]}
        S.dma("sp", DMA(wrep, wfin_d), reads=[PH], writes=[R7["wrep"]])
        S.op("dve", MEMSET(ssq, 0.0), reads=[PH], writes=[R7["nr"]])

        def emit_final(tt):
            S.op("act", ACTV(junk, hres[:, tt, :], AF.Square, accum=ssq[:, tt:tt + 1]), reads=[R_h[tt], PH], writes=[R7["nr"]])
            S.op("act", ACTV(sd[:, tt:tt + 1], ssq[:, tt:tt + 1], AF.Sqrt, bias=eps_c[:, 0:1], scale=1.0 / 2048.0),
                 reads=[R7["nr"]] + CONS, writes=[R7["nr"]])
            S.op("dve", RECIP(rstd[:, tt:tt + 1], sd[:, tt:tt + 1]), reads=[R7["nr"]], writes=[R7["nr"]])
            S.op("dve", STT(hres[:, tt, :], hres[:, tt, :], rstd[:, tt:tt + 1], wrep, ALU.mult, ALU.mult),
                 reads=[R_h[tt], R7["nr"], R7["wrep"]], writes=[R_h[tt]])
            S.dma("sp", DMA(out_d[tt * 128:(tt + 1) * 128, :], hres[:, tt, :]), reads=[R_h[tt]])

        if RUN_P6:
            sa = ws.get()
            for un in range(8):
                emit_up_unit(0, un, sa)
            ws.release(sa)
            for fb in range(16):
                sb = ws.get()
                sa = ws.get() if fb + 1 < 16 else None
                for tt in range(8):
                    emit_down_tt(fb, tt, sb)
                    if sa is not None:
                        emit_up_unit(fb + 1, tt, sa)
                    else:
                        emit_final(tt)
                ws.release(sb)
                if sa is not None:
                    ws.release(sa)
        else:
            for tt in range(8):
                emit_final(tt)
        A.pop()
        S.finish()
        S.replay()
        build_nc.peak = A.peak
        build_nc.log = A.log
    return nc


def _host_prep(inputs):
    x = np.asarray(inputs["x"], dtype=np.float32)
    rb = np.asarray(inputs["rel_bias"], dtype=np.float32)[0]
    lbl = np.ascontiguousarray(np.asarray(inputs["lb_logits"], np.float32).reshape(2, 8, 128).transpose(2, 0, 1).reshape(128, 16))
    hgw = np.ascontiguousarray(np.asarray(inputs["hg_norm_w"], np.float32)[0].reshape(128, 1))
    wc = np.concatenate([np.asarray(inputs["norm_mix_w"], np.float32)[0].reshape(16, 128).T,
                         np.asarray(inputs["norm_mlp_w"], np.float32)[0].reshape(16, 128).T], axis=1)
    wc = np.ascontiguousarray(wc)
    wfin = np.ascontiguousarray(np.broadcast_to(np.asarray(inputs["norm_final_w"], np.float32).reshape(1, 2048), (128, 2048)))
    wmlp = np.ascontiguousarray(np.broadcast_to(np.asarray(inputs["norm_mlp_w"], np.float32)[0].reshape(1, 2048), (128, 2048)))
    k = np.arange(128)[:, None, None]
    o = np.arange(3)[None, :, None]
    t = np.arange(128)[None, None, :]
    idx = np.clip(128 * o + t - k, -256, 256) + 256
    rbt = rb[:, idx]
    invalid = np.broadcast_to((o == 0) & (k >= 64) & (t < 64), idx.shape)
    rbt = np.where(invalid[None], np.float32(NEG), rbt)
    rbt = np.ascontiguousarray(rbt.transpose(1, 0, 2, 3).reshape(128, 16, 384)).astype(np.float32)
    crep = np.ascontiguousarray(np.broadcast_to(rb[:, 512][None, :], (128, 16))).astype(np.float32)
    shared = {
        "w_in": np.ascontiguousarray(np.asarray(inputs["w_in"], np.float32)[0]),
        "w_a": np.ascontiguousarray(np.asarray(inputs["w_branch_a"], np.float32)[0]),
        "w_b": np.ascontiguousarray(np.asarray(inputs["w_branch_b"], np.float32)[0]),
        "w_out": np.ascontiguousarray(np.asarray(inputs["w_out"], np.float32)[0]),
        "w_up": np.ascontiguousarray(np.asarray(inputs["w_up"], np.float32)[0]),
        "w_down": np.ascontiguousarray(np.asarray(inputs["w_down"], np.float32)[0]),
        "lbl": lbl, "hgw": hgw, "wcols": wc, "wfin": wfin, "wmlp": wmlp, "rbt": rbt, "crep": crep,
    }
    in_maps = []
    for c in range(8):
        b, half = c // 2, c % 2
        xwin = np.zeros((2048, 2048), np.float32)
        if half == 1:
            xwin[:] = x[b]
        else:
            xwin[1024:] = x[b, 0:1024]
        d = dict(shared)
        d["xw"] = xwin
        d["hbias"] = np.full((128, 1), 0.0 if half == 1 else NEG, np.float32)
        in_maps.append(d)
    return in_maps


_NC_CACHE = {}


def kernel(**inputs):
    in_maps = _host_prep(inputs)
    if "nc" not in _NC_CACHE:
        _NC_CACHE["nc"] = build_nc()
    nc = _NC_CACHE["nc"]
    res = run_bass_kernel_spmd(nc, in_maps, core_ids=list(range(8)))
    out = np.zeros((4, 2048, 2048), np.float32)
    for c in range(8):
        b, half = c // 2, c % 2
        out[b, half * 1024:(half + 1) * 1024] = res.results[c]["out"]
    return out
```
